# Optimizing a Trainium2 kernel written in Bass

```python
import jax, jax.numpy as jnp
from jax import lax
import numpy as np

D_MODEL = 1024
BATCH = 8
SEQ = 2048
DEPTH = 2
DEC_BATCH = 32
DEC_SEQ = 4
PAST_LEN = 16384
PAGE_SIZE = 128

EXPAND = 2
D_A = EXPAND * D_MODEL // 2
CONV_WIDTH = 31
HEAD_DIM_B = 64
N_HEADS_B = (EXPAND * D_MODEL // 2) // HEAD_DIM_B
D_B = N_HEADS_B * HEAD_DIM_B
DSWA_CONFIGS = ((128, 1), (512, 4), (2048, 16))
DSWA_MAX_WINDOW = 2048
ROPE_THETA = 10000.0
HGRN_HEAD_K = 128
HGRN_HEAD_V = 128
N_HEADS_C = EXPAND * D_MODEL // HGRN_HEAD_V
D_CK = N_HEADS_C * HGRN_HEAD_K
D_C = N_HEADS_C * HGRN_HEAD_V
HGRN_CHUNK = 64
N_EVEN = (DEPTH + 1) // 2
N_ODD = DEPTH // 2
NORM_EPS = 1e-6
D_IN_AB = 3 * D_A + 4 * D_B
D_IN_C = 2 * D_CK + 2 * D_C

kernel_name = 'hybrid_conv_dilswa_hgrn2_step'


def rms_norm(x, g):
    xf = x.astype(jnp.float32)
    y = xf * lax.rsqrt(jnp.mean(xf * xf, axis=-1, keepdims=True) + NORM_EPS)
    return (y * g.astype(jnp.float32)).astype(x.dtype)


def layer_norm(x, g, b):
    xf = x.astype(jnp.float32)
    xc = xf - jnp.mean(xf, axis=-1, keepdims=True)
    y = xc * lax.rsqrt(jnp.mean(xc * xc, axis=-1, keepdims=True) + NORM_EPS)
    return (y * g.astype(jnp.float32) + b.astype(jnp.float32)).astype(x.dtype)


def rope(x, pos):
    half = x.shape[-1] // 2
    inv_freq = ROPE_THETA ** (-jnp.arange(half, dtype=jnp.float32) / half)
    ang = pos.astype(jnp.float32)[:, None] * inv_freq[None, :]
    cos = jnp.cos(ang)[None, :, None, :]
    sin = jnp.sin(ang)[None, :, None, :]
    xf = x.astype(jnp.float32)
    x1, x2 = xf[..., :half], xf[..., half:]
    return jnp.concatenate([x1 * cos - x2 * sin, x2 * cos + x1 * sin], axis=-1).astype(x.dtype)


def conformer_conv(a_val, a_gate, conv_buf, conv_w, conv_b, ln_g, ln_b):
    u = a_val * jax.nn.sigmoid(a_gate)
    u_ext = jnp.concatenate([conv_buf.astype(u.dtype), u], axis=1)
    y = lax.conv_general_dilated(
        u_ext, conv_w[:, None, :].astype(u.dtype), window_strides=(1,), padding='VALID',
        dimension_numbers=('NWC', 'WIO', 'NWC'), feature_group_count=u.shape[-1])
    y = y + conv_b
    y = jax.nn.silu(layer_norm(y, ln_g, ln_b))
    return y, u_ext[:, -(CONV_WIDTH - 1):]


def dswa_prompt(q, k, v, window, dilation):
    n, s, h, hd = q.shape
    span = window // dilation
    blk = span
    n_sub = s // dilation
    nb = -(-n_sub // blk)
    lp = nb * blk

    def split(t):
        t = t.reshape(n, n_sub, dilation, h, hd).transpose(0, 2, 1, 3, 4)
        t = jnp.pad(t, ((0, 0), (0, 0), (0, lp - n_sub), (0, 0), (0, 0)))
        return t.reshape(n, dilation, nb, blk, h, hd)

    def with_prev(t):
        prev = jnp.pad(t, ((0, 0), (0, 0), (1, 0), (0, 0), (0, 0), (0, 0)))[:, :, :-1]
        return jnp.concatenate([prev, t], axis=3)

    qb = split(q)
    kk = with_prev(split(k))
    vv = with_prev(split(v))
    scores = jnp.einsum('ndbqhe,ndbkhe->ndbhqk', qb, kk).astype(jnp.float32) * (hd ** -0.5)
    qi = jnp.arange(nb)[:, None] * blk + jnp.arange(blk)[None, :]
    ki = jnp.arange(nb)[:, None] * blk - blk + jnp.arange(2 * blk)[None, :]
    dist = qi[:, :, None] - ki[:, None, :]
    mask = (dist >= 0) & (dist <= span) & (ki[:, None, :] >= 0)
    scores = jnp.where(mask[None, None, :, None, :, :], scores, -jnp.inf)
    lse = jax.nn.logsumexp(scores, axis=-1)
    p = jnp.exp(scores - lse[..., None])
    o = jnp.einsum('ndbhqk,ndbkhe->ndbqhe', p.astype(vv.dtype), vv)
    o = o.reshape(n, dilation, lp, h, hd)[:, :, :n_sub].transpose(0, 2, 1, 3, 4).reshape(n, s, h, hd)
    lse = lse.transpose(0, 1, 2, 4, 3).reshape(n, dilation, lp, h)[:, :, :n_sub]
    lse = lse.transpose(0, 2, 1, 3).reshape(n, s, h)
    return o, lse


def dswa_sample(q, k_all, v_all, n_past, window, dilation):
    t = q.shape[1]
    span = window // dilation
    idx = n_past + jnp.arange(t)[:, None] - dilation * jnp.arange(span + 1)[None, :]
    valid = idx >= 0
    idx = jnp.maximum(idx, 0)
    kg = k_all[:, idx]
    vg = v_all[:, idx]
    scores = jnp.einsum('nthe,ntkhe->nthk', q, kg).astype(jnp.float32) * (q.shape[-1] ** -0.5)
    scores = jnp.where(valid[None, :, None, :], scores, -jnp.inf)
    lse = jax.nn.logsumexp(scores, axis=-1)
    p = jnp.exp(scores - lse[..., None])
    o = jnp.einsum('nthk,ntkhe->nthe', p.astype(vg.dtype), vg)
    return o, lse


def dswa_merge(parts):
    outs = jnp.stack([o for o, _ in parts], axis=0).astype(jnp.float32)
    lses = jnp.stack([l for _, l in parts], axis=0)
    w = jax.nn.softmax(lses, axis=0)
    return jnp.einsum('cnth,cnthe->nthe', w, outs)


def hgrn2_recurrence(q, k, v, logf, s0):
    n, t, h, dk = q.shape
    dv = v.shape[-1]
    c = min(HGRN_CHUNK, t)
    nc = -(-t // c)
    pad = nc * c - t

    def blocks(a):
        a = jnp.pad(a.astype(jnp.float32), ((0, 0), (0, pad), (0, 0), (0, 0)))
        return a.reshape(n, nc, c, h, a.shape[-1]).transpose(1, 0, 3, 2, 4)

    causal = jnp.tril(jnp.ones((c, c), dtype=bool))

    def step(s, inp):
        qc, kc, vc, gc = inp
        b = jnp.cumsum(gc, axis=2)
        expo = b[:, :, :, None, :] - b[:, :, None, :, :]
        decay = jnp.exp(jnp.where(causal[None, None, :, :, None], expo, -jnp.inf))
        attn = jnp.einsum('nhtd,nhsd,nhtsd->nhts', qc, kc, decay)
        o = jnp.einsum('nhts,nhsv->nhtv', attn, vc) + jnp.einsum('nhtd,nhdv->nhtv', qc * jnp.exp(b), s)
        b_last = b[:, :, -1]
        s_new = jnp.exp(b_last)[..., None] * s + jnp.einsum(
            'nhsd,nhsv->nhdv', kc * jnp.exp(b_last[:, :, None, :] - b), vc)
        return s_new, o

    s_fin, o = lax.scan(step, s0.astype(jnp.float32), (blocks(q), blocks(k), blocks(v), blocks(logf)))
    o = o.transpose(1, 0, 3, 2, 4).reshape(n, nc * c, h, dv)[:, :t]
    return o, s_fin


def even_layer(x, pos, conv_buf, kv_cache, pre_g, post_g, w_in, w_out, conv_w, conv_b, ln_g, ln_b):
    n, t, _ = x.shape
    h = rms_norm(x, pre_g)
    proj = jnp.einsum('ntd,de->nte', h, w_in)
    a_val, a_gate, a_out_gate, qb, kb, vb, b_gate = jnp.split(
        proj, [D_A, 2 * D_A, 3 * D_A, 3 * D_A + D_B, 3 * D_A + 2 * D_B, 3 * D_A + 3 * D_B], axis=-1)
    ya, conv_new = conformer_conv(a_val, a_gate, conv_buf, conv_w, conv_b, ln_g, ln_b)
    ya = ya * jax.nn.silu(a_out_gate)
    q = rope(qb.reshape(n, t, N_HEADS_B, HEAD_DIM_B), pos)
    k = rope(kb.reshape(n, t, N_HEADS_B, HEAD_DIM_B), pos)
    v = vb.reshape(n, t, N_HEADS_B, HEAD_DIM_B)
    if kv_cache is None:
        parts = [dswa_prompt(q, k, v, wnd, dil) for (wnd, dil) in DSWA_CONFIGS]
    else:
        k_cache, v_cache = kv_cache
        n_past = k_cache.shape[1]
        k_all = jnp.concatenate([k_cache.astype(k.dtype), k], axis=1)
        v_all = jnp.concatenate([v_cache.astype(v.dtype), v], axis=1)
        parts = [dswa_sample(q, k_all, v_all, n_past, wnd, dil) for (wnd, dil) in DSWA_CONFIGS]
    yb = dswa_merge(parts).astype(x.dtype).reshape(n, t, D_B) * jax.nn.silu(b_gate)
    y = jnp.einsum('nte,ed->ntd', jnp.concatenate([ya, yb], axis=-1), w_out)
    return x + rms_norm(y, post_g), conv_new, k, v


def odd_layer(x, s0, lb, pre_g, post_g, w_in, w_out, g_norm):
    n, t, _ = x.shape
    h = rms_norm(x, pre_g)
    proj = jnp.einsum('ntd,de->nte', h, w_in)
    qc, fc, ic, gate = jnp.split(proj, [D_CK, 2 * D_CK, 2 * D_CK + D_C], axis=-1)
    q = jax.nn.silu(qc).reshape(n, t, N_HEADS_C, HGRN_HEAD_K)
    f = lb + (1.0 - lb) * jax.nn.sigmoid(fc.astype(jnp.float32))
    logf = jnp.log(f).reshape(n, t, N_HEADS_C, HGRN_HEAD_K)
    k = (1.0 - f).reshape(n, t, N_HEADS_C, HGRN_HEAD_K)
    v = ic.reshape(n, t, N_HEADS_C, HGRN_HEAD_V)
    o, s_new = hgrn2_recurrence(q, k, v, logf, s0)
    o = rms_norm(o.astype(x.dtype), g_norm).reshape(n, t, D_C) * jax.nn.silu(gate)
    y = jnp.einsum('nte,ed->ntd', o, w_out)
    return x + rms_norm(y, post_g), s_new


def _normal(key, shape, scale):
    return scale * jax.random.normal(key, shape, jnp.float32)


def setup_inputs(seed: int = 0) -> dict:
    key = jax.random.key(seed)
    ks = jax.random.split(key, 18)
    cache_len = min(DSWA_MAX_WINDOW, PAST_LEN)
    return {
        'x_prompt': _normal(ks[0], (BATCH, SEQ, D_MODEL), 1.0),
        'x_sample': _normal(ks[1], (DEC_BATCH, DEC_SEQ, D_MODEL), 1.0),
        'cache_conv': _normal(ks[2], (N_EVEN, DEC_BATCH, CONV_WIDTH - 1, D_A), 0.5),
        'cache_swa_k': _normal(ks[3], (N_EVEN, DEC_BATCH, cache_len, N_HEADS_B, HEAD_DIM_B), 1.0),
        'cache_swa_v': _normal(ks[4], (N_EVEN, DEC_BATCH, cache_len, N_HEADS_B, HEAD_DIM_B), 1.0),
        'state_hgrn': _normal(ks[5], (N_ODD, DEC_BATCH, N_HEADS_C, HGRN_HEAD_K, HGRN_HEAD_V), 0.5),
        'pre_norm': 1.0 + _normal(ks[6], (DEPTH, D_MODEL), 0.05),
        'post_norm': 1.0 + _normal(ks[7], (DEPTH, D_MODEL), 0.05),
        'w_in_ab': _normal(ks[8], (N_EVEN, D_MODEL, D_IN_AB), D_MODEL ** -0.5),
        'w_out_ab': _normal(ks[9], (N_EVEN, D_A + D_B, D_MODEL), (D_A + D_B) ** -0.5),
        'conv_w': _normal(ks[10], (N_EVEN, CONV_WIDTH, D_A), CONV_WIDTH ** -0.5),
        'conv_b': _normal(ks[11], (N_EVEN, D_A), 0.02),
        'conv_ln_g': 1.0 + _normal(ks[12], (N_EVEN, D_A), 0.05),
        'conv_ln_b': _normal(ks[13], (N_EVEN, D_A), 0.02),
        'w_in_c': _normal(ks[14], (N_ODD, D_MODEL, D_IN_C), D_MODEL ** -0.5),
        'w_out_c': _normal(ks[15], (N_ODD, D_C, D_MODEL), D_C ** -0.5),
        'hgrn_gnorm': 1.0 + _normal(ks[16], (N_ODD, HGRN_HEAD_V), 0.05),
        'hgrn_lb': _normal(ks[17], (DEPTH, D_CK), 0.1),
    }


def reference(x_prompt, x_sample, cache_conv, cache_swa_k, cache_swa_v, state_hgrn,
              pre_norm, post_norm, w_in_ab, w_out_ab, conv_w, conv_b, conv_ln_g, conv_ln_b,
              w_in_c, w_out_c, hgrn_gnorm, hgrn_lb):
    n_p, seq_p, _ = x_prompt.shape
    n_s, seq_s, _ = x_sample.shape
    pos_p = jnp.arange(seq_p, dtype=jnp.int32)
    pos_s = PAST_LEN + jnp.arange(seq_s, dtype=jnp.int32)
    lb_table = jnp.cumsum(jax.nn.softmax(hgrn_lb.astype(jnp.float32), axis=0), axis=0)
    lb_table = lb_table - lb_table[0]
    keep = min(DSWA_MAX_WINDOW, seq_p)
    yp, ys = x_prompt, x_sample
    conv_p, conv_s, kp_l, vp_l, ks_l, vs_l, sp_l, ss_l = [], [], [], [], [], [], [], []
    for layer in range(DEPTH):
        if layer % 2 == 0:
            e = layer // 2
            prm = (pre_norm[layer], post_norm[layer], w_in_ab[e], w_out_ab[e],
                   conv_w[e], conv_b[e], conv_ln_g[e], conv_ln_b[e])
            zero_buf = jnp.zeros((n_p, CONV_WIDTH - 1, D_A), yp.dtype)
            yp, cb_p, k_p, v_p = even_layer(yp, pos_p, zero_buf, None, *prm)
            ys, cb_s, k_s, v_s = even_layer(ys, pos_s, cache_conv[e], (cache_swa_k[e], cache_swa_v[e]), *prm)
            conv_p.append(cb_p)
            conv_s.append(cb_s)
            kp_l.append(k_p[:, -keep:])
            vp_l.append(v_p[:, -keep:])
            ks_l.append(k_s)
            vs_l.append(v_s)
        else:
            o = layer // 2
            prm = (pre_norm[layer], post_norm[layer], w_in_c[o], w_out_c[o], hgrn_gnorm[o])
            s0 = jnp.zeros((n_p, N_HEADS_C, HGRN_HEAD_K, HGRN_HEAD_V), jnp.float32)
            yp, s_p = odd_layer(yp, s0, lb_table[layer], *prm)
            ys, s_s = odd_layer(ys, state_hgrn[o], lb_table[layer], *prm)
            sp_l.append(s_p)
            ss_l.append(s_s)
    new_conv_prompt = jnp.stack(conv_p, axis=0)
    new_conv_sample = jnp.stack(conv_s, axis=0)
    new_k_prompt = jnp.stack(kp_l, axis=0)
    new_v_prompt = jnp.stack(vp_l, axis=0)
    new_k_sample = jnp.stack(ks_l, axis=0)
    new_v_sample = jnp.stack(vs_l, axis=0)
    new_state_prompt = jnp.stack(sp_l, axis=0)
    new_state_sample = jnp.stack(ss_l, axis=0)
    return (yp, ys, new_conv_prompt, new_conv_sample, new_k_prompt, new_v_prompt,
            new_k_sample, new_v_sample, new_state_prompt, new_state_sample)
```

```python
import contextlib
import numpy as np
import ml_dtypes
import concourse.bass as bass
import concourse.mybir as mybir
from concourse.bass_utils import run_bass_kernel_spmd

F32 = mybir.dt.float32
BF16 = mybir.dt.bfloat16
AF = mybir.ActivationFunctionType
ALU = mybir.AluOpType
AX = mybir.AxisListType

ENGS = ("pe", "act", "dve", "pool", "sp")
NTOK = 2048
NS = 16
NCOL = NTOK + NS
NCOLP = NTOK + 32
EPS = 1e-6
TCH = [(0, 512), (512, 512), (1024, 512), (1536, 512), (2048, 32)]
PAST = 16384


import heapq
import os

SCHED = os.environ.get("MK_SCHED", "1") == "1"


class Buf:
    __slots__ = ("name", "excl", "lwn", "rdn", "wsem", "rsem", "wtot", "rtot")

    def __init__(self, name, excl=False):
        self.name = name
        self.excl = excl
        self.lwn = None
        self.rdn = []
        self.wsem = None
        self.rsem = None
        self.wtot = 0
        self.rtot = 0


class Node:
    __slots__ = ("idx", "e", "fns", "deps", "cost", "kind", "buf", "ticket", "semval", "start", "fin", "nd", "users", "is_output", "grp")

    def __init__(self, idx, e, fns, cost, kind="op", buf=None):
        self.idx = idx
        self.e = e
        self.fns = fns
        self.deps = set()
        self.cost = cost
        self.kind = kind
        self.buf = buf
        self.ticket = None
        self.semval = None
        self.start = None
        self.fin = 0.0
        self.nd = 0
        self.users = []
        self.is_output = False
        self.grp = None


import types


def freeze(fn):
    if fn.__closure__ is None:
        return fn
    cells = []
    for c in fn.__closure__:
        try:
            cells.append(types.CellType(c.cell_contents))
        except ValueError:
            cells.append(c)
    return types.FunctionType(fn.__code__, fn.__globals__, fn.__name__, fn.__defaults__, tuple(cells))


PER_ELEM = {"pe": 0.00046, "act": 0.00072, "dve": 0.0009, "pool": 0.0021, "sp": 0.0}
FIXED = {"pe": 0.04, "act": 0.14, "dve": 0.09, "pool": 0.3, "sp": 0.1}


class MK:
    def __init__(self, nc, stack):
        self.nc = nc
        self.stack = stack
        self.eng = {"pe": nc.tensor, "act": nc.scalar, "dve": nc.vector, "pool": nc.gpsimd, "sp": nc.sync}
        self.sem = {e: stack.enter_context(nc.semaphore("s_" + e)) for e in ENGS}
        self.cnt = {e: 0 for e in ENGS}
        self.seen = {e: {} for e in ENGS}
        self.nsem = 0
        self.ninstr = {e: 0 for e in ENGS}
        self.nodes = []
        self.nidx = 0
        self.pend = {e: [] for e in ENGS}
        self.dmabufs = []
        self.outbufs = []

    def _add_writer_dep(self, node, w):
        if w.grp is not None:
            node.deps.update(w.grp)
        else:
            node.deps.add(w)

    def _deps(self, node, reads, writes):
        for b in reads:
            if b.lwn is not None:
                self._add_writer_dep(node, b.lwn)
            if b.excl:
                for r in b.rdn:
                    if r.e != node.e:
                        node.deps.add(r)
        for b in writes:
            if b.lwn is not None:
                self._add_writer_dep(node, b.lwn)
            for r in b.rdn:
                node.deps.add(r)
        for b in reads:
            b.rdn.append(node)
        for b in writes:
            b.lwn = node
            b.rdn = []
        node.deps.discard(node)

    def op(self, e, fn, reads=(), writes=(), mark=True, n=512):
        cost = FIXED[e] + PER_ELEM[e] * n
        fn = freeze(fn)
        if not mark:
            self.pend[e].append((fn, list(reads), list(writes), cost))
            return
        fns = [fn]
        rr, ww = list(reads), list(writes)
        if self.pend[e]:
            pre = self.pend[e]
            self.pend[e] = []
            fns = [p[0] for p in pre] + fns
            for p in pre:
                rr += p[1]
                ww += p[2]
                cost += p[3]
            rr = list(dict.fromkeys(rr))
            ww = list(dict.fromkeys(ww))
        node = Node(self.nidx, e, fns, cost)
        self.nidx += 1
        self._deps(node, rr, ww)
        self.nodes.append(node)

    def newsem(self):
        self.nsem += 1
        return self.stack.enter_context(self.nc.semaphore("d%d" % self.nsem))

    def dma(self, q, out, in_, wbuf=None, rbuf=None, is_output=False, after=(), **kw):
        assert not self.pend[q]
        fn = lambda eng: eng.dma_start(out=out, in_=in_, **kw)
        try:
            dcost = 2.0 + max(in_.nbytes(), out.nbytes()) / 150e3
        except Exception:
            dcost = 2.5
        if wbuf is not None:
            assert rbuf is None
            node = Node(self.nidx, q, [fn], dcost, "dmaw", wbuf)
            prev = wbuf.lwn
            if prev is not None and prev.kind == "dmaw" and prev.buf is wbuf and prev.e == q and not wbuf.rdn:
                node.deps = set(prev.deps)
                if prev.grp is None:
                    prev.grp = [prev]
                prev.grp.append(node)
                node.grp = prev.grp
                wbuf.lwn = node
            else:
                self._deps(node, [], [wbuf])
            if wbuf.wsem is None:
                wbuf.wsem = self.newsem()
                self.dmabufs.append(wbuf)
        else:
            node = Node(self.nidx, q, [fn], dcost, "dmar", rbuf)
            self._deps(node, [rbuf], [])
            if rbuf.rsem is None:
                rbuf.rsem = self.newsem()
                self.dmabufs.append(rbuf)
            node.is_output = is_output
            if is_output and rbuf not in self.outbufs:
                self.outbufs.append(rbuf)
        for b in after:
            if b.lwn is not None:
                self._add_writer_dep(node, b.lwn)
        self.nidx += 1
        self.nodes.append(node)

    def dma_d2d(self, q, out, in_, tok):
        fn = lambda eng: eng.dma_start(out=out, in_=in_)
        node = Node(self.nidx, q, [fn], 2.5, "dmar", tok)
        self.nidx += 1
        if tok.rsem is None:
            tok.rsem = self.newsem()
            self.dmabufs.append(tok)
        if tok not in self.outbufs:
            self.outbufs.append(tok)
        tok.rdn.append(node)
        self.nodes.append(node)

    def _schedule(self, nodes):
        inseg = set(id(n) for n in nodes)
        for n in nodes:
            n.nd = 0
            n.users = []
        for n in nodes:
            for d in n.deps:
                if id(d) in inseg:
                    n.nd += 1
                    d.users.append(n)
        if not SCHED:
            return {e: [n for n in nodes if n.e == e] for e in ENGS}
        PRIO = os.environ.get("MK_PRIO", "bl")
        bl = {}
        for n in reversed(nodes):
            m = 0.0
            for u in n.users:
                if bl[id(u)] > m:
                    m = bl[id(u)]
            bl[id(n)] = m + n.cost + 0.15
        if PRIO == "bl":
            key = lambda n: (-bl[id(n)], n.idx)
        else:
            key = lambda n: (n.idx, 0)
        CP = {} if os.environ.get("MK_SIMLOG") == "2" else None
        free = {e: 0.0 for e in ENGS}
        pending = {e: [] for e in ENGS}
        avail = {e: [] for e in ENGS}
        ready_t = {}
        for n in nodes:
            if n.nd == 0:
                heapq.heappush(pending[n.e], (0.0, n.idx, n))
        order = {e: [] for e in ENGS}
        left = len(nodes)
        while left:
            best = None
            for e in ENGS:
                pe_, av = pending[e], avail[e]
                while pe_ and pe_[0][0] <= free[e]:
                    _, i_, n_ = heapq.heappop(pe_)
                    heapq.heappush(av, (key(n_), i_, n_))
                if av:
                    cand = (free[e], av[0][0], e, 0)
                elif pe_:
                    cand = (pe_[0][0], key(pe_[0][2]), e, 1)
                else:
                    continue
                if best is None or cand < best:
                    best = cand
            stt, _, e, src = best
            if src == 0:
                _, _, n = heapq.heappop(avail[e])
            else:
                _, _, n = heapq.heappop(pending[e])
            n.start = stt
            if CP is not None:
                dd = max((d for d in n.deps if id(d) in inseg), key=lambda d: d.fin, default=None)
                if order[e] and abs(free[e] - stt) < 1e-9:
                    CP[id(n)] = ("eng", order[e][-1])
                elif dd is not None:
                    CP[id(n)] = ("dep", dd)
            issue = 0.3 if n.kind != "op" else n.cost
            free[e] = stt + issue
            n.fin = stt + n.cost
            order[e].append(n)
            left -= 1
            for u in n.users:
                u.nd -= 1
                rt = ready_t.get(id(u), 0.0)
                lat = n.fin + (0.15 if u.e != n.e else 0.08)
                if lat > rt:
                    rt = lat
                ready_t[id(u)] = rt
                if u.nd == 0:
                    heapq.heappush(pending[u.e], (rt, u.idx, u))
        if os.environ.get("MK_SIMLOG"):
            mx = max(n.fin for n in nodes)
            busy = {e: round(sum(n.cost for n in order[e])) for e in ENGS}
            print("SEG nodes=%d sim_makespan=%.0fus busy=%s" % (len(nodes), mx, busy))
            if CP is not None:
                cur = max(nodes, key=lambda n: n.fin)
                acc = {}
                path = []
                while cur is not None:
                    why, prv = CP.get(id(cur), (None, None))
                    k_ = (cur.e, cur.kind, why)
                    acc[k_] = acc.get(k_, 0.0) + (cur.fin - cur.start)
                    path.append((round(cur.start, 1), cur.e, cur.kind, why, cur.idx))
                    cur = prv
                print("   critical path time by (engine, kind, reason):", {k: round(v) for k, v in sorted(acc.items(), key=lambda x: -x[1])})
                print("   path len", len(path), "sample:", path[:: max(1, len(path) // 25)])
        return order

    def _wait(self, e, key, sem, val):
        if val <= 0 or self.seen[e].get(key, 0) >= val:
            return
        self.seen[e][key] = val
        self.eng[e].wait_ge(sem, val)
        self.ninstr[e] += 1

    def _wait_node(self, e, d):
        if d.kind == "op":
            if d.e == "pe" and e == "pe":
                return
            self._wait(e, d.e, self.sem[d.e], d.ticket)
        elif d.kind == "dmaw":
            self._wait(e, ("w", id(d.buf)), d.buf.wsem, d.semval)
        else:
            self._wait(e, ("r", id(d.buf)), d.buf.rsem, d.semval)

    def flush(self):
        for e in ENGS:
            assert not self.pend[e], "dangling unmarked ops on " + e
        nodes = self.nodes
        self.nodes = []
        if not nodes:
            return
        order = self._schedule(nodes)
        for e in ENGS:
            for n in order[e]:
                if n.kind == "op":
                    self.cnt[e] += 1
                    n.ticket = self.cnt[e]
                elif n.kind == "dmaw":
                    n.buf.wtot += 16
                    n.semval = n.buf.wtot
                else:
                    n.buf.rtot += 16
                    n.semval = n.buf.rtot
        for e in ENGS:
            for n in order[e]:
                dmax = {}
                for d in sorted(n.deps, key=lambda x: x.idx):
                    if d.kind == "op":
                        self._wait_node(e, d)
                    else:
                        k_ = (d.kind, id(d.buf))
                        if k_ not in dmax or d.semval > dmax[k_].semval:
                            dmax[k_] = d
                for d in dmax.values():
                    self._wait_node(e, d)
                ins = None
                for fn in n.fns:
                    ins = fn(self.eng[e])
                    self.ninstr[e] += 1
                if n.kind == "op":
                    ins.then_inc(self.sem[e], 1)
                elif n.kind == "dmaw":
                    ins.then_inc(n.buf.wsem, 16)
                else:
                    ins.then_inc(n.buf.rsem, 16)
                n.fns = None

    def barrier(self):
        self.flush()
        for e in ENGS:
            for f in ENGS:
                if not (e == "pe" and f == "pe"):
                    self._wait(e, f, self.sem[f], self.cnt[f])
            for b in self.dmabufs:
                if b.wsem is not None:
                    self._wait(e, ("w", id(b)), b.wsem, b.wtot)
                if b.rsem is not None:
                    self._wait(e, ("r", id(b)), b.rsem, b.rtot)

    def wait_dma_reads(self, e, bufs):
        self.flush()
        for b in bufs:
            if b.rsem is not None:
                self._wait(e, ("r", id(b)), b.rsem, b.rtot)

    def finish(self):
        self.flush()
        for b in self.outbufs:
            self._wait("sp", ("r", id(b)), b.rsem, b.rtot)
        for f in ENGS:
            if f != "sp":
                self._wait("sp", f, self.sem[f], self.cnt[f])


def host_consts():
    c = {}
    c["identb"] = np.eye(128, dtype=np.float32).astype(ml_dtypes.bfloat16)
    c["identf"] = np.eye(128, dtype=np.float32)
    perm = np.zeros((128, 128), np.float32)
    for m in range(128):
        base = (m // 64) * 64
        src = base + ((m % 64) + 32) % 64
        perm[src, m] = 1.0
    c["permb"] = perm.astype(ml_dtypes.bfloat16)
    half = 32
    inv_freq = 10000.0 ** (-(np.arange(half, dtype=np.float64) / half))
    pos = np.concatenate([np.arange(NTOK, dtype=np.float64),
                          np.tile(PAST + np.arange(4, dtype=np.float64), 4), np.zeros(16, np.float64)])
    ang = pos[None, :] * inv_freq[:, None]
    p = np.arange(128)
    fi = (p % 64) % 32
    sign = np.where((p % 64) < 32, -1.0, 1.0)
    c["cosT"] = np.cos(ang)[fi, :].astype(np.float32)
    c["sinT"] = (np.sin(ang)[fi, :] * sign[:, None]).astype(np.float32)
    def mult(dist):
        m = np.zeros(dist.shape, np.float32)
        ok = dist >= 0
        m += ok & (dist <= 128)
        m += ok & (dist % 4 == 0) & (dist <= 512)
        m += ok & (dist % 16 == 0) & (dist <= 2048)
        return m
    k = np.arange(128)[:, None]
    q = np.arange(19 * 128)[None, :] - 3 * 128
    c["maskall"] = mult(q - k).astype(ml_dtypes.bfloat16)
    sm = np.zeros((128, 4, 8, 4), np.float32)
    pp = np.arange(128)
    for tq in range(4):
        for it in range(3):
            row = 16 * (32 * it + (pp % 32)) + (pp // 32)
            sm[:, :, it, tq] = mult(2048 + tq - row)[:, None]
        for it in range(3, 7):
            row = 1536 + 128 * (it - 3) + pp
            sm[:, :, it, tq] = mult(2048 + tq - row)[:, None]
        for sq in range(4):
            for tk in range(4):
                if tk <= tq:
                    sm[sq * 4 + tk, sq, 7, tq] = mult(np.array([tq - tk]))[0]
    c["smask"] = sm.astype(ml_dtypes.bfloat16)
    rm = np.ones((128, 512), np.float32); rm[:, 0::64] = 0.0
    c["resetm"] = rm.astype(ml_dtypes.bfloat16)
    rs = np.ones((128, 32), np.float32); rs[:, 0::4] = 0.0
    c["resets"] = rs.astype(ml_dtypes.bfloat16)
    a = np.arange(32)
    c["smaskh"] = ((a[:, None] <= a[None, :]) & (a[:, None] // 4 == a[None, :] // 4) & (a[None, :] < 16)).astype(np.float32).astype(ml_dtypes.bfloat16)
    sel = np.zeros((32, 4), np.float32)
    for sq in range(4):
        sel[sq * 4:(sq + 1) * 4, sq] = 1.0
    c["sels"] = sel
    s = np.arange(128)[:, None]
    t = np.arange(128)[None, :]
    c["hmask"] = ((s <= t) & (s // 64 == t // 64)).astype(np.float32).astype(ml_dtypes.bfloat16)
    return c


CONST_SPECS = [("identb", [128, 128], BF16), ("identf", [128, 128], F32), ("permb", [128, 128], BF16),
               ("cosT", [128, NCOLP], F32), ("sinT", [128, NCOLP], F32), ("maskall", [128, 19 * 128], BF16),
               ("hmask", [128, 128], BF16), ("smask", [128, 4, 8, 4], BF16),
               ("resetm", [128, 512], BF16), ("resets", [128, 32], BF16), ("smaskh", [32, 32], BF16), ("sels", [32, 4], F32)]

IN_SPECS = [
    ("xp", [NTOK, 1024]), ("xs", [NS, 1024]), ("cconv", [4 * 30, 1024]), ("ck", [4, 2048, 1024]),
    ("cv", [4, 2048, 1024]), ("st", [4 * 16, 128, 128]),
    ("w_in_ab", [1024, 7168]), ("w_out_ab", [2048, 1024]), ("w_in_c", [1024, 8192]), ("w_out_c", [2048, 1024]),
    ("gpre", [128, 2, 8]), ("gpost", [128, 2, 1024]), ("convw", [128, 8, 31]), ("convb", [128, 8]),
    ("lng", [128, 8]), ("lnb", [128, 8]), ("gnorm", [128, 1]), ("hlb", [128, 2, 16]),
]

OUT_SPECS = [
    ("yp", [NTOK, 1024]), ("ys", [NS, 1024]), ("ncp", [30, 1024]), ("ncs", [4, 30, 1024]),
    ("nkp", [NTOK, 1024]), ("nvp", [NTOK, 1024]), ("nks", [NS, 1024]), ("nvs", [NS, 1024]),
    ("nsp", [16, 128, 128]), ("nss", [4 * 16, 128, 128]),
]


def build(stop_after=None, dbg=False):
    import os
    LIM = os.environ.get('MK_LIM', '')
    nc = bass.Bass("TRN2", target_bir_lowering=False)
    I = {n: nc.dram_tensor(n, s, F32, kind="ExternalInput").ap() for n, s in IN_SPECS}
    C = {n: nc.dram_tensor(n, s, d, kind="ExternalInput").ap() for n, s, d in CONST_SPECS}
    O = {n: nc.dram_tensor(n, s, F32, kind="ExternalOutput").ap() for n, s in OUT_SPECS}
    x1s = nc.dram_tensor("x1s", [NCOL, 1024], F32, kind="Internal").ap()
    woutb = [nc.dram_tensor("woutb%d" % i, [2048, 1024], BF16, kind="Internal").ap() for i in range(2)]
    kvselb = [nc.dram_tensor("kvselb%d" % i, [4, 128, 7, 1024], BF16, kind="Internal").ap() for i in range(2)]
    DBG = {}
    if dbg:
        DBG["x1"] = nc.dram_tensor("dbg_x1", [NCOL, 1024], F32, kind="ExternalOutput").ap()
        DBG["ybT"] = nc.dram_tensor("dbg_ybT", [128, 8, NCOL], BF16, kind="ExternalOutput").ap()
        DBG["hT"] = nc.dram_tensor("dbg_hT", [128, 8, NCOL], BF16, kind="ExternalOutput").ap()
        DBG["yaT"] = nc.dram_tensor("dbg_yaT", [128, 8, NCOL], BF16, kind="ExternalOutput").ap()
        DBG["zT"] = nc.dram_tensor("dbg_zT", [128, 16, NCOL], BF16, kind="ExternalOutput").ap()

    with contextlib.ExitStack() as st:
        mk = MK(nc, st)

        def T(name, shape, dt=F32, stack=st):
            return stack.enter_context(nc.sbuf_tensor("sb_" + name, shape, dt))

        def PS(name, shape, dt=F32, stack=st):
            return stack.enter_context(nc.psum_tensor("pp_" + name, shape, dt))

        ps = [PS("ps%d" % i, [128, 512]) for i in range(7)]
        bps = [Buf("ps%d" % i, excl=True) for i in range(7)]
        psb = PS("psb", [128, 1024], BF16)
        bpsb = Buf("psb", excl=True)

        cst = {}
        bcst = Buf("consts")
        for n, s, d in CONST_SPECS:
            if n in ("cosT", "sinT", "maskall", "smask"):
                continue
            cst[n] = T("c_" + n, s, d)
            mk.dma("sp", cst[n][:], C[n], wbuf=bcst)
        prm = {}
        for n in ("gpre", "convw", "convb", "lng", "lnb", "gnorm", "hlb"):
            s = dict(IN_SPECS)[n]
            prm[n] = T("p_" + n, s, F32)
            mk.dma("sp", prm[n][:], I[n], wbuf=bcst)
        onesb = T("onesb", [128, 128], BF16)
        bones = Buf("onesb")
        mk.op("dve", lambda e: e.memset(onesb[:], 1.0), writes=[bones])

        bwoutb = [Buf("woutb%d" % i) for i in range(2)]

        def precast_wout(i_, gate):
            nm_ = ("w_out_ab", "w_out_c")[i_]
            for g4 in range(4):
                mk.dma("pool", woutb[i_][g4 * 512:(g4 + 1) * 512, :], I[nm_][g4 * 512:(g4 + 1) * 512, :], wbuf=bwoutb[i_], after=gate)

        hT = T("hT", [128, 8, NCOLP], BF16)
        bhT = [Buf("hT%d" % i) for i in range(17)]

        ybT = T("ybT", [128, 8, NCOLP], BF16)
        bybT = [[Buf("ybT%d_%d" % (hp, tc)) for tc in range(5)] for hp in range(8)]

        def hT_bufs(tc):
            return [bhT[16]] if tc == 4 else bhT[4 * tc:4 * tc + 4]

        TILES = [(i * 128, 128) for i in range(16)] + [(NTOK, 32)]

        nrm = {"i": 0}
        junk = T("junk", [128, 1024], BF16)
        bjunk = Buf("junk")
        hb = [T("hb%d" % i, [128, 1024], BF16) for i in range(2)]
        bhb = [Buf("hb%d" % i) for i in range(2)]
        ssb = [T("ss%d" % i, [128, 4], F32) for i in range(2)]
        bss = [Buf("ss%d" % i) for i in range(2)]

        def rms_stats(xt, bx, P, width, ss, bssb):
            mk.op("act", lambda e: e.activation(junk[:P, 0:width], xt, AF.Square, accum_out=ss[:P, 0:1]),
                  reads=[bx], writes=[bjunk, bssb])
            mk.op("dve", lambda e: e.tensor_scalar(ss[:P, 1:2], ss[:P, 0:1], 1.0 / width, EPS, ALU.mult, ALU.add),
                  reads=[bssb], writes=[bssb])
            mk.op("act", lambda e: e.activation(ss[:P, 1:2], ss[:P, 1:2], AF.Ln), reads=[bssb], writes=[bssb])
            mk.op("act", lambda e: e.activation(ss[:P, 2:3], ss[:P, 1:2], AF.Exp, scale=-0.5), reads=[bssb], writes=[bssb])

        def norm_to_hT(xt, bx, ti, layer):
            row0, P = TILES[ti]
            k = nrm["i"] % 2
            nrm["i"] += 1
            ss, bs_ = ssb[k], bss[k]
            rms_stats(xt[:P, :], bx, P, 1024, ss, bs_)
            mk.op("act", lambda e: e.activation(hb[k][:P, :], xt[:P, :], AF.Identity, scale=ss[:P, 2:3]),
                  reads=[bx, bs_], writes=[bhb[k]])
            if 'noT' in LIM:
                return
            for d in range(8):
                mk.op("pe", lambda e, d=d: e.transpose(psb[:, d * 128:d * 128 + P], hb[k][:P, d * 128:(d + 1) * 128],
                                                      cst["identb"][:P, :P]),
                      reads=[bhb[k], bcst], writes=[bpsb], mark=(d == 7), n=128)
            src = psb[:, :].rearrange("p (d t) -> p d t", t=128)[:, :, 0:P]
            g = prm["gpre"][:, layer, :].rearrange("p (d o) -> p d o", o=1).to_broadcast([128, 8, P])
            mk.op("dve", lambda e: e.tensor_tensor(hT[:, :, row0:row0 + P], src, g, ALU.mult),
                  reads=[bpsb, bcst], writes=[bhT[ti]])

        xr = [T("xr%d" % i, [128, 1024], F32) for i in range(2)]
        bxr = [Buf("xr%d" % i) for i in range(2)]

        def x_src(ti):
            row0, P = TILES[ti]
            return I["xp"][row0:row0 + P, :] if ti < 16 else I["xs"][:, :]

        for ti in range((16 if 'nosamp' in LIM else 17) if 'ti1' not in LIM else 1):
            row0, P = TILES[ti]
            k = ti % 2
            mk.dma("sp", xr[k][:(P if ti < 16 else NS), :], x_src(ti), wbuf=bxr[k])
            if 'noN' not in LIM:
                norm_to_hT(xr[k], bxr[k], ti, 0)

        NW = 2
        wsl = [T("w%d" % i, [128, 8, 512], BF16) for i in range(NW)]
        bws = [Buf("w%d" % i) for i in range(NW)]
        wstate = {"n": 0}

        def wload(parts):
            k = wstate["n"] % NW
            wstate["n"] += 1
            off = 0
            for ap in parts:
                w = ap.shape[1]
                mk.dma("pool", wsl[k][:, :, off:off + w], ap.rearrange("(c p) f -> p c f", p=128), wbuf=bws[k])
                off += w
            return wsl[k], bws[k]

        def proj_fm(w, bw, f0, tc, bank):
            c0, wd = TCH[tc]
            pt_, pb_ = (ps[bank], bps[bank]) if isinstance(bank, int) else bank
            for d in range(8):
                mk.op("pe", lambda e, d=d: e.matmul(pt_[:, 0:wd], w[:, d, f0:f0 + 128], hT[:, d, c0:c0 + wd],
                                                    start=(d == 0), stop=(d == 7)),
                      reads=[bw] + hT_bufs(tc), writes=[pb_], mark=(d == 7), n=wd)

        def sigmoid_to(dst, bdst, src, bsrc, keep_e=None):
            e_ap, e_b = keep_e if keep_e is not None else (dst, bdst)
            mk.op("act", lambda e: e.activation(e_ap, src, AF.Exp, scale=-1.0), reads=[bsrc], writes=[e_b])
            mk.op("act", lambda e: e.activation(dst, e_ap, AF.Ln, bias=1.0), reads=[e_b], writes=[bdst])
            mk.op("act", lambda e: e.activation(dst, dst, AF.Exp, scale=-1.0), reads=[bdst], writes=[bdst])

        def silu_to(dst, bdst, src, bsrc, tmp, btmp, eng="dve", xs=None):
            if os.environ.get("MK_NOXS"):
                xs = None
            if xs is not None:
                mk.op("dve", lambda e: e.tensor_copy(xs[0], src), reads=[bsrc], writes=[xs[1]])
            sigmoid_to(tmp, btmp, src, bsrc)
            if xs is not None:
                mk.op(eng, lambda e: e.tensor_tensor(dst, xs[0], tmp, ALU.mult), reads=[xs[1], btmp], writes=[bdst])
            else:
                mk.op(eng, lambda e: e.tensor_tensor(dst, src, tmp, ALU.mult), reads=[bsrc, btmp], writes=[bdst])

        sAtt = st.enter_context(contextlib.ExitStack())
        KTs = T("KTs", [128, 8, 32], BF16, sAtt)
        QTs = T("QTs", [128, 8, 32], BF16, sAtt)
        SGs = T("SGs", [128, 8, 32], BF16, sAtt)
        VAs = T("VAs", [32, 8, 192], BF16, sAtt)
        bsmp = [Buf("smp%d" % hp) for hp in range(8)]
        mk.op("dve", lambda e: e.memset(VAs[:, :, 64:128], 1.0), writes=bsmp)

        with contextlib.ExitStack() as sB:
            if 'noB' in LIM:
                raise_skip = True
            cosT = T("cosT", [128, NCOLP], F32, sB)
            sinT = T("sinT", [128, NCOLP], F32, sB)
            maskall = T("maskall", [128, 19 * 128], BF16, sB)
            brope = Buf("rope")
            mk.dma("sp", cosT[:], C["cosT"], wbuf=brope)
            mk.dma("sp", sinT[:], C["sinT"], wbuf=brope)
            mk.dma("sp", maskall[:], C["maskall"], wbuf=brope)
            KT = [T("KT%d" % i, [128, 2, NCOLP], BF16, sB) for i in range(2)]
            bKT = [Buf("KT%d" % i) for i in range(2)]
            for i in range(2):
                mk.op("dve", lambda e, i=i: e.memset(KT[i][:, :, :], 0.0), writes=[bKT[i]], n=4200)
            QT = [T("QT%d" % i, [128, NCOLP], BF16, sB) for i in range(2)]
            bQT = [Buf("QT%d" % i) for i in range(2)]
            SG = [T("SG%d" % i, [128, NCOLP], BF16, sB) for i in range(2)]
            bSG = [Buf("SG%d" % i) for i in range(2)]
            VA = [T("VA%d" % i, [128, 16, 192], BF16, sB) for i in range(2)]
            bVA = [Buf("VA%d" % i) for i in range(2)]
            for i in range(2):
                mk.op("dve", lambda e, i=i: e.memset(VA[i][:, :, 64:128], 1.0), writes=[bVA[i]])
            raw = [T("raw%d" % i, [128, 512], BF16, sB) for i in range(2)]
            braw = [Buf("raw%d" % i) for i in range(2)]
            t1 = [T("t1_%d" % i, [128, 512], F32, sB) for i in range(2)]
            bt1 = [Buf("t1_%d" % i) for i in range(2)]
            t2 = [T("t2_%d" % i, [128, 512], F32, sB) for i in range(2)]
            bt2 = [Buf("t2_%d" % i) for i in range(2)]
            kst = [T("kst%d" % i, [128, 4, 128], F32, sB) for i in range(2)]
            bkst = [Buf("kst%d" % i) for i in range(2)]
            pt = [T("pt%d" % i, [128, 512], BF16, sB) for i in range(4)]
            bpt = [Buf("pt%d" % i) for i in range(4)]
            rz = T("rz", [128, 512], F32, sB)
            brz = Buf("rz")
            ytmp = T("ytmp", [128, 512], F32, sB)
            bytmp = Buf("ytmp")
            ctr = {"rope": 0, "kst": 0, "pt": 0, "st": 0, "ot": 0, "sg": 0}
            sgtB = [T("sgtB%d" % i, [128, 512], F32, sB) for i in range(2)]
            bsgtB = [Buf("sgtB%d" % i) for i in range(2)]

            def rope_chunk(bank, tc, dst, bdst, want_f32):
                c0, wd = TCH[tc]
                k = ctr["rope"] % 2
                ctr["rope"] += 1
                mk.op("act", lambda e: e.copy(raw[k][:, 0:wd], ps[bank][:, 0:wd]), reads=[bps[bank]], writes=[braw[k]])
                mk.op("dve", lambda e: e.tensor_tensor(t1[k][:, 0:wd], ps[bank][:, 0:wd], cosT[:, c0:c0 + wd], ALU.mult),
                      reads=[bps[bank], brope], writes=[bt1[k]])
                mk.op("pe", lambda e: e.matmul(ps[2][:, 0:wd], cst["permb"][:], raw[k][:, 0:wd], start=True, stop=True),
                      reads=[bcst, braw[k]], writes=[bps[2]])
                mk.op("dve", lambda e: e.tensor_tensor(t2[k][:, 0:wd], ps[2][:, 0:wd], sinT[:, c0:c0 + wd], ALU.mult),
                      reads=[bps[2], brope], writes=[bt2[k]])
                if want_f32:
                    mk.op("dve", lambda e: e.tensor_tensor(t1[k][:, 0:wd], t1[k][:, 0:wd], t2[k][:, 0:wd], ALU.add),
                          reads=[bt1[k], bt2[k]], writes=[bt1[k]])
                    mk.op("act", lambda e: e.copy(dst[0:64, 0, c0:c0 + wd], t1[k][0:64, 0:wd]), reads=[bt1[k]], writes=[bdst])
                    mk.op("act", lambda e: e.copy(dst[64:128, 1, c0:c0 + wd], t1[k][64:128, 0:wd]), reads=[bt1[k]], writes=[bdst])
                    return t1[k], bt1[k]
                mk.op("dve", lambda e: e.tensor_tensor(dst[:, c0:c0 + wd], t1[k][:, 0:wd], t2[k][:, 0:wd], ALU.add),
                      reads=[bt1[k], bt2[k]], writes=[bdst])
                return None, None

            psbF = psb[:, :].bitcast(F32)
            ST_RING = [(ps[3], bps[3]), (ps[4], bps[4]), (psbF, bpsb)]

            def proj_gen(hp, kk):
                w, bw = wload([I["w_in_ab"][:, 3072 + hp * 128:3072 + (hp + 1) * 128],
                               I["w_in_ab"][:, 4096 + hp * 128:4096 + (hp + 1) * 128],
                               I["w_in_ab"][:, 5120 + hp * 128:5120 + (hp + 1) * 128],
                               I["w_in_ab"][:, 6144 + hp * 128:6144 + (hp + 1) * 128]])
                for tc in range(5):
                    c0, wd = TCH[tc]
                    bank = tc % 2
                    proj_fm(w, bw, 128, tc, bank)
                    kf, bkf = rope_chunk(bank, tc, KT[kk], bKT[kk], True)
                    ks = ctr["kst"] % 2
                    ctr["kst"] += 1
                    nsub = (wd + 127) // 128
                    for j in range(nsub):
                        P = min(128, wd - j * 128)
                        mk.op("pe", lambda e, j=j, P=P: e.transpose(ps[2][0:P, j * 128:(j + 1) * 128],
                                                                      kf[:, j * 128:j * 128 + P], cst["identf"][:, :]),
                              reads=[bkf, bcst], writes=[bps[2]], mark=(j == nsub - 1), n=128)
                    P = min(128, wd)
                    mk.op("act", lambda e, P=P, nsub=nsub: e.copy(
                        kst[ks][0:P, 0:nsub, :], ps[2][0:P, 0:nsub * 128].rearrange("p (j f) -> p j f", f=128)),
                        reads=[bps[2]], writes=[bkst[ks]])
                    if tc < 4:
                        dst = O["nkp"][c0:c0 + 512, hp * 128:(hp + 1) * 128].rearrange("(j p) f -> p j f", p=128)
                        mk.dma("sp", dst, kst[ks][:, :, :], rbuf=bkst[ks], is_output=True)
                    else:
                        mk.dma("sp", O["nks"][:, hp * 128:(hp + 1) * 128], kst[ks][0:NS, 0, :], rbuf=bkst[ks], is_output=True)
                    yield
                for g in range(5):
                    bank = g % 2
                    if g < 4:
                        for j in range(4):
                            ti = g * 4 + j
                            for d in range(8):
                                mk.op("pe", lambda e, d=d, j=j, ti=ti: e.matmul(
                                    ps[bank][:, j * 128:(j + 1) * 128], hT[:, d, ti * 128:(ti + 1) * 128],
                                    w[:, d, 256:384], start=(d == 0), stop=(d == 7)),
                                    reads=[bw, bhT[ti]], writes=[bps[bank]], mark=(d == 7 and j == 3), n=128)
                        src = ps[bank][:, :].rearrange("p (j f) -> p j f", f=128)
                        mk.op("act", lambda e: e.copy(VA[kk][:, g * 4:g * 4 + 4, 0:64], src[:, :, 0:64]),
                              reads=[bps[bank]], writes=[bVA[kk]])
                        mk.op("act", lambda e: e.copy(VA[kk][:, g * 4:g * 4 + 4, 128:192], src[:, :, 64:128]),
                              reads=[bps[bank]], writes=[bVA[kk]])
                        ks = ctr["kst"] % 2
                        ctr["kst"] += 1
                        mk.op("dve", lambda e: e.tensor_copy(kst[ks][:, :, :], src), reads=[bps[bank]], writes=[bkst[ks]])
                        dst = O["nvp"][g * 512:(g + 1) * 512, hp * 128:(hp + 1) * 128].rearrange("(j p) f -> p j f", p=128)
                        mk.dma("sp", dst, kst[ks][:, :, :], rbuf=bkst[ks], is_output=True)
                    else:
                        for d in range(8):
                            mk.op("pe", lambda e, d=d: e.matmul(ps[bank][0:32, 0:128], hT[:, d, NTOK:NCOLP], w[:, d, 256:384],
                                                                start=(d == 0), stop=(d == 7)),
                                  reads=[bw, bhT[16]], writes=[bps[bank]], mark=(d == 7), n=128)
                        ks = ctr["kst"] % 2
                        ctr["kst"] += 1
                        mk.op("dve", lambda e: e.tensor_copy(kst[ks][0:32, 0, :], ps[bank][0:32, 0:128]),
                              reads=[bps[bank]], writes=[bkst[ks]])
                        mk.op("act", lambda e: e.copy(VAs[:, hp, 0:64], ps[bank][0:32, 0:64]), reads=[bps[bank]], writes=[bsmp[hp]])
                        mk.op("act", lambda e: e.copy(VAs[:, hp, 128:192], ps[bank][0:32, 64:128]), reads=[bps[bank]], writes=[bsmp[hp]])
                        mk.dma("sp", O["nvs"][:, hp * 128:(hp + 1) * 128], kst[ks][0:NS, 0, :], rbuf=bkst[ks], is_output=True)
                    yield
                for tc in range(5):
                    bank = tc % 2
                    proj_fm(w, bw, 0, tc, bank)
                    rope_chunk(bank, tc, QT[kk], bQT[kk], False)
                    yield
                for tc in range(5):
                    c0, wd = TCH[tc]
                    bank = tc % 2
                    proj_fm(w, bw, 384, tc, bank)
                    k_ = ctr["sg"] % 2
                    ctr["sg"] += 1
                    silu_to(SG[kk][:, c0:c0 + wd], bSG[kk], ps[bank][:, 0:wd], bps[bank], sgtB[k_][:, 0:wd], bsgtB[k_],
                            eng="dve", xs=(t2[k_][:, 0:wd], bt2[k_]))
                    yield
                mk.op("dve", lambda e: e.tensor_copy(KTs[0:64, hp, :], KT[kk][0:64, 0, NTOK:NCOLP]), reads=[bKT[kk]], writes=[bsmp[hp]])
                mk.op("dve", lambda e: e.tensor_copy(KTs[64:128, hp, :], KT[kk][64:128, 1, NTOK:NCOLP]), reads=[bKT[kk]], writes=[bsmp[hp]])
                mk.op("dve", lambda e: e.tensor_copy(QTs[:, hp, :], QT[kk][:, NTOK:NCOLP]), reads=[bQT[kk]], writes=[bsmp[hp]])
                mk.op("dve", lambda e: e.tensor_copy(SGs[:, hp, :], SG[kk][:, NTOK:NCOLP]), reads=[bSG[kk]], writes=[bsmp[hp]])

            def attn(hp, kk, bg):
                its = [(hh, c, j) for hh in range(2) for c in range(4) for j in range(4 * c + 4)]
                LA = 2
                slots = {}
                obs = {}
                for n in range(len(its) + LA):
                    if n < len(its):
                        hh, c, j = its[n]
                        hb0 = hh * 64
                        stt, bst = ST_RING[ctr["st"] % 3]
                        ctr["st"] += 1
                        slots[n] = (stt, bst)
                        mk.op("pe", lambda e: e.matmul(stt[:, :], KT[kk][:, hh, j * 128:(j + 1) * 128],
                                                       QT[kk][:, c * 512:(c + 1) * 512], start=True, stop=True),
                              reads=[bKT[kk], bQT[kk]], writes=[bst])
                    m = n - LA
                    if m < 0:
                        continue
                    hh, c, j = its[m]
                    nj = 4 * c + 4
                    vcol = 0 if hh == 0 else 64
                    num0, z0 = (0, 64) if hh == 0 else (64, 0)
                    if j == 0:
                        obs[(hh, c)] = 5 + (ctr["ot"] % 2)
                        ctr["ot"] += 1
                    ob = obs[(hh, c)]
                    stt, bst = slots.pop(m)
                    pk = ctr["pt"] % 4
                    ctr["pt"] += 1
                    mk.op("act", lambda e: e.activation(pt[pk][:, :], stt[:, :], AF.Exp, scale=0.125), reads=[bst], writes=[bpt[pk]])
                    mo = (4 * c - j + 3) * 128
                    mk.op("dve", lambda e: e.tensor_tensor(pt[pk][:, :], pt[pk][:, :], maskall[:, mo:mo + 512], ALU.mult),
                          reads=[bpt[pk], brope], writes=[bpt[pk]])
                    mk.op("pe", lambda e: e.matmul(ps[ob][:, :], VA[kk][:, j, vcol:vcol + 128], pt[pk][:, :],
                                                   start=(j == 0), stop=(j == nj - 1)),
                          reads=[bVA[kk], bpt[pk]], writes=[bps[ob]])
                    if j == nj - 1:
                        mk.op("act", lambda e: e.activation(rz[num0:num0 + 64, :], ps[ob][z0:z0 + 64, :], AF.Ln), reads=[bps[ob]], writes=[brz])
                        mk.op("act", lambda e: e.activation(rz[num0:num0 + 64, :], rz[num0:num0 + 64, :], AF.Exp, scale=-1.0), reads=[brz], writes=[brz])
                        mk.op("dve", lambda e: e.tensor_tensor(ytmp[num0:num0 + 64, :], ps[ob][num0:num0 + 64, :], rz[num0:num0 + 64, :], ALU.mult),
                              reads=[bps[ob], brz], writes=[bytmp])
                        mk.op("dve", lambda e: e.tensor_tensor(ybT[num0:num0 + 64, hp, c * 512:(c + 1) * 512], ytmp[num0:num0 + 64, :],
                                                                SG[kk][num0:num0 + 64, c * 512:(c + 1) * 512], ALU.mult),
                              reads=[bytmp, bSG[kk]], writes=[bybT[hp][c]])
                    if bg is not None and m % 4 == 3:
                        next(bg, None)

            NP = 8 if 'hp1' not in LIM else 1
            for _ in proj_gen(0, 0):
                pass
            bkvsel = [Buf("kvselb%d" % i) for i in range(2)]

            def precast_sel(i_, src, gate):
                for s_ in range(4):
                    v = src[s_]
                    for r in range(4):
                        sv = v[0:1536, :].rearrange("(t i r) f -> r i t f", r=16, i=32)[r]
                        mk.dma("pool", kvselb[i_][s_, r * 32:(r + 1) * 32, 0:3, :], sv, wbuf=bkvsel[i_], after=gate)
                    mk.dma("pool", kvselb[i_][s_, :, 3:7, :], v[1536:2048, :].rearrange("(j p) f -> p j f", p=128), wbuf=bkvsel[i_], after=gate)

            for hp in range(NP):
                bg = proj_gen(hp + 1, (hp + 1) % 2) if hp + 1 < NP else None
                if hp == 1:
                    precast_sel(0, I["ck"], [bybT[0][3]])
                if hp == 2:
                    precast_sel(1, I["cv"], [bybT[1][3]])
                if hp == 3:
                    precast_wout(0, [bybT[2][3]])
                if hp == 4:
                    precast_wout(1, [bybT[3][3]])
                attn(hp, hp % 2, bg)
                if bg is not None:
                    for _ in bg:
                        pass
            mk.barrier()
        with contextlib.ExitStack() as sS:
            smask = T("smask", [128, 4, 8, 4], BF16, sS)
            bsmask = Buf("smask")
            mk.dma("sp", smask[:], C["smask"], wbuf=bsmask)
            Ksel = [T("Ksel%d" % i, [128, 7, 1024], BF16, sS) for i in range(2)]
            Vsel = [T("Vsel%d" % i, [128, 7, 1024], BF16, sS) for i in range(2)]
            bKsel = [Buf("Ksel%d" % i) for i in range(2)]
            bVsel = [Buf("Vsel%d" % i) for i in range(2)]
            KselT = [T("KselT%d" % i, [128, 896], BF16, sS) for i in range(2)]
            bKselT = [Buf("KselT%d" % i) for i in range(2)]
            VselA = [T("VselA%d" % i, [128, 7, 192], BF16, sS) for i in range(2)]
            bVselA = [Buf("VselA%d" % i) for i in range(2)]
            for i in range(2):
                mk.op("dve", lambda e, i=i: e.memset(VselA[i][:, :, 64:128], 1.0), writes=[bVselA[i]])
            pts = [T("pts%d" % i, [128, 32], BF16, sS) for i in range(2)]
            bpts = [Buf("pts%d" % i) for i in range(2)]
            rzs = T("rzs", [128, 4], F32, sS)
            brzs = Buf("rzs")
            yts = T("yts", [128, 4], F32, sS)
            byts = Buf("yts")
            cs = {"kt": 0, "pt": 0, "sb": 0}

            def load_sel(dst, bdst, src, s_):
                i_ = 0 if src is I["ck"] else 1
                mk.dma("sp", dst[:, :, :], kvselb[i_][s_], wbuf=bdst, after=[bkvsel[i_]])

            load_sel(Ksel[0], bKsel[0], I["ck"], 0)
            load_sel(Vsel[0], bVsel[0], I["cv"], 0)
            for s_ in range(4):
                kb = s_ % 2
                if s_ + 1 < 4:
                    load_sel(Ksel[1 - kb], bKsel[1 - kb], I["ck"], s_ + 1)
                    load_sel(Vsel[1 - kb], bVsel[1 - kb], I["cv"], s_ + 1)
                for hp in range(8):
                    kt = cs["kt"] % 2
                    cs["kt"] += 1
                    for it in range(7):
                        mk.op("pe", lambda e, it=it: e.transpose(psb[:, it * 128:(it + 1) * 128], Ksel[kb][:, it, hp * 128:(hp + 1) * 128],
                                                                 cst["identb"][:, :]),
                              reads=[bKsel[kb], bcst], writes=[bpsb], mark=(it == 6), n=128)
                    mk.op("act", lambda e: e.copy(KselT[kt][:, :], psb[:, 0:896]), reads=[bpsb], writes=[bKselT[kt]])
                    mk.op("act", lambda e: e.copy(VselA[kt][:, :, 0:64], Vsel[kb][:, :, hp * 128:hp * 128 + 64]),
                          reads=[bVsel[kb]], writes=[bVselA[kt]], n=448)
                    mk.op("act", lambda e: e.copy(VselA[kt][:, :, 128:192], Vsel[kb][:, :, hp * 128 + 64:hp * 128 + 128]),
                          reads=[bVsel[kb]], writes=[bVselA[kt]], n=448)
                    for hh in range(2):
                        hb0 = hh * 64
                        vcol = 0 if hh == 0 else 64
                        num0, z0 = (0, 64) if hh == 0 else (64, 0)
                        sb = 3 + (cs["sb"] % 2)
                        ob = 5 + (cs["sb"] % 2)
                        cs["sb"] += 1
                        pk = cs["pt"] % 2
                        cs["pt"] += 1
                        qv = QTs[hb0:hb0 + 64, hp, s_ * 4:s_ * 4 + 4]
                        for it in range(7):
                            mk.op("pe", lambda e, it=it: e.matmul(ps[sb][:, it * 4:it * 4 + 4], KselT[kt][hb0:hb0 + 64, it * 128:(it + 1) * 128], qv,
                                                                  start=True, stop=True),
                                  reads=[bKselT[kt], bsmp[hp]], writes=[bps[sb]], mark=False, n=16)
                        mk.op("pe", lambda e: e.matmul(ps[sb][0:32, 28:32], KTs[hb0:hb0 + 64, hp, :], qv, start=True, stop=True),
                              reads=[bsmp[hp]], writes=[bps[sb]])
                        mk.op("act", lambda e: e.activation(pts[pk][:, 0:28], ps[sb][:, 0:28], AF.Exp, scale=0.125), reads=[bps[sb]], writes=[bpts[pk]], n=28)
                        mk.op("act", lambda e: e.activation(pts[pk][0:32, 28:32], ps[sb][0:32, 28:32], AF.Exp, scale=0.125),
                              reads=[bps[sb]], writes=[bpts[pk]], n=4)
                        mk.op("dve", lambda e: e.tensor_tensor(pts[pk][:, 0:28], pts[pk][:, 0:28],
                                                               smask[:, s_, 0:7, :], ALU.mult),
                              reads=[bpts[pk], bsmask], writes=[bpts[pk]], n=28)
                        mk.op("dve", lambda e: e.tensor_tensor(pts[pk][0:32, 28:32], pts[pk][0:32, 28:32], smask[0:32, s_, 7, :], ALU.mult),
                              reads=[bpts[pk], bsmask], writes=[bpts[pk]], n=4)
                        for it in range(7):
                            mk.op("pe", lambda e, it=it: e.matmul(ps[ob][:, 0:4], VselA[kt][:, it, vcol:vcol + 128], pts[pk][:, it * 4:it * 4 + 4],
                                                                  start=(it == 0), stop=False),
                                  reads=[bVselA[kt], bpts[pk]], writes=[bps[ob]], mark=False, n=16)
                        mk.op("pe", lambda e: e.matmul(ps[ob][:, 0:4], VAs[:, hp, vcol:vcol + 128], pts[pk][0:32, 28:32], start=False, stop=True),
                              reads=[bsmp[hp], bpts[pk]], writes=[bps[ob]])
                        mk.op("dve", lambda e: e.reciprocal(rzs[num0:num0 + 64, :], ps[ob][z0:z0 + 64, 0:4]), reads=[bps[ob]], writes=[brzs], n=16)
                        mk.op("dve", lambda e: e.tensor_tensor(yts[num0:num0 + 64, :], ps[ob][num0:num0 + 64, 0:4], rzs[num0:num0 + 64, :], ALU.mult),
                              reads=[bps[ob], brzs], writes=[byts], n=4)
                        mk.op("dve", lambda e: e.tensor_tensor(ybT[num0:num0 + 64, hp, NTOK + s_ * 4:NTOK + s_ * 4 + 4], yts[num0:num0 + 64, :],
                                                               SGs[num0:num0 + 64, hp, s_ * 4:s_ * 4 + 4], ALU.mult),
                              reads=[byts, bsmp[hp]], writes=[bybT[hp][4]], n=4)
            for hp in range(8):
                mk.op("dve", lambda e, hp=hp: e.memset(ybT[:, hp, NTOK + NS:NCOLP], 0.0), writes=[bybT[hp][4]])
            mk.barrier()
        sAtt.close()
        if dbg:
            bdbg = Buf("dbg")
            mk.barrier()
            mk.dma("sp", DBG["ybT"], ybT[:, :, 0:NCOL], rbuf=bdbg, is_output=True)
            mk.dma("sp", DBG["hT"], hT[:, :, 0:NCOL], rbuf=bdbg, is_output=True)

        sCD = st.enter_context(contextlib.ExitStack())
        yaT = T("yaT", [128, 8, NCOLP], BF16, sCD)
        byaT = [[Buf("yaT%d_%d" % (c, tc)) for tc in range(5)] for c in range(8)]
        with contextlib.ExitStack() as sC:
            yconv2 = [T("yconv%d" % i, [128, 8, 512], F32, sC) for i in range(2)]
            byc2 = [[Buf("yconv%d_%d" % (i, c)) for c in range(8)] for i in range(2)]
            meant2 = [T("meant0", [128, 512], F32, sC)] * 2
            bmean2 = [Buf("meant0")] * 2
            rstdt2 = [T("rstdt0", [128, 512], F32, sC)] * 2
            brstd2 = [Buf("rstdt0")] * 2
            uext = [T("uext%d" % i, [128, 544], F32, sC) for i in range(2)]
            buext = [Buf("uext%d" % i) for i in range(2)]
            tails = T("tails", [128, 8, 30], F32, sC)
            btails = [Buf("tails%d" % c) for c in range(8)]
            ybf = [T("ybf0", [128, 512], BF16, sC)] * 2
            bybf = [Buf("ybf0")] * 2
            ysq = [T("ysq0", [128, 512], BF16, sC)] * 2
            bysq = [Buf("ysq0")] * 2
            lnt = [T("lnt%d" % i, [128, 512], F32, sC) for i in range(2)]
            blnt = [Buf("lnt%d" % i) for i in range(2)]
            saog = [T("saog%d" % i, [128, 512], BF16, sC) for i in range(2)]
            bsaog = [Buf("saog%d" % i) for i in range(2)]
            uexts = T("uexts", [128, 8, 4, 34], F32, sC)
            buexts = [Buf("uexts%d" % c) for c in range(8)]
            cct = T("cct", [128, 1024], F32, sC)
            bcct = Buf("cct")
            ostg = [T("ostg%d" % i, [32, 128], F32, sC) for i in range(2)]
            bostg = [Buf("ostg%d" % i) for i in range(2)]
            cn = {"u": 0, "sg": 0, "yb": 0, "ln": 0, "sa": 0, "os": 0, "dg": 0}
            dgs = [T("dgs%d" % i, [128, 31, 128], BF16, sC) for i in range(2)]
            bdgs = [Buf("dgs%d" % i) for i in range(2)]
            ubf = [T("ubf%d" % i, [128, 544], BF16, sC) for i in range(2)]
            bubf = [Buf("ubf%d" % i) for i in range(2)]
            for c in range(8):
                mk.op("dve", lambda e, c=c: e.memset(tails[:, c, :], 0.0), writes=[btails[c]])
            mk.dma("sp", cct[0:120, :], I["cconv"], wbuf=bcct)
            for c in range(8):
                mk.op("pe", lambda e, c=c: e.transpose(ps[0][:, 0:120], cct[0:120, c * 128:(c + 1) * 128], cst["identf"][0:120, 0:120]),
                      reads=[bcct, bcst], writes=[bps[0]], n=128)
                mk.op("act", lambda e, c=c: e.copy(uexts[:, c, :, 0:30], ps[0][:, 0:120].rearrange("p (s t) -> p s t", t=30)),
                      reads=[bps[0]], writes=[buexts[c]])
            sgt = [cct[:, i * 512:(i + 1) * 512] for i in range(2)]
            bsgt = [Buf("sgt%d" % i) for i in range(2)]
            for b_ in bsgt:
                b_.lwn = bcct.lwn
                b_.rdn = list(bcct.rdn)
            bd2d = Buf("d2d")
            for s_ in range(4):
                mk.dma_d2d("sp", O["ncs"][s_, 0:26, :], I["cconv"][s_ * 30 + 4:s_ * 30 + 30, :], bd2d)

            def out_rows(src_ap, bsrc, ncols, dst_fn):
                k = cn["os"] % 2
                cn["os"] += 1
                mk.op("pe", lambda e: e.transpose(ps[1][0:ncols, 0:128], src_ap, cst["identf"][:, :]),
                      reads=[bsrc, bcst], writes=[bps[1]], n=128)
                mk.op("act", lambda e: e.copy(ostg[k][0:ncols, :], ps[1][0:ncols, 0:128]), reads=[bps[1]], writes=[bostg[k]])
                dst_fn(ostg[k], bostg[k])

            bwsA = [Buf("wA%d" % i) for i in range(NW)]
            bwsB = [Buf("wB%d" % i) for i in range(NW)]
            wsub = {"A": 0, "B": 0}

            def wload_sub(which, parts):
                k_ = wsub[which] % NW
                wsub[which] += 1
                base = 0 if which == "A" else 256
                bufs = bwsA if which == "A" else bwsB
                off = base
                for ap in parts:
                    wd_ = ap.shape[1]
                    mk.dma("pool", wsl[k_][:, :, off:off + wd_], ap.rearrange("(c p) f -> p c f", p=128), wbuf=bufs[k_])
                    off += wd_
                return wsl[k_][:, :, base:base + 256], bufs[k_]

            for tc in (4, 0, 1, 2, 3):
                c0, wd = TCH[tc]
                yconv, byc = yconv2[tc % 2], byc2[tc % 2]
                meant, bmean, rstdt, brstd = meant2[tc % 2], bmean2[tc % 2], rstdt2[tc % 2], brstd2[tc % 2]
                for c in range(8):
                    w, bw = wload_sub("A", [I["w_in_ab"][:, c * 128:(c + 1) * 128], I["w_in_ab"][:, 1024 + c * 128:1024 + (c + 1) * 128]])
                    proj_fm(w, bw, 0, tc, 0)
                    proj_fm(w, bw, 128, tc, 1)
                    k = cn["sg"] % 2
                    cn["sg"] += 1
                    sigmoid_to(sgt[k][:, 0:wd], bsgt[k], ps[1][:, 0:wd], bps[1])
                    if tc < 4:
                        u = cn["u"] % 2
                        cn["u"] += 1
                        mk.op("dve", lambda e: e.tensor_tensor(uext[u][:, 30:542], ps[0][:, :], sgt[k][:, :], ALU.mult),
                              reads=[bps[0], bsgt[k]], writes=[buext[u]])
                        mk.op("dve", lambda e, c=c: e.tensor_copy(uext[u][:, 0:30], tails[:, c, :]), reads=[btails[c]], writes=[buext[u]])
                        mk.op("dve", lambda e, c=c: e.tensor_copy(tails[:, c, :], uext[u][:, 512:542]), reads=[buext[u]], writes=[btails[c]])
                        dk = cn["dg"] % 2
                        cn["dg"] += 1
                        mk.op("dve", lambda e, c=c: e.tensor_tensor(
                            dgs[dk][:, :, :], cst["identb"][:, :].rearrange("p (o q) -> p o q", o=1).to_broadcast([128, 31, 128]),
                            prm["convw"][:, c, :].rearrange("p (j o) -> p j o", o=1).to_broadcast([128, 31, 128]), ALU.mult),
                            reads=[bcst], writes=[bdgs[dk]], n=3000)
                        mk.op("act", lambda e: e.copy(ubf[u][:, 0:542], uext[u][:, 0:542]), reads=[buext[u]], writes=[bubf[u]])
                        cb = 4 + dk
                        NPE = 31
                        for j in range(NPE):
                            mk.op("pe", lambda e, j=j: e.matmul(ps[cb][:, :], dgs[dk][:, j, :], ubf[u][:, j:j + 512], start=(j == 0), stop=(j == NPE - 1)),
                                  reads=[bdgs[dk], bubf[u]], writes=[bps[cb]], mark=(j == NPE - 1))
                        mk.op("dve", lambda e, c=c: e.tensor_scalar(yconv[:, c, :], ps[cb][:, :], prm["convb"][:, c:c + 1], None, ALU.add),
                              reads=[bps[cb], bcst], writes=[byc[c]])
                        for j in range(NPE, 31):
                            mk.op("dve", lambda e, c=c, j=j: e.scalar_tensor_tensor(yconv[:, c, :], uext[u][:, j:j + 512],
                                                                                    prm["convw"][:, c, j:j + 1], yconv[:, c, :], ALU.mult, ALU.add),
                                  reads=[buext[u], bcst, byc[c]], writes=[byc[c]])
                        if tc == 3:
                            out_rows(uext[u][:, 510:542], buext[u], 32,
                                     lambda stg, bstg, c=c: mk.dma("sp", O["ncp"][:, c * 128:(c + 1) * 128], stg[2:32, :], rbuf=bstg, is_output=True))
                    else:
                        uv = uexts[:, c, :, 30:34]
                        mk.op("dve", lambda e: e.tensor_tensor(uv, ps[0][:, 0:NS].rearrange("p (s t) -> p s t", t=4),
                                                               sgt[k][:, 0:NS].rearrange("p (s t) -> p s t", t=4), ALU.mult),
                              reads=[bps[0], bsgt[k]], writes=[buexts[c]])
                        mk.op("dve", lambda e, c=c: e.memset(yconv[:, c, 0:32], 0.0), writes=[byc[c]])
                        yv = yconv[:, c, 0:NS].rearrange("p (s t) -> p s t", t=4)
                        mk.op("dve", lambda e, c=c: e.tensor_scalar(yv, uexts[:, c, :, 0:4], prm["convw"][:, c, 0:1],
                                                                  prm["convb"][:, c:c + 1], ALU.mult, ALU.add),
                              reads=[buexts[c], bcst], writes=[byc[c]])
                        for j in range(1, 31):
                            mk.op("dve", lambda e, c=c, j=j: e.scalar_tensor_tensor(yv, uexts[:, c, :, j:j + 4], prm["convw"][:, c, j:j + 1],
                                                                                    yv, ALU.mult, ALU.add),
                                  reads=[buexts[c], bcst, byc[c]], writes=[byc[c]], n=16)
                        u = cn["u"] % 2
                        cn["u"] += 1
                        mk.op("dve", lambda e: e.memset(uext[u][:, 0:32], 0.0), writes=[buext[u]])
                        mk.op("dve", lambda e: e.tensor_copy(uext[u][:, 0:NS].rearrange("p (s t) -> p s t", t=4), uv),
                              reads=[buexts[c]], writes=[buext[u]])

                        def dst_fn(stg, bstg, c=c):
                            for s_ in range(4):
                                mk.dma("sp", O["ncs"][s_, 26:30, c * 128:(c + 1) * 128], stg[s_ * 4:s_ * 4 + 4, :], rbuf=bstg, is_output=True)
                        out_rows(uext[u][:, 0:32], buext[u], 32, dst_fn)
                for c in range(8):
                    k = cn["yb"] % 2
                    cn["yb"] += 1
                    mk.op("act", lambda e, c=c: e.copy(ybf[k][:, 0:wd], yconv[:, c, 0:wd]), reads=[byc[c]], writes=[bybf[k]])
                    mk.op("act", lambda e, c=c: e.activation(ysq[k][:, 0:wd], yconv[:, c, 0:wd], AF.Square), reads=[byc[c]], writes=[bysq[k]])
                    mk.op("pe", lambda e, c=c: e.matmul(ps[2][:, 0:wd], onesb[:, :], ybf[k][:, 0:wd], start=(c == 0), stop=(c == 7)),
                          reads=[bones, bybf[k]], writes=[bps[2]])
                    mk.op("pe", lambda e, c=c: e.matmul(ps[3][:, 0:wd], onesb[:, :], ysq[k][:, 0:wd], start=(c == 0), stop=(c == 7)),
                          reads=[bones, bysq[k]], writes=[bps[3]])
                mk.op("dve", lambda e: e.tensor_scalar(meant[:, 0:wd], ps[2][:, 0:wd], 1.0 / 1024, None, ALU.mult), reads=[bps[2]], writes=[bmean])
                mk.op("dve", lambda e: e.tensor_tensor(rstdt[:, 0:wd], meant[:, 0:wd], meant[:, 0:wd], ALU.mult), reads=[bmean], writes=[brstd])
                mk.op("dve", lambda e: e.scalar_tensor_tensor(rstdt[:, 0:wd], ps[3][:, 0:wd], 1.0 / 1024, rstdt[:, 0:wd], ALU.mult, ALU.subtract),
                      reads=[bps[3], brstd], writes=[brstd])
                mk.op("dve", lambda e: e.tensor_scalar(rstdt[:, 0:wd], rstdt[:, 0:wd], EPS, None, ALU.add), reads=[brstd], writes=[brstd])
                mk.op("act", lambda e: e.activation(rstdt[:, 0:wd], rstdt[:, 0:wd], AF.Ln), reads=[brstd], writes=[brstd])
                mk.op("act", lambda e: e.activation(rstdt[:, 0:wd], rstdt[:, 0:wd], AF.Exp, scale=-0.5), reads=[brstd], writes=[brstd])
                for c in range(8):
                    w, bw = wload_sub("B", [I["w_in_ab"][:, 2048 + c * 128:2048 + (c + 1) * 128]])
                    proj_fm(w, bw, 0, tc, 6)
                    k = cn["sa"] % 2
                    cn["sa"] += 1
                    k2 = cn["sg"] % 2
                    cn["sg"] += 1
                    silu_to(saog[k][:, 0:wd], bsaog[k], ps[6][:, 0:wd], bps[6], sgt[k2][:, 0:wd], bsgt[k2])
                    l = cn["ln"] % 2
                    cn["ln"] += 1
                    mk.op("dve", lambda e, c=c: e.tensor_tensor(lnt[l][:, 0:wd], yconv[:, c, 0:wd], meant[:, 0:wd], ALU.subtract),
                          reads=[byc[c], bmean], writes=[blnt[l]])
                    mk.op("dve", lambda e: e.tensor_tensor(lnt[l][:, 0:wd], lnt[l][:, 0:wd], rstdt[:, 0:wd], ALU.mult),
                          reads=[blnt[l], brstd], writes=[blnt[l]])
                    mk.op("act", lambda e, c=c: e.activation(lnt[l][:, 0:wd], lnt[l][:, 0:wd], AF.Identity, bias=prm["lnb"][:, c:c + 1],
                                                             scale=prm["lng"][:, c:c + 1]),
                          reads=[blnt[l], bcst], writes=[blnt[l]])
                    k3 = cn["sg"] % 2
                    cn["sg"] += 1
                    sigmoid_to(sgt[k3][:, 0:wd], bsgt[k3], lnt[l][:, 0:wd], blnt[l])
                    mk.op("dve", lambda e: e.tensor_tensor(lnt[l][:, 0:wd], lnt[l][:, 0:wd], sgt[k3][:, 0:wd], ALU.mult),
                          reads=[blnt[l], bsgt[k3]], writes=[blnt[l]])
                    mk.op("dve", lambda e, c=c: e.tensor_tensor(yaT[:, c, c0:c0 + wd], lnt[l][:, 0:wd], saog[k][:, 0:wd], ALU.mult),
                          reads=[blnt[l], bsaog[k]], writes=[byaT[c][tc]])
            mk.barrier()

        def out_proj_stage(sD, zsrc, w_dram, layer, x_in_fn, final):
            wout = T("wout%d" % layer, [128, 16, 1024], BF16, sD)
            bwoutg = [Buf("wout%d" % g8) for g8 in range(8)]
            for g8 in range(8):
                mk.dma("sp", wout[:, g8 * 2:(g8 + 1) * 2, :], woutb[layer][g8 * 256:(g8 + 1) * 256, :].rearrange("(e p) f -> p e f", p=128),
                       wbuf=bwoutg[g8], after=[bwoutb[layer]])
            gpo = T("gpo%d" % layer, [128, 1024], F32, sD)
            bgpo = Buf("gpo")
            mk.dma("sp", gpo[:], I["gpost"][:, layer, :], wbuf=bgpo)
            yn = [T("yn%d_%d" % (layer, i), [128, 1024], F32, sD) for i in range(2)]
            byn = [Buf("yn%d" % i) for i in range(2)]
            x1t = [T("x1t%d_%d" % (layer, i), [128, 1024], F32, sD) for i in range(2)]
            bx1t = [Buf("x1t%d" % i) for i in range(2)]
            sso = [T("sso%d_%d" % (layer, i), [128, 8], F32, sD) for i in range(2)]
            bsso = [Buf("sso%d" % i) for i in range(2)]
            for ti in range(17):
                row0, P = TILES[ti]
                tc = min(ti // 4, 4)
                k = ti % 2
                nreal = P if ti < 16 else NS
                mk.dma("sp", xr[k][:nreal, :], x_in_fn(ti), wbuf=bxr[k])
                for half in range(2):
                    for e_ in range(16):
                        zt, zb = zsrc(e_, tc)
                        mk.op("pe", lambda e, e_=e_, half=half, zt=zt: e.matmul(ps[2 * k + half][:P, :], zt[:, row0:row0 + P],
                                                                                wout[:, e_, half * 512:(half + 1) * 512],
                                                                                start=(e_ == 0), stop=(e_ == 15)),
                              reads=[zb, bwoutg[e_ // 2]], writes=[bps[2 * k + half]], mark=(e_ == 15))
                ss, bs_ = sso[k], bsso[k]
                for half in range(2):
                    mk.op("act", lambda e, half=half: e.activation(junk[:P, 0:512], ps[2 * k + half][:P, :], AF.Square, accum_out=ss[:P, half:half + 1]),
                          reads=[bps[2 * k + half]], writes=[bjunk, bs_])
                mk.op("dve", lambda e: e.tensor_tensor(ss[:P, 2:3], ss[:P, 0:1], ss[:P, 1:2], ALU.add), reads=[bs_], writes=[bs_])
                mk.op("dve", lambda e: e.tensor_scalar(ss[:P, 2:3], ss[:P, 2:3], 1.0 / 1024, EPS, ALU.mult, ALU.add), reads=[bs_], writes=[bs_])
                mk.op("act", lambda e: e.activation(ss[:P, 2:3], ss[:P, 2:3], AF.Ln), reads=[bs_], writes=[bs_])
                mk.op("act", lambda e: e.activation(ss[:P, 3:4], ss[:P, 2:3], AF.Exp, scale=-0.5), reads=[bs_], writes=[bs_])
                for half in range(2):
                    mk.op("dve", lambda e, half=half: e.scalar_tensor_tensor(yn[k][:P, half * 512:(half + 1) * 512], ps[2 * k + half][:P, :], ss[:P, 3:4],
                                                                             gpo[:P, half * 512:(half + 1) * 512], ALU.mult, ALU.mult),
                          reads=[bps[2 * k + half], bs_, bgpo], writes=[byn[k]])
                if ti == 16:
                    mk.op("dve", lambda e: e.memset(x1t[k][0:32, :], 0.0), writes=[bx1t[k]])
                mk.op("dve", lambda e: e.tensor_tensor(x1t[k][:nreal, :], yn[k][:nreal, :], xr[k][:nreal, :], ALU.add),
                      reads=[byn[k], bxr[k]], writes=[bx1t[k]])
                final(ti, x1t[k], bx1t[k], nreal)

        def zsrc0(e_, tc):
            return (yaT[:, e_, :], byaT[e_][tc]) if e_ < 8 else (ybT[:, e_ - 8, :], bybT[e_ - 8][tc])

        def x1_dst(ti):
            row0, P = TILES[ti]
            return x1s[row0:row0 + (P if ti < 16 else NS), :]

        x1bufs0 = []

        def final0(ti, xt, bxt, nreal):
            mk.dma("sp", x1_dst(ti), xt[:nreal, :], rbuf=bxt)
            if bxt not in x1bufs0:
                x1bufs0.append(bxt)
            if dbg:
                row0, P = TILES[ti]
                mk.dma("sp", DBG["x1"][row0:row0 + nreal, :], xt[:nreal, :], rbuf=bxt, is_output=True)
            norm_to_hT(xt, bxt, ti, 1)

        if dbg:
            mk.dma("sp", DBG["yaT"], yaT[:, :, 0:NCOL], rbuf=bdbg, is_output=True)
        with contextlib.ExitStack() as sD:
            out_proj_stage(sD, zsrc0, I["w_out_ab"], 0, x_src, final0)
            mk.barrier()
        sCD.close()

        x1bufs = x1bufs0
        with contextlib.ExitStack() as sL1:
            zhi = T("zhi", [128, 8, NCOLP], BF16, sL1)
            bz = [[Buf("z%d_%d" % (h, tc)) for tc in range(5)] for h in range(16)]

            def zT(h):
                return ybT[:, h, :] if h < 8 else zhi[:, h - 8, :]

            with contextlib.ExitStack() as sH:
                lbt = T("lbt", [128, 16], F32, sH)
                oml = T("oml", [128, 16], F32, sH)
                blb = Buf("lb")
                mk.op("dve", lambda e: e.tensor_tensor(lbt[:], prm["hlb"][:, 1, :], prm["hlb"][:, 0, :], ALU.subtract), reads=[bcst], writes=[blb])
                sigmoid_to(lbt[:], blb, lbt[:], blb)
                mk.op("dve", lambda e: e.tensor_scalar(oml[:], lbt[:], -1.0, 1.0, ALU.mult, ALU.add), reads=[blb], writes=[blb])
                R2 = lambda name, shape, dt: ([T("%s%d" % (name, i), shape, dt, sH) for i in range(2)], [Buf("%s%d" % (name, i)) for i in range(2)])
                RD = 2
                R3 = lambda name, shape, dt: ([T("%s%d" % (name, i), shape, dt, sH) for i in range(RD)], [Buf("%s%d" % (name, i)) for i in range(RD)])
                qs, bqs = R3("qs", [128, 512], F32)
                fg, bfg = R3("fg", [128, 512], F32)
                sgn, bsgn = R3("sgn", [128, 512], F32)
                bc, bbc = R3("bc", [128, 512], F32)
                ebt, bebt = R3("ebt", [128, 512], F32)
                enb, benb = R3("enb", [128, 512], F32)
                kpp = T("kpp", [128, NCOLP], BF16, sH)
                bkpp = [Buf("kpp%d" % tc) for tc in range(5)]
                HSETS = 1
                hsets = []
                for i_ in range(HSETS):
                    d_ = {}
                    for nm_ in ("qp", "kp", "sgate"):
                        d_[nm_] = T("%s_%d" % (nm_, i_), [128, NCOLP], BF16, sH)
                        d_["b" + nm_] = [Buf("%s%d_%d" % (nm_, i_, tc)) for tc in range(5)]
                    d_["ebl"] = T("ebl_%d" % i_, [128, 40], F32, sH)
                    d_["bebl"] = [Buf("ebl%d_%d" % (i_, tc)) for tc in range(5)]
                    d_["Vh"] = T("Vh_%d" % i_, [128, 16, 128], BF16, sH)
                    d_["bVh"] = [Buf("Vh%d_%d" % (i_, g)) for g in range(4)]
                    d_["ktok"] = T("ktok_%d" % i_, [128, 2, 16, 128], BF16, sH)
                    d_["bktok"] = [Buf("ktok%d_%d" % (i_, g)) for g in range(2)]
                    mk.op("dve", lambda e, t_=d_["ktok"]: e.memset(t_[:, :, :, :], 0.0), writes=d_["bktok"], n=4096)
                    hsets.append(d_)
                S0_ = T("S0", [128, 4, 128], F32, sH)
                bS0_ = Buf("S0")
                for d_ in hsets:
                    d_["S0"], d_["bS0"] = S0_, bS0_
                Vhs = T("Vhs", [32, 128], BF16, sH)
                bVhs = Buf("Vhs")
                ktoks = T("ktoks", [32, 128], BF16, sH)
                bktoks = Buf("ktoks")
                ktm = T("ktm", [32, 4, 128], BF16, sH)
                bktm = Buf("ktm")
                atm, batm = R2("atm", [128, 128], BF16)
                Sf, bSf = R2("Sf", [128, 128], F32)
                Sb = [T("Sb%d" % i, [128, 128], BF16, sH) for i in range(4)]
                bSb = [Buf("Sb%d" % i) for i in range(4)]
                osq, bosq = R2("osq", [128, 512], BF16)
                rst, brst = sgn, bsgn
                zt, bzt = bc, bbc
                S0b = T("S0b", [128, 4, 128], BF16, sH)
                bS0b = Buf("S0b")
                Sn = T("Sn", [128, 2, 128], F32, sH)
                bSn = Buf("Sn")
                hc = {"r": 0, "atm": 0, "sb": 0, "o": 0, "z": 0}
                psbF1 = psb[:, :].bitcast(F32)
                st4 = I["st"].rearrange("(s h) d v -> h d s v", h=16)
                nss4 = O["nss"].rearrange("(s h) d v -> h d s v", h=16)

                for h in range(16):
                    hs_ = hsets[h % HSETS]
                    qp, bqp, kp, bkp, sgate, bsgate = (hs_["qp"], hs_["bqp"], hs_["kp"], hs_["bkp"], hs_["sgate"], hs_["bsgate"])
                    ebl, bebl, Vh, bVh, ktok, bktok, S0, bS0 = (hs_["ebl"], hs_["bebl"], hs_["Vh"], hs_["bVh"], hs_["ktok"], hs_["bktok"],
                                                                hs_["S0"], hs_["bS0"])
                    w, bw = wload([I["w_in_c"][:, h * 128:(h + 1) * 128], I["w_in_c"][:, 2048 + h * 128:2048 + (h + 1) * 128],
                                   I["w_in_c"][:, 4096 + h * 128:4096 + (h + 1) * 128], I["w_in_c"][:, 6144 + h * 128:6144 + (h + 1) * 128]])
                    mk.dma("sp", S0[:], st4[h], wbuf=bS0)
                    for tc in range(5):
                        c0, wd = TCH[tc]
                        r = hc["r"] % RD
                        hc["r"] += 1
                        proj_fm(w, bw, 0, tc, 0)
                        silu_to(qs[r][:, 0:wd], bqs[r], ps[0][:, 0:wd], bps[0], enb[r][:, 0:wd], benb[r], xs=(qs[r][:, 0:wd], bqs[r]))
                        proj_fm(w, bw, 128, tc, 1)
                        sigmoid_to(fg[r][:, 0:wd], bfg[r], ps[1][:, 0:wd], bps[1], keep_e=(sgn[r][:, 0:wd], bsgn[r]))
                        mk.op("dve", lambda e: e.tensor_tensor(sgn[r][:, 0:wd], sgn[r][:, 0:wd], fg[r][:, 0:wd], ALU.mult),
                              reads=[bsgn[r], bfg[r]], writes=[bsgn[r]])
                        mk.op("dve", lambda e: e.tensor_scalar(fg[r][:, 0:wd], fg[r][:, 0:wd], oml[:, h:h + 1], lbt[:, h:h + 1], ALU.mult, ALU.add),
                              reads=[bfg[r], blb], writes=[bfg[r]])
                        mk.op("act", lambda e: e.activation(fg[r][:, 0:wd], fg[r][:, 0:wd], AF.Ln), reads=[bfg[r]], writes=[bfg[r]])
                        rmask = cst["resetm"][:, 0:wd] if tc < 4 else cst["resets"][:, 0:wd]
                        mk.op("dve", lambda e: e.tensor_tensor_scan(bc[r][:, 0:wd], rmask, fg[r][:, 0:wd], 0.0, ALU.mult, ALU.add),
                              reads=[bfg[r], bcst], writes=[bbc[r]], n=2 * wd)
                        mk.op("act", lambda e: e.activation(ebt[r][:, 0:wd], bc[r][:, 0:wd], AF.Exp), reads=[bbc[r]], writes=[bebt[r]])
                        mk.op("act", lambda e: e.activation(enb[r][:, 0:wd], bc[r][:, 0:wd], AF.Exp, scale=-1.0), reads=[bbc[r]], writes=[benb[r]])
                        cl = 64 if tc < 4 else 4
                        ncl = wd // cl
                        mk.op("dve", lambda e: e.tensor_copy(ebl[:, tc * 8:tc * 8 + ncl], ebt[r][:, cl - 1:wd:cl]), reads=[bebt[r]], writes=[bebl[tc]])
                        mk.op("dve", lambda e: e.tensor_tensor(qp[:, c0:c0 + wd], qs[r][:, 0:wd], ebt[r][:, 0:wd], ALU.mult),
                              reads=[bqs[r], bebt[r]], writes=[bqp[tc]])
                        mk.op("dve", lambda e: e.scalar_tensor_tensor(kp[:, c0:c0 + wd], sgn[r][:, 0:wd], oml[:, h:h + 1], enb[r][:, 0:wd], ALU.mult, ALU.mult),
                              reads=[bsgn[r], blb, benb[r]], writes=[bkp[tc]])
                        eblb = ebl[:, tc * 8:tc * 8 + ncl].rearrange("p (c o) -> p c o", o=1).to_broadcast([128, ncl, cl])
                        mk.op("dve", lambda e: e.tensor_tensor(kpp[:, c0:c0 + wd].rearrange("p (c t) -> p c t", t=cl),
                                                               kp[:, c0:c0 + wd].rearrange("p (c t) -> p c t", t=cl), eblb, ALU.mult),
                              reads=[bkp[tc], bebl[tc]], writes=[bkpp[tc]])
                        proj_fm(w, bw, 384, tc, (psbF1, bpsb))
                        silu_to(sgate[:, c0:c0 + wd], bsgate[tc], psbF1[:, 0:wd], bpsb, bc[r][:, 0:wd], bbc[r], xs=(qs[r][:, 0:wd], bqs[r]))
                    for g in range(5):
                        bank = g % 2
                        if g < 4:
                            for j in range(4):
                                ti = g * 4 + j
                                for d in range(8):
                                    mk.op("pe", lambda e, d=d, j=j, ti=ti: e.matmul(ps[bank][:, j * 128:(j + 1) * 128], hT[:, d, ti * 128:(ti + 1) * 128],
                                                                                    w[:, d, 256:384], start=(d == 0), stop=(d == 7)),
                                          reads=[bw, bhT[ti]], writes=[bps[bank]], mark=(d == 7 and j == 3), n=128)
                            mk.op("act", lambda e: e.copy(Vh[:, g * 4:g * 4 + 4, :], ps[bank][:, :].rearrange("p (j f) -> p j f", f=128)),
                                  reads=[bps[bank]], writes=[bVh[g]])
                        else:
                            for d in range(8):
                                mk.op("pe", lambda e, d=d: e.matmul(ps[bank][0:32, 0:128], hT[:, d, NTOK:NCOLP], w[:, d, 256:384],
                                                                    start=(d == 0), stop=(d == 7)),
                                      reads=[bw, bhT[16]], writes=[bps[bank]], mark=(d == 7), n=128)
                            mk.op("act", lambda e: e.copy(Vhs[:, :], ps[bank][0:32, 0:128]), reads=[bps[bank]], writes=[bVhs])
                    for g in range(2):
                        for j in range(8):
                            ti = g * 8 + j
                            mk.op("pe", lambda e, j=j, ti=ti: e.transpose(psb[:, j * 128:(j + 1) * 128], kpp[:, ti * 128:(ti + 1) * 128], cst["identb"][:, :]),
                                  reads=[bkpp[ti // 4], bcst], writes=[bpsb], mark=(j == 7), n=128)
                        mk.op("act", lambda e: e.copy(ktok[0:64, 0, g * 8:(g + 1) * 8, :], psb[0:64, :].rearrange("p (j f) -> p j f", f=128)),
                              reads=[bpsb], writes=[bktok[g]])
                        mk.op("act", lambda e: e.copy(ktok[64:128, 1, g * 8:(g + 1) * 8, :], psb[64:128, :].rearrange("p (j f) -> p j f", f=128)),
                              reads=[bpsb], writes=[bktok[g]])
                    mk.op("pe", lambda e: e.transpose(psb[0:32, 0:128], kpp[:, NTOK:NCOLP], cst["identb"][:, :]), reads=[bkpp[4], bcst], writes=[bpsb], n=128)
                    mk.op("act", lambda e: e.copy(ktoks[:, :], psb[0:32, 0:128]), reads=[bpsb], writes=[bktoks])

                    def finish_o(ob, c0, wd, tc):
                        k = hc["z"] % 2
                        hc["z"] += 1
                        mk.op("act", lambda e: e.activation(osq[k][:, 0:wd], ps[ob][:, 0:wd], AF.Square), reads=[bps[ob]], writes=[bosq[k]])
                        mk.op("pe", lambda e: e.matmul(ps[6][:, 0:wd], onesb[:, :], osq[k][:, 0:wd], start=True, stop=True),
                              reads=[bones, bosq[k]], writes=[bps[6]])
                        mk.op("dve", lambda e: e.tensor_scalar(rst[k][:, 0:wd], ps[6][:, 0:wd], 1.0 / 128, EPS, ALU.mult, ALU.add),
                              reads=[bps[6]], writes=[brst[k]])
                        mk.op("act", lambda e: e.activation(rst[k][:, 0:wd], rst[k][:, 0:wd], AF.Ln), reads=[brst[k]], writes=[brst[k]])
                        mk.op("act", lambda e: e.activation(rst[k][:, 0:wd], rst[k][:, 0:wd], AF.Exp, scale=-0.5), reads=[brst[k]], writes=[brst[k]])
                        mk.op("dve", lambda e: e.scalar_tensor_tensor(zt[k][:, 0:wd], ps[ob][:, 0:wd], prm["gnorm"][:, 0:1], rst[k][:, 0:wd], ALU.mult, ALU.mult),
                              reads=[bps[ob], bcst, brst[k]], writes=[bzt[k]])
                        mk.op("dve", lambda e: e.tensor_tensor(zT(h)[:, c0:c0 + wd], zt[k][:, 0:wd], sgate[:, c0:c0 + wd], ALU.mult),
                              reads=[bzt[k], bsgate[tc]], writes=[bz[h][tc]])

                    scur = None
                    sbcur = None
                    for Tt in range(16):
                        tc = Tt // 4
                        a_ = hc["atm"] % 2
                        hc["atm"] += 1
                        cols = slice(Tt * 128, (Tt + 1) * 128)
                        mk.op("pe", lambda e: e.matmul(ps[6][:, 0:128], kp[:, cols], qp[:, cols], start=True, stop=True),
                              reads=[bkp[tc], bqp[tc]], writes=[bps[6]], n=128)
                        mk.op("dve", lambda e: e.tensor_tensor(atm[a_][:, :], ps[6][:, 0:128], cst["hmask"][:, :], ALU.mult),
                              reads=[bps[6], bcst], writes=[batm[a_]], n=128)
                        if Tt % 4 == 0:
                            ob = 4 + (hc["o"] % 2)
                            hc["o"] += 1
                        oc = (Tt % 4) * 128
                        sb_for = []
                        for cc in range(2):
                            c = Tt * 2 + cc
                            sb_for.append(sbcur)
                            ub = 2 + cc
                            mk.op("pe", lambda e, cc=cc: e.matmul(ps[ub][:, 0:128], ktok[:, cc, Tt, :], Vh[:, Tt, :],
                                                                  start=True, stop=True),
                                  reads=[bktok[Tt // 8], bVh[Tt // 4]], writes=[bps[ub]], n=128)
                            nxt = 0 if scur is None else 1 - scur
                            if scur is None:
                                mk.op("dve", lambda e: e.tensor_copy(Sf[nxt][:, :], ps[ub][:, 0:128]), reads=[bps[ub]], writes=[bSf[nxt]])
                            else:
                                mk.op("dve", lambda e, c=c, scur=scur: e.scalar_tensor_tensor(Sf[nxt][:, :], Sf[scur][:, :], ebl[:, c:c + 1], ps[ub][:, 0:128],
                                                                                              ALU.mult, ALU.add),
                                      reads=[bSf[scur], bebl[tc], bps[ub]], writes=[bSf[nxt]], n=128)
                            scur = nxt
                            sbcur = hc["sb"] % 4
                            hc["sb"] += 1
                            mk.op("act", lambda e, sbcur=sbcur, scur=scur: e.copy(Sb[sbcur][:, :], Sf[scur][:, :]), reads=[bSf[scur]], writes=[bSb[sbcur]], n=128)
                        n_extra = sum(1 for x in sb_for if x is not None)
                        mk.op("pe", lambda e: e.matmul(ps[ob][:, oc:oc + 128], Vh[:, Tt, :], atm[a_][:, :], start=True, stop=(n_extra == 0)),
                              reads=[bVh[Tt // 4], batm[a_]], writes=[bps[ob]], mark=(n_extra == 0), n=128)
                        done = 0
                        for cc in range(2):
                            if sb_for[cc] is None:
                                continue
                            done += 1
                            mk.op("pe", lambda e, cc=cc: e.matmul(ps[ob][:, oc + cc * 64:oc + (cc + 1) * 64], Sb[sb_for[cc]][:, :],
                                                                  qp[:, Tt * 128 + cc * 64:Tt * 128 + (cc + 1) * 64], start=False, stop=(done == n_extra)),
                                  reads=[bSb[sb_for[cc]], bqp[tc]], writes=[bps[ob]], mark=(done == n_extra), n=64)
                        if Tt % 4 == 3:
                            finish_o(ob, tc * 512, 512, tc)
                    mk.dma("sp", O["nsp"][h], Sf[scur][:, :], rbuf=bSf[scur], is_output=True)
                    mk.op("act", lambda e: e.copy(S0b[:], S0[:]), reads=[bS0], writes=[bS0b])
                    mk.op("pe", lambda e: e.matmul(ps[6][0:32, 0:32], kp[:, NTOK:NCOLP], qp[:, NTOK:NCOLP], start=True, stop=True),
                          reads=[bkp[4], bqp[4]], writes=[bps[6]])
                    a_ = hc["atm"] % 2
                    hc["atm"] += 1
                    mk.op("dve", lambda e: e.tensor_tensor(atm[a_][0:32, 0:32], ps[6][0:32, 0:32], cst["smaskh"][:, :], ALU.mult),
                          reads=[bps[6], bcst], writes=[batm[a_]])
                    ob = 4 + (hc["o"] % 2)
                    hc["o"] += 1
                    mk.op("pe", lambda e: e.matmul(ps[ob][:, 0:32], Vhs[:, :], atm[a_][0:32, 0:32], start=True, stop=False),
                          reads=[bVhs, batm[a_]], writes=[bps[ob]], mark=False)
                    for s_ in range(4):
                        mk.op("pe", lambda e, s_=s_: e.matmul(ps[ob][:, s_ * 4:s_ * 4 + 4], S0b[:, s_, :], qp[:, NTOK + s_ * 4:NTOK + s_ * 4 + 4],
                                                              start=False, stop=(s_ == 3)),
                              reads=[bS0b, bqp[4]], writes=[bps[ob]], mark=(s_ == 3))
                    finish_o(ob, NTOK, 32, 4)
                    for s_ in range(4):
                        mk.op("dve", lambda e, s_=s_: e.tensor_scalar(ktm[:, s_, :], ktoks[:, :], cst["sels"][:, s_:s_ + 1], None, ALU.mult),
                              reads=[bktoks, bcst], writes=[bktm])
                    for s_ in range(4):
                        ub = 2 + s_ % 2
                        mk.op("pe", lambda e, s_=s_: e.matmul(ps[ub][:, 0:128], ktm[:, s_, :], Vhs[:, :], start=True, stop=True),
                              reads=[bktm, bVhs], writes=[bps[ub]])
                        mk.op("dve", lambda e, s_=s_: e.scalar_tensor_tensor(Sn[:, s_ % 2, :], S0[:, s_, :], ebl[:, 32 + s_:33 + s_], ps[ub][:, 0:128],
                                                                             ALU.mult, ALU.add),
                              reads=[bS0, bebl[4], bps[ub]], writes=[bSn])
                        if s_ % 2 == 1:
                            mk.dma("sp", nss4[h][:, s_ - 1:s_ + 1, :], Sn[:], rbuf=bSn, is_output=True)
                mk.barrier()

            def zsrc1(e_, tc):
                return (zT(e_), bz[e_][tc])

            def final1(ti, xt, bxt, nreal):
                row0, P = TILES[ti]
                if ti < 16:
                    mk.dma("sp", O["yp"][row0:row0 + P, :], xt[:P, :], rbuf=bxt, is_output=True)
                else:
                    mk.dma("sp", O["ys"][:, :], xt[:NS, :], rbuf=bxt, is_output=True)

            mk.barrier()
            with contextlib.ExitStack() as sD1:
                out_proj_stage(sD1, zsrc1, I["w_out_c"], 1, x1_dst, final1)
                mk.barrier()
            if dbg:
                mk.dma("sp", DBG["zT"][:, 0:8, :], ybT[:, :, 0:NCOL], rbuf=bdbg, is_output=True)
                mk.dma("sp", DBG["zT"][:, 8:16, :], zhi[:, :, 0:NCOL], rbuf=bdbg, is_output=True)
                mk.barrier()

        mk.finish()
        print("instr counts", mk.ninstr, "dma sems", mk.nsem)
    return nc


def make_in_maps(inputs, cores=range(8)):
    f = lambda a: np.ascontiguousarray(np.asarray(a, dtype=np.float32))
    consts = host_consts()
    pre, post = f(inputs["pre_norm"]), f(inputs["post_norm"])
    shared = {
        "w_in_ab": f(inputs["w_in_ab"][0]), "w_out_ab": f(inputs["w_out_ab"][0]),
        "w_in_c": f(inputs["w_in_c"][0]), "w_out_c": f(inputs["w_out_c"][0]),
        "gpre": f(pre.reshape(2, 8, 128).transpose(2, 0, 1)),
        "gpost": f(np.broadcast_to(post[None, :, :], (128, 2, 1024))),
        "convw": f(np.asarray(inputs["conv_w"][0]).reshape(31, 8, 128).transpose(2, 1, 0)),
        "convb": f(np.asarray(inputs["conv_b"][0]).reshape(8, 128).T),
        "lng": f(np.asarray(inputs["conv_ln_g"][0]).reshape(8, 128).T),
        "lnb": f(np.asarray(inputs["conv_ln_b"][0]).reshape(8, 128).T),
        "gnorm": f(np.asarray(inputs["hgrn_gnorm"][0]).reshape(128, 1)),
        "hlb": f(np.asarray(inputs["hgrn_lb"]).reshape(2, 16, 128).transpose(2, 0, 1)),
    }
    shared.update(consts)
    maps = []
    for c in cores:
        m = dict(shared)
        m["xp"] = f(inputs["x_prompt"][c])
        m["xs"] = f(np.asarray(inputs["x_sample"][4 * c:4 * c + 4]).reshape(NS, 1024))
        m["cconv"] = f(np.asarray(inputs["cache_conv"][0, 4 * c:4 * c + 4]).reshape(120, 1024))
        m["ck"] = f(np.asarray(inputs["cache_swa_k"][0, 4 * c:4 * c + 4]).reshape(4, 2048, 1024))
        m["cv"] = f(np.asarray(inputs["cache_swa_v"][0, 4 * c:4 * c + 4]).reshape(4, 2048, 1024))
        m["st"] = f(np.asarray(inputs["state_hgrn"][0, 4 * c:4 * c + 4]).reshape(64, 128, 128))
        maps.append(m)
    return maps


def assemble(results):
    n = len(results)
    g = lambda k: [np.asarray(r[k], dtype=np.float32) for r in results]
    yp = np.stack(g("yp"), 0)
    ys = np.concatenate([a.reshape(4, 4, 1024) for a in g("ys")], 0)
    ncp = np.stack(g("ncp"), 0)[None]
    ncs = np.concatenate(g("ncs"), 0)[None]
    nkp = np.stack([a.reshape(2048, 16, 64) for a in g("nkp")], 0)[None]
    nvp = np.stack([a.reshape(2048, 16, 64) for a in g("nvp")], 0)[None]
    nks = np.concatenate([a.reshape(4, 4, 16, 64) for a in g("nks")], 0)[None]
    nvs = np.concatenate([a.reshape(4, 4, 16, 64) for a in g("nvs")], 0)[None]
    nsp = np.stack(g("nsp"), 0)[None]
    nss = np.concatenate([a.reshape(4, 16, 128, 128) for a in g("nss")], 0)[None]
    return (yp, ys, ncp, ncs, nkp, nvp, nks, nvs, nsp, nss)


def kernel(**inputs):
    nc = build()
    maps = make_in_maps(inputs)
    res = run_bass_kernel_spmd(nc, maps, core_ids=list(range(8)))
    return assemble(res.results)
```

```python
import contextlib
import numpy as np
import ml_dtypes
import concourse.bass as bass
import concourse.mybir as mybir
from concourse.bass_utils import run_bass_kernel_spmd

F32 = mybir.dt.float32
BF16 = mybir.dt.bfloat16
AF = mybir.ActivationFunctionType
ALU = mybir.AluOpType
AX = mybir.AxisListType

ENGS = ("pe", "act", "dve", "pool", "sp")
NTOK = 2048
NS = 16
NCOL = NTOK + NS
NCOLP = NTOK + 32
EPS = 1e-6
TCH = [(0, 512), (512, 512), (1024, 512), (1536, 512), (2048, 32)]
PAST = 16384


import heapq
import os

SCHED = os.environ.get("MK_SCHED", "1") == "1"


class Buf:
    __slots__ = ("name", "excl", "lwn", "rdn", "wsem", "rsem", "wtot", "rtot")

    def __init__(self, name, excl=False):
        self.name = name
        self.excl = excl
        self.lwn = None
        self.rdn = []
        self.wsem = None
        self.rsem = None
        self.wtot = 0
        self.rtot = 0


class Node:
    __slots__ = ("idx", "e", "fns", "deps", "cost", "kind", "buf", "ticket", "semval", "start", "fin", "nd", "users", "is_output", "grp")

    def __init__(self, idx, e, fns, cost, kind="op", buf=None):
        self.idx = idx
        self.e = e
        self.fns = fns
        self.deps = set()
        self.cost = cost
        self.kind = kind
        self.buf = buf
        self.ticket = None
        self.semval = None
        self.start = None
        self.fin = 0.0
        self.nd = 0
        self.users = []
        self.is_output = False
        self.grp = None


import types


def freeze(fn):
    if fn.__closure__ is None:
        return fn
    cells = []
    for c in fn.__closure__:
        try:
            cells.append(types.CellType(c.cell_contents))
        except ValueError:
            cells.append(c)
    return types.FunctionType(fn.__code__, fn.__globals__, fn.__name__, fn.__defaults__, tuple(cells))


PER_ELEM = {"pe": 0.00046, "act": 0.00072, "dve": 0.0009, "pool": 0.0021, "sp": 0.0}
FIXED = {"pe": 0.04, "act": 0.14, "dve": 0.09, "pool": 0.3, "sp": 0.1}


class MK:
    def __init__(self, nc, stack):
        self.nc = nc
        self.stack = stack
        self.eng = {"pe": nc.tensor, "act": nc.scalar, "dve": nc.vector, "pool": nc.gpsimd, "sp": nc.sync}
        self.sem = {e: stack.enter_context(nc.semaphore("s_" + e)) for e in ENGS}
        self.cnt = {e: 0 for e in ENGS}
        self.seen = {e: {} for e in ENGS}
        self.nsem = 0
        self.ninstr = {e: 0 for e in ENGS}
        self.nodes = []
        self.nidx = 0
        self.pend = {e: [] for e in ENGS}
        self.dmabufs = []
        self.outbufs = []

    def _add_writer_dep(self, node, w):
        if w.grp is not None:
            node.deps.update(w.grp)
        else:
            node.deps.add(w)

    def _deps(self, node, reads, writes):
        for b in reads:
            if b.lwn is not None:
                self._add_writer_dep(node, b.lwn)
            if b.excl:
                for r in b.rdn:
                    if r.e != node.e:
                        node.deps.add(r)
        for b in writes:
            if b.lwn is not None:
                self._add_writer_dep(node, b.lwn)
            for r in b.rdn:
                node.deps.add(r)
        for b in reads:
            b.rdn.append(node)
        for b in writes:
            b.lwn = node
            b.rdn = []
        node.deps.discard(node)

    def op(self, e, fn, reads=(), writes=(), mark=True, n=512):
        cost = FIXED[e] + PER_ELEM[e] * n
        fn = freeze(fn)
        if not mark:
            self.pend[e].append((fn, list(reads), list(writes), cost))
            return
        fns = [fn]
        rr, ww = list(reads), list(writes)
        if self.pend[e]:
            pre = self.pend[e]
            self.pend[e] = []
            fns = [p[0] for p in pre] + fns
            for p in pre:
                rr += p[1]
                ww += p[2]
                cost += p[3]
            rr = list(dict.fromkeys(rr))
            ww = list(dict.fromkeys(ww))
        node = Node(self.nidx, e, fns, cost)
        self.nidx += 1
        self._deps(node, rr, ww)
        self.nodes.append(node)

    def newsem(self):
        self.nsem += 1
        return self.stack.enter_context(self.nc.semaphore("d%d" % self.nsem))

    def dma(self, q, out, in_, wbuf=None, rbuf=None, is_output=False, after=(), **kw):
        assert not self.pend[q]
        fn = lambda eng: eng.dma_start(out=out, in_=in_, **kw)
        try:
            dcost = 2.0 + max(in_.nbytes(), out.nbytes()) / 150e3
        except Exception:
            dcost = 2.5
        if wbuf is not None:
            assert rbuf is None
            node = Node(self.nidx, q, [fn], dcost, "dmaw", wbuf)
            prev = wbuf.lwn
            if prev is not None and prev.kind == "dmaw" and prev.buf is wbuf and prev.e == q and not wbuf.rdn:
                node.deps = set(prev.deps)
                if prev.grp is None:
                    prev.grp = [prev]
                prev.grp.append(node)
                node.grp = prev.grp
                wbuf.lwn = node
            else:
                self._deps(node, [], [wbuf])
            if wbuf.wsem is None:
                wbuf.wsem = self.newsem()
                self.dmabufs.append(wbuf)
        else:
            node = Node(self.nidx, q, [fn], dcost, "dmar", rbuf)
            self._deps(node, [rbuf], [])
            if rbuf.rsem is None:
                rbuf.rsem = self.newsem()
                self.dmabufs.append(rbuf)
            node.is_output = is_output
            if is_output and rbuf not in self.outbufs:
                self.outbufs.append(rbuf)
        for b in after:
            if b.lwn is not None:
                self._add_writer_dep(node, b.lwn)
        self.nidx += 1
        self.nodes.append(node)

    def dma_d2d(self, q, out, in_, tok):
        fn = lambda eng: eng.dma_start(out=out, in_=in_)
        node = Node(self.nidx, q, [fn], 2.5, "dmar", tok)
        self.nidx += 1
        if tok.rsem is None:
            tok.rsem = self.newsem()
            self.dmabufs.append(tok)
        if tok not in self.outbufs:
            self.outbufs.append(tok)
        tok.rdn.append(node)
        self.nodes.append(node)

    def _schedule(self, nodes):
        inseg = set(id(n) for n in nodes)
        for n in nodes:
            n.nd = 0
            n.users = []
        for n in nodes:
            for d in n.deps:
                if id(d) in inseg:
                    n.nd += 1
                    d.users.append(n)
        if not SCHED:
            return {e: [n for n in nodes if n.e == e] for e in ENGS}
        PRIO = os.environ.get("MK_PRIO", "bl")
        bl = {}
        for n in reversed(nodes):
            m = 0.0
            for u in n.users:
                if bl[id(u)] > m:
                    m = bl[id(u)]
            bl[id(n)] = m + n.cost + 0.15
        if PRIO == "bl":
            key = lambda n: (-bl[id(n)], n.idx)
        else:
            key = lambda n: (n.idx, 0)
        CP = {} if os.environ.get("MK_SIMLOG") == "2" else None
        free = {e: 0.0 for e in ENGS}
        pending = {e: [] for e in ENGS}
        avail = {e: [] for e in ENGS}
        ready_t = {}
        for n in nodes:
            if n.nd == 0:
                heapq.heappush(pending[n.e], (0.0, n.idx, n))
        order = {e: [] for e in ENGS}
        left = len(nodes)
        while left:
            best = None
            for e in ENGS:
                pe_, av = pending[e], avail[e]
                while pe_ and pe_[0][0] <= free[e]:
                    _, i_, n_ = heapq.heappop(pe_)
                    heapq.heappush(av, (key(n_), i_, n_))
                if av:
                    cand = (free[e], av[0][0], e, 0)
                elif pe_:
                    cand = (pe_[0][0], key(pe_[0][2]), e, 1)
                else:
                    continue
                if best is None or cand < best:
                    best = cand
            stt, _, e, src = best
            if src == 0:
                _, _, n = heapq.heappop(avail[e])
            else:
                _, _, n = heapq.heappop(pending[e])
            n.start = stt
            if CP is not None:
                dd = max((d for d in n.deps if id(d) in inseg), key=lambda d: d.fin, default=None)
                if order[e] and abs(free[e] - stt) < 1e-9:
                    CP[id(n)] = ("eng", order[e][-1])
                elif dd is not None:
                    CP[id(n)] = ("dep", dd)
            issue = 0.3 if n.kind != "op" else n.cost
            free[e] = stt + issue
            n.fin = stt + n.cost
            order[e].append(n)
            left -= 1
            for u in n.users:
                u.nd -= 1
                rt = ready_t.get(id(u), 0.0)
                lat = n.fin + (0.15 if u.e != n.e else 0.08)
                if lat > rt:
                    rt = lat
                ready_t[id(u)] = rt
                if u.nd == 0:
                    heapq.heappush(pending[u.e], (rt, u.idx, u))
        if os.environ.get("MK_SIMLOG"):
            mx = max(n.fin for n in nodes)
            busy = {e: round(sum(n.cost for n in order[e])) for e in ENGS}
            print("SEG nodes=%d sim_makespan=%.0fus busy=%s" % (len(nodes), mx, busy))
            if CP is not None:
                cur = max(nodes, key=lambda n: n.fin)
                acc = {}
                path = []
                while cur is not None:
                    why, prv = CP.get(id(cur), (None, None))
                    k_ = (cur.e, cur.kind, why)
                    acc[k_] = acc.get(k_, 0.0) + (cur.fin - cur.start)
                    path.append((round(cur.start, 1), cur.e, cur.kind, why, cur.idx))
                    cur = prv
                print("   critical path time by (engine, kind, reason):", {k: round(v) for k, v in sorted(acc.items(), key=lambda x: -x[1])})
                print("   path len", len(path), "sample:", path[:: max(1, len(path) // 25)])
        return order

    def _wait(self, e, key, sem, val):
        if val <= 0 or self.seen[e].get(key, 0) >= val:
            return
        self.seen[e][key] = val
        self.eng[e].wait_ge(sem, val)
        self.ninstr[e] += 1

    def _wait_node(self, e, d):
        if d.kind == "op":
            if d.e == "pe" and e == "pe":
                return
            self._wait(e, d.e, self.sem[d.e], d.ticket)
        elif d.kind == "dmaw":
            self._wait(e, ("w", id(d.buf)), d.buf.wsem, d.semval)
        else:
            self._wait(e, ("r", id(d.buf)), d.buf.rsem, d.semval)

    def flush(self):
        for e in ENGS:
            assert not self.pend[e], "dangling unmarked ops on " + e
        nodes = self.nodes
        self.nodes = []
        if not nodes:
            return
        order = self._schedule(nodes)
        for e in ENGS:
            for n in order[e]:
                if n.kind == "op":
                    self.cnt[e] += 1
                    n.ticket = self.cnt[e]
                elif n.kind == "dmaw":
                    n.buf.wtot += 16
                    n.semval = n.buf.wtot
                else:
                    n.buf.rtot += 16
                    n.semval = n.buf.rtot
        for e in ENGS:
            for n in order[e]:
                dmax = {}
                for d in sorted(n.deps, key=lambda x: x.idx):
                    if d.kind == "op":
                        self._wait_node(e, d)
                    else:
                        k_ = (d.kind, id(d.buf))
                        if k_ not in dmax or d.semval > dmax[k_].semval:
                            dmax[k_] = d
                for d in dmax.values():
                    self._wait_node(e, d)
                ins = None
                for fn in n.fns:
                    ins = fn(self.eng[e])
                    self.ninstr[e] += 1
                if n.kind == "op":
                    ins.then_inc(self.sem[e], 1)
                elif n.kind == "dmaw":
                    ins.then_inc(n.buf.wsem, 16)
                else:
                    ins.then_inc(n.buf.rsem, 16)
                n.fns = None

    def barrier(self):
        self.flush()
        for e in ENGS:
            for f in ENGS:
                if not (e == "pe" and f == "pe"):
                    self._wait(e, f, self.sem[f], self.cnt[f])
            for b in self.dmabufs:
                if b.wsem is not None:
                    self._wait(e, ("w", id(b)), b.wsem, b.wtot)
                if b.rsem is not None:
                    self._wait(e, ("r", id(b)), b.rsem, b.rtot)

    def wait_dma_reads(self, e, bufs):
        self.flush()
        for b in bufs:
            if b.rsem is not None:
                self._wait(e, ("r", id(b)), b.rsem, b.rtot)

    def finish(self):
        self.flush()
        for b in self.outbufs:
            self._wait("sp", ("r", id(b)), b.rsem, b.rtot)
        for f in ENGS:
            if f != "sp":
                self._wait("sp", f, self.sem[f], self.cnt[f])


def host_consts():
    c = {}
    c["identb"] = np.eye(128, dtype=np.float32).astype(ml_dtypes.bfloat16)
    c["identf"] = np.eye(128, dtype=np.float32)
    perm = np.zeros((128, 128), np.float32)
    for m in range(128):
        base = (m // 64) * 64
        src = base + ((m % 64) + 32) % 64
        perm[src, m] = 1.0
    c["permb"] = perm.astype(ml_dtypes.bfloat16)
    half = 32
    inv_freq = 10000.0 ** (-(np.arange(half, dtype=np.float64) / half))
    pos = np.concatenate([np.arange(NTOK, dtype=np.float64),
                          np.tile(PAST + np.arange(4, dtype=np.float64), 4), np.zeros(16, np.float64)])
    ang = pos[None, :] * inv_freq[:, None]
    p = np.arange(128)
    fi = (p % 64) % 32
    sign = np.where((p % 64) < 32, -1.0, 1.0)
    c["cosT"] = np.cos(ang)[fi, :].astype(np.float32)
    c["sinT"] = (np.sin(ang)[fi, :] * sign[:, None]).astype(np.float32)
    def mult(dist):
        m = np.zeros(dist.shape, np.float32)
        ok = dist >= 0
        m += ok & (dist <= 128)
        m += ok & (dist % 4 == 0) & (dist <= 512)
        m += ok & (dist % 16 == 0) & (dist <= 2048)
        return m
    k = np.arange(128)[:, None]
    q = np.arange(19 * 128)[None, :] - 3 * 128
    c["maskall"] = mult(q - k).astype(ml_dtypes.bfloat16)
    sm = np.zeros((128, 4, 8, 4), np.float32)
    pp = np.arange(128)
    for tq in range(4):
        for it in range(3):
            row = 16 * (32 * it + (pp % 32)) + (pp // 32)
            sm[:, :, it, tq] = mult(2048 + tq - row)[:, None]
        for it in range(3, 7):
            row = 1536 + 128 * (it - 3) + pp
            sm[:, :, it, tq] = mult(2048 + tq - row)[:, None]
        for sq in range(4):
            for tk in range(4):
                if tk <= tq:
                    sm[sq * 4 + tk, sq, 7, tq] = mult(np.array([tq - tk]))[0]
    c["smask"] = sm.astype(ml_dtypes.bfloat16)
    rm = np.ones((128, 512), np.float32); rm[:, 0::64] = 0.0
    c["resetm"] = rm.astype(ml_dtypes.bfloat16)
    rs = np.ones((128, 32), np.float32); rs[:, 0::4] = 0.0
    c["resets"] = rs.astype(ml_dtypes.bfloat16)
    a = np.arange(32)
    c["smaskh"] = ((a[:, None] <= a[None, :]) & (a[:, None] // 4 == a[None, :] // 4) & (a[None, :] < 16)).astype(np.float32).astype(ml_dtypes.bfloat16)
    sel = np.zeros((32, 4), np.float32)
    for sq in range(4):
        sel[sq * 4:(sq + 1) * 4, sq] = 1.0
    c["sels"] = sel
    s = np.arange(128)[:, None]
    t = np.arange(128)[None, :]
    c["hmask"] = ((s <= t) & (s // 64 == t // 64)).astype(np.float32).astype(ml_dtypes.bfloat16)
    return c


CONST_SPECS = [("identb", [128, 128], BF16), ("identf", [128, 128], F32), ("permb", [128, 128], BF16),
               ("cosT", [128, NCOLP], F32), ("sinT", [128, NCOLP], F32), ("maskall", [128, 19 * 128], BF16),
               ("hmask", [128, 128], BF16), ("smask", [128, 4, 8, 4], BF16),
               ("resetm", [128, 512], BF16), ("resets", [128, 32], BF16), ("smaskh", [32, 32], BF16), ("sels", [32, 4], F32)]

IN_SPECS = [
    ("xp", [NTOK, 1024]), ("xs", [NS, 1024]), ("cconv", [4 * 30, 1024]), ("ck", [4, 2048, 1024]),
    ("cv", [4, 2048, 1024]), ("st", [4 * 16, 128, 128]),
    ("w_in_ab", [1024, 7168]), ("w_out_ab", [2048, 1024]), ("w_in_c", [1024, 8192]), ("w_out_c", [2048, 1024]),
    ("gpre", [128, 2, 8]), ("gpost", [128, 2, 1024]), ("convw", [128, 8, 31]), ("convb", [128, 8]),
    ("lng", [128, 8]), ("lnb", [128, 8]), ("gnorm", [128, 1]), ("hlb", [128, 2, 16]),
]

OUT_SPECS = [
    ("yp", [NTOK, 1024]), ("ys", [NS, 1024]), ("ncp", [30, 1024]), ("ncs", [4, 30, 1024]),
    ("nkp", [NTOK, 1024]), ("nvp", [NTOK, 1024]), ("nks", [NS, 1024]), ("nvs", [NS, 1024]),
    ("nsp", [16, 128, 128]), ("nss", [4 * 16, 128, 128]),
]


def build(stop_after=None, dbg=False):
    import os
    LIM = os.environ.get('MK_LIM', '')
    nc = bass.Bass("TRN2", target_bir_lowering=False)
    I = {n: nc.dram_tensor(n, s, F32, kind="ExternalInput").ap() for n, s in IN_SPECS}
    C = {n: nc.dram_tensor(n, s, d, kind="ExternalInput").ap() for n, s, d in CONST_SPECS}
    O = {n: nc.dram_tensor(n, s, F32, kind="ExternalOutput").ap() for n, s in OUT_SPECS}
    x1s = nc.dram_tensor("x1s", [NCOL, 1024], F32, kind="Internal").ap()
    woutb = [nc.dram_tensor("woutb%d" % i, [2048, 1024], BF16, kind="Internal").ap() for i in range(2)]
    kvselb = [nc.dram_tensor("kvselb%d" % i, [4, 128, 7, 1024], BF16, kind="Internal").ap() for i in range(2)]
    DBG = {}
    if dbg:
        DBG["x1"] = nc.dram_tensor("dbg_x1", [NCOL, 1024], F32, kind="ExternalOutput").ap()
        DBG["ybT"] = nc.dram_tensor("dbg_ybT", [128, 8, NCOL], BF16, kind="ExternalOutput").ap()
        DBG["hT"] = nc.dram_tensor("dbg_hT", [128, 8, NCOL], BF16, kind="ExternalOutput").ap()
        DBG["yaT"] = nc.dram_tensor("dbg_yaT", [128, 8, NCOL], BF16, kind="ExternalOutput").ap()
        DBG["zT"] = nc.dram_tensor("dbg_zT", [128, 16, NCOL], BF16, kind="ExternalOutput").ap()

    with contextlib.ExitStack() as st:
        mk = MK(nc, st)

        def T(name, shape, dt=F32, stack=st):
            return stack.enter_context(nc.sbuf_tensor("sb_" + name, shape, dt))

        def PS(name, shape, dt=F32, stack=st):
            return stack.enter_context(nc.psum_tensor("pp_" + name, shape, dt))

        ps = [PS("ps%d" % i, [128, 512]) for i in range(7)]
        bps = [Buf("ps%d" % i, excl=True) for i in range(7)]
        psb = PS("psb", [128, 1024], BF16)
        bpsb = Buf("psb", excl=True)

        cst = {}
        bcst = Buf("consts")
        for n, s, d in CONST_SPECS:
            if n in ("cosT", "sinT", "maskall", "smask"):
                continue
            cst[n] = T("c_" + n, s, d)
            mk.dma("sp", cst[n][:], C[n], wbuf=bcst)
        prm = {}
        for n in ("gpre", "convw", "convb", "lng", "lnb", "gnorm", "hlb"):
            s = dict(IN_SPECS)[n]
            prm[n] = T("p_" + n, s, F32)
            mk.dma("sp", prm[n][:], I[n], wbuf=bcst)
        onesb = T("onesb", [128, 128], BF16)
        bones = Buf("onesb")
        mk.op("dve", lambda e: e.memset(onesb[:], 1.0), writes=[bones])

        bwoutb = [Buf("woutb%d" % i) for i in range(2)]

        def precast_wout(i_, gate):
            nm_ = ("w_out_ab", "w_out_c")[i_]
            for g4 in range(4):
                mk.dma("pool", woutb[i_][g4 * 512:(g4 + 1) * 512, :], I[nm_][g4 * 512:(g4 + 1) * 512, :], wbuf=bwoutb[i_], after=gate)

        hT = T("hT", [128, 8, NCOLP], BF16)
        bhT = [Buf("hT%d" % i) for i in range(17)]

        ybT = T("ybT", [128, 8, NCOLP], BF16)
        bybT = [[Buf("ybT%d_%d" % (hp, tc)) for tc in range(5)] for hp in range(8)]

        def hT_bufs(tc):
            return [bhT[16]] if tc == 4 else bhT[4 * tc:4 * tc + 4]

        TILES = [(i * 128, 128) for i in range(16)] + [(NTOK, 32)]

        nrm = {"i": 0}
        junk = T("junk", [128, 1024], BF16)
        bjunk = Buf("junk")
        hb = [T("hb%d" % i, [128, 1024], BF16) for i in range(2)]
        bhb = [Buf("hb%d" % i) for i in range(2)]
        ssb = [T("ss%d" % i, [128, 4], F32) for i in range(2)]
        bss = [Buf("ss%d" % i) for i in range(2)]

        def rms_stats(xt, bx, P, width, ss, bssb):
            mk.op("act", lambda e: e.activation(junk[:P, 0:width], xt, AF.Square, accum_out=ss[:P, 0:1]),
                  reads=[bx], writes=[bjunk, bssb])
            mk.op("dve", lambda e: e.tensor_scalar(ss[:P, 1:2], ss[:P, 0:1], 1.0 / width, EPS, ALU.mult, ALU.add),
                  reads=[bssb], writes=[bssb])
            mk.op("act", lambda e: e.activation(ss[:P, 1:2], ss[:P, 1:2], AF.Ln), reads=[bssb], writes=[bssb])
            mk.op("act", lambda e: e.activation(ss[:P, 2:3], ss[:P, 1:2], AF.Exp, scale=-0.5), reads=[bssb], writes=[bssb])

        def norm_to_hT(xt, bx, ti, layer):
            row0, P = TILES[ti]
            k = nrm["i"] % 2
            nrm["i"] += 1
            ss, bs_ = ssb[k], bss[k]
            rms_stats(xt[:P, :], bx, P, 1024, ss, bs_)
            mk.op("act", lambda e: e.activation(hb[k][:P, :], xt[:P, :], AF.Identity, scale=ss[:P, 2:3]),
                  reads=[bx, bs_], writes=[bhb[k]])
            if 'noT' in LIM:
                return
            for d in range(8):
                mk.op("pe", lambda e, d=d: e.transpose(psb[:, d * 128:d * 128 + P], hb[k][:P, d * 128:(d + 1) * 128],
                                                      cst["identb"][:P, :P]),
                      reads=[bhb[k], bcst], writes=[bpsb], mark=(d == 7), n=128)
            src = psb[:, :].rearrange("p (d t) -> p d t", t=128)[:, :, 0:P]
            g = prm["gpre"][:, layer, :].rearrange("p (d o) -> p d o", o=1).to_broadcast([128, 8, P])
            mk.op("dve", lambda e: e.tensor_tensor(hT[:, :, row0:row0 + P], src, g, ALU.mult),
                  reads=[bpsb, bcst], writes=[bhT[ti]])

        xr = [T("xr%d" % i, [128, 1024], F32) for i in range(2)]
        bxr = [Buf("xr%d" % i) for i in range(2)]

        def x_src(ti):
            row0, P = TILES[ti]
            return I["xp"][row0:row0 + P, :] if ti < 16 else I["xs"][:, :]

        for ti in range((16 if 'nosamp' in LIM else 17) if 'ti1' not in LIM else 1):
            row0, P = TILES[ti]
            k = ti % 2
            mk.dma("sp", xr[k][:(P if ti < 16 else NS), :], x_src(ti), wbuf=bxr[k])
            if 'noN' not in LIM:
                norm_to_hT(xr[k], bxr[k], ti, 0)

        NW = 2
        wsl = [T("w%d" % i, [128, 8, 512], BF16) for i in range(NW)]
        bws = [Buf("w%d" % i) for i in range(NW)]
        wstate = {"n": 0}

        def wload(parts):
            k = wstate["n"] % NW
            wstate["n"] += 1
            off = 0
            for ap in parts:
                w = ap.shape[1]
                mk.dma("pool", wsl[k][:, :, off:off + w], ap.rearrange("(c p) f -> p c f", p=128), wbuf=bws[k])
                off += w
            return wsl[k], bws[k]

        def proj_fm(w, bw, f0, tc, bank):
            c0, wd = TCH[tc]
            pt_, pb_ = (ps[bank], bps[bank]) if isinstance(bank, int) else bank
            for d in range(8):
                mk.op("pe", lambda e, d=d: e.matmul(pt_[:, 0:wd], w[:, d, f0:f0 + 128], hT[:, d, c0:c0 + wd],
                                                    start=(d == 0), stop=(d == 7)),
                      reads=[bw] + hT_bufs(tc), writes=[pb_], mark=(d == 7), n=wd)

        def sigmoid_to(dst, bdst, src, bsrc, keep_e=None):
            e_ap, e_b = keep_e if keep_e is not None else (dst, bdst)
            mk.op("act", lambda e: e.activation(e_ap, src, AF.Exp, scale=-1.0), reads=[bsrc], writes=[e_b])
            mk.op("act", lambda e: e.activation(dst, e_ap, AF.Ln, bias=1.0), reads=[e_b], writes=[bdst])
            mk.op("act", lambda e: e.activation(dst, dst, AF.Exp, scale=-1.0), reads=[bdst], writes=[bdst])

        def silu_to(dst, bdst, src, bsrc, tmp, btmp, eng="dve", xs=None):
            if os.environ.get("MK_NOXS"):
                xs = None
            if xs is not None:
                mk.op("dve", lambda e: e.tensor_copy(xs[0], src), reads=[bsrc], writes=[xs[1]])
            sigmoid_to(tmp, btmp, src, bsrc)
            if xs is not None:
                mk.op(eng, lambda e: e.tensor_tensor(dst, xs[0], tmp, ALU.mult), reads=[xs[1], btmp], writes=[bdst])
            else:
                mk.op(eng, lambda e: e.tensor_tensor(dst, src, tmp, ALU.mult), reads=[bsrc, btmp], writes=[bdst])

        sAtt = st.enter_context(contextlib.ExitStack())
        KTs = T("KTs", [128, 8, 32], BF16, sAtt)
        QTs = T("QTs", [128, 8, 32], BF16, sAtt)
        SGs = T("SGs", [128, 8, 32], BF16, sAtt)
        VAs = T("VAs", [32, 8, 192], BF16, sAtt)
        bsmp = [Buf("smp%d" % hp) for hp in range(8)]
        mk.op("dve", lambda e: e.memset(VAs[:, :, 64:128], 1.0), writes=bsmp)

        with contextlib.ExitStack() as sB:
            if 'noB' in LIM:
                raise_skip = True
            cosT = T("cosT", [128, NCOLP], F32, sB)
            sinT = T("sinT", [128, NCOLP], F32, sB)
            maskall = T("maskall", [128, 19 * 128], BF16, sB)
            brope = Buf("rope")
            mk.dma("sp", cosT[:], C["cosT"], wbuf=brope)
            mk.dma("sp", sinT[:], C["sinT"], wbuf=brope)
            mk.dma("sp", maskall[:], C["maskall"], wbuf=brope)
            KT = [T("KT%d" % i, [128, 2, NCOLP], BF16, sB) for i in range(2)]
            bKT = [Buf("KT%d" % i) for i in range(2)]
            for i in range(2):
                mk.op("dve", lambda e, i=i: e.memset(KT[i][:, :, :], 0.0), writes=[bKT[i]], n=4200)
            QT = [T("QT%d" % i, [128, NCOLP], BF16, sB) for i in range(2)]
            bQT = [Buf("QT%d" % i) for i in range(2)]
            SG = [T("SG%d" % i, [128, NCOLP], BF16, sB) for i in range(2)]
            bSG = [Buf("SG%d" % i) for i in range(2)]
            VA = [T("VA%d" % i, [128, 16, 192], BF16, sB) for i in range(2)]
            bVA = [Buf("VA%d" % i) for i in range(2)]
            for i in range(2):
                mk.op("dve", lambda e, i=i: e.memset(VA[i][:, :, 64:128], 1.0), writes=[bVA[i]])
            raw = [T("raw%d" % i, [128, 512], BF16, sB) for i in range(2)]
            braw = [Buf("raw%d" % i) for i in range(2)]
            t1 = [T("t1_%d" % i, [128, 512], F32, sB) for i in range(2)]
            bt1 = [Buf("t1_%d" % i) for i in range(2)]
            t2 = [T("t2_%d" % i, [128, 512], F32, sB) for i in range(2)]
            bt2 = [Buf("t2_%d" % i) for i in range(2)]
            kst = [T("kst%d" % i, [128, 4, 128], F32, sB) for i in range(2)]
            bkst = [Buf("kst%d" % i) for i in range(2)]
            pt = [T("pt%d" % i, [128, 512], BF16, sB) for i in range(4)]
            bpt = [Buf("pt%d" % i) for i in range(4)]
            rz = T("rz", [128, 512], F32, sB)
            brz = Buf("rz")
            ytmp = T("ytmp", [128, 512], F32, sB)
            bytmp = Buf("ytmp")
            ctr = {"rope": 0, "kst": 0, "pt": 0, "st": 0, "ot": 0, "sg": 0}
            sgtB = [T("sgtB%d" % i, [128, 512], F32, sB) for i in range(2)]
            bsgtB = [Buf("sgtB%d" % i) for i in range(2)]

            def rope_chunk(bank, tc, dst, bdst, want_f32):
                c0, wd = TCH[tc]
                k = ctr["rope"] % 2
                ctr["rope"] += 1
                mk.op("act", lambda e: e.copy(raw[k][:, 0:wd], ps[bank][:, 0:wd]), reads=[bps[bank]], writes=[braw[k]])
                mk.op("dve", lambda e: e.tensor_tensor(t1[k][:, 0:wd], ps[bank][:, 0:wd], cosT[:, c0:c0 + wd], ALU.mult),
                      reads=[bps[bank], brope], writes=[bt1[k]])
                mk.op("pe", lambda e: e.matmul(ps[2][:, 0:wd], cst["permb"][:], raw[k][:, 0:wd], start=True, stop=True),
                      reads=[bcst, braw[k]], writes=[bps[2]])
                mk.op("dve", lambda e: e.tensor_tensor(t2[k][:, 0:wd], ps[2][:, 0:wd], sinT[:, c0:c0 + wd], ALU.mult),
                      reads=[bps[2], brope], writes=[bt2[k]])
                if want_f32:
                    mk.op("dve", lambda e: e.tensor_tensor(t1[k][:, 0:wd], t1[k][:, 0:wd], t2[k][:, 0:wd], ALU.add),
                          reads=[bt1[k], bt2[k]], writes=[bt1[k]])
                    mk.op("act", lambda e: e.copy(dst[0:64, 0, c0:c0 + wd], t1[k][0:64, 0:wd]), reads=[bt1[k]], writes=[bdst])
                    mk.op("dve", lambda e: e.tensor_copy(dst[64:128, 1, c0:c0 + wd], t1[k][64:128, 0:wd]), reads=[bt1[k]], writes=[bdst])
                    return t1[k], bt1[k]
                mk.op("dve", lambda e: e.tensor_tensor(dst[:, c0:c0 + wd], t1[k][:, 0:wd], t2[k][:, 0:wd], ALU.add),
                      reads=[bt1[k], bt2[k]], writes=[bdst])
                return None, None

            psbF = psb[:, :].bitcast(F32)
            ST_RING = [(ps[3], bps[3]), (ps[4], bps[4]), (psbF, bpsb)]

            def proj_gen(hp, kk):
                w, bw = wload([I["w_in_ab"][:, 3072 + hp * 128:3072 + (hp + 1) * 128],
                               I["w_in_ab"][:, 4096 + hp * 128:4096 + (hp + 1) * 128],
                               I["w_in_ab"][:, 5120 + hp * 128:5120 + (hp + 1) * 128],
                               I["w_in_ab"][:, 6144 + hp * 128:6144 + (hp + 1) * 128]])
                for tc in range(5):
                    c0, wd = TCH[tc]
                    bank = tc % 2
                    proj_fm(w, bw, 128, tc, bank)
                    kf, bkf = rope_chunk(bank, tc, KT[kk], bKT[kk], True)
                    ks = ctr["kst"] % 2
                    ctr["kst"] += 1
                    nsub = (wd + 127) // 128
                    for j in range(nsub):
                        P = min(128, wd - j * 128)
                        mk.op("pe", lambda e, j=j, P=P: e.transpose(ps[2][0:P, j * 128:(j + 1) * 128],
                                                                      kf[:, j * 128:j * 128 + P], cst["identf"][:, :]),
                              reads=[bkf, bcst], writes=[bps[2]], mark=(j == nsub - 1), n=128)
                    P = min(128, wd)
                    mk.op("act", lambda e, P=P, nsub=nsub: e.copy(
                        kst[ks][0:P, 0:nsub, :], ps[2][0:P, 0:nsub * 128].rearrange("p (j f) -> p j f", f=128)),
                        reads=[bps[2]], writes=[bkst[ks]])
                    if tc < 4:
                        dst = O["nkp"][c0:c0 + 512, hp * 128:(hp + 1) * 128].rearrange("(j p) f -> p j f", p=128)
                        mk.dma("sp", dst, kst[ks][:, :, :], rbuf=bkst[ks], is_output=True)
                    else:
                        mk.dma("sp", O["nks"][:, hp * 128:(hp + 1) * 128], kst[ks][0:NS, 0, :], rbuf=bkst[ks], is_output=True)
                    yield
                for g in range(5):
                    bank = g % 2
                    if g < 4:
                        for j in range(4):
                            ti = g * 4 + j
                            for d in range(8):
                                mk.op("pe", lambda e, d=d, j=j, ti=ti: e.matmul(
                                    ps[bank][:, j * 128:(j + 1) * 128], hT[:, d, ti * 128:(ti + 1) * 128],
                                    w[:, d, 256:384], start=(d == 0), stop=(d == 7)),
                                    reads=[bw, bhT[ti]], writes=[bps[bank]], mark=(d == 7 and j == 3), n=128)
                        src = ps[bank][:, :].rearrange("p (j f) -> p j f", f=128)
                        mk.op("act", lambda e: e.copy(VA[kk][:, g * 4:g * 4 + 4, 0:64], src[:, :, 0:64]),
                              reads=[bps[bank]], writes=[bVA[kk]])
                        mk.op("act", lambda e: e.copy(VA[kk][:, g * 4:g * 4 + 4, 128:192], src[:, :, 64:128]),
                              reads=[bps[bank]], writes=[bVA[kk]])
                        ks = ctr["kst"] % 2
                        ctr["kst"] += 1
                        mk.op("dve", lambda e: e.tensor_copy(kst[ks][:, :, :], src), reads=[bps[bank]], writes=[bkst[ks]])
                        dst = O["nvp"][g * 512:(g + 1) * 512, hp * 128:(hp + 1) * 128].rearrange("(j p) f -> p j f", p=128)
                        mk.dma("sp", dst, kst[ks][:, :, :], rbuf=bkst[ks], is_output=True)
                    else:
                        for d in range(8):
                            mk.op("pe", lambda e, d=d: e.matmul(ps[bank][0:32, 0:128], hT[:, d, NTOK:NCOLP], w[:, d, 256:384],
                                                                start=(d == 0), stop=(d == 7)),
                                  reads=[bw, bhT[16]], writes=[bps[bank]], mark=(d == 7), n=128)
                        ks = ctr["kst"] % 2
                        ctr["kst"] += 1
                        mk.op("dve", lambda e: e.tensor_copy(kst[ks][0:32, 0, :], ps[bank][0:32, 0:128]),
                              reads=[bps[bank]], writes=[bkst[ks]])
                        mk.op("act", lambda e: e.copy(VAs[:, hp, 0:64], ps[bank][0:32, 0:64]), reads=[bps[bank]], writes=[bsmp[hp]])
                        mk.op("act", lambda e: e.copy(VAs[:, hp, 128:192], ps[bank][0:32, 64:128]), reads=[bps[bank]], writes=[bsmp[hp]])
                        mk.dma("sp", O["nvs"][:, hp * 128:(hp + 1) * 128], kst[ks][0:NS, 0, :], rbuf=bkst[ks], is_output=True)
                    yield
                for tc in range(5):
                    bank = tc % 2
                    proj_fm(w, bw, 0, tc, bank)
                    rope_chunk(bank, tc, QT[kk], bQT[kk], False)
                    yield
                for tc in range(5):
                    c0, wd = TCH[tc]
                    bank = tc % 2
                    proj_fm(w, bw, 384, tc, bank)
                    k_ = ctr["sg"] % 2
                    ctr["sg"] += 1
                    silu_to(SG[kk][:, c0:c0 + wd], bSG[kk], ps[bank][:, 0:wd], bps[bank], sgtB[k_][:, 0:wd], bsgtB[k_],
                            eng="dve", xs=(t2[k_][:, 0:wd], bt2[k_]))
                    yield
                mk.op("dve", lambda e: e.tensor_copy(KTs[0:64, hp, :], KT[kk][0:64, 0, NTOK:NCOLP]), reads=[bKT[kk]], writes=[bsmp[hp]])
                mk.op("dve", lambda e: e.tensor_copy(KTs[64:128, hp, :], KT[kk][64:128, 1, NTOK:NCOLP]), reads=[bKT[kk]], writes=[bsmp[hp]])
                mk.op("dve", lambda e: e.tensor_copy(QTs[:, hp, :], QT[kk][:, NTOK:NCOLP]), reads=[bQT[kk]], writes=[bsmp[hp]])
                mk.op("dve", lambda e: e.tensor_copy(SGs[:, hp, :], SG[kk][:, NTOK:NCOLP]), reads=[bSG[kk]], writes=[bsmp[hp]])

            def attn(hp, kk, bg):
                its = [(hh, c, j) for hh in range(2) for c in range(4) for j in range(4 * c + 4)]
                LA = 2
                slots = {}
                obs = {}
                for n in range(len(its) + LA):
                    if n < len(its):
                        hh, c, j = its[n]
                        hb0 = hh * 64
                        stt, bst = ST_RING[ctr["st"] % 3]
                        ctr["st"] += 1
                        slots[n] = (stt, bst)
                        mk.op("pe", lambda e: e.matmul(stt[:, :], KT[kk][:, hh, j * 128:(j + 1) * 128],
                                                       QT[kk][:, c * 512:(c + 1) * 512], start=True, stop=True),
                              reads=[bKT[kk], bQT[kk]], writes=[bst])
                    m = n - LA
                    if m < 0:
                        continue
                    hh, c, j = its[m]
                    nj = 4 * c + 4
                    vcol = 0 if hh == 0 else 64
                    num0, z0 = (0, 64) if hh == 0 else (64, 0)
                    if j == 0:
                        obs[(hh, c)] = 5 + (ctr["ot"] % 2)
                        ctr["ot"] += 1
                    ob = obs[(hh, c)]
                    stt, bst = slots.pop(m)
                    pk = ctr["pt"] % 4
                    ctr["pt"] += 1
                    mk.op("act", lambda e: e.activation(pt[pk][:, :], stt[:, :], AF.Exp, scale=0.125), reads=[bst], writes=[bpt[pk]])
                    mo = (4 * c - j + 3) * 128
                    mk.op("dve", lambda e: e.tensor_tensor(pt[pk][:, :], pt[pk][:, :], maskall[:, mo:mo + 512], ALU.mult),
                          reads=[bpt[pk], brope], writes=[bpt[pk]])
                    mk.op("pe", lambda e: e.matmul(ps[ob][:, :], VA[kk][:, j, vcol:vcol + 128], pt[pk][:, :],
                                                   start=(j == 0), stop=(j == nj - 1)),
                          reads=[bVA[kk], bpt[pk]], writes=[bps[ob]])
                    if j == nj - 1:
                        mk.op("act", lambda e: e.activation(rz[num0:num0 + 64, :], ps[ob][z0:z0 + 64, :], AF.Ln), reads=[bps[ob]], writes=[brz])
                        mk.op("act", lambda e: e.activation(rz[num0:num0 + 64, :], rz[num0:num0 + 64, :], AF.Exp, scale=-1.0), reads=[brz], writes=[brz])
                        mk.op("dve", lambda e: e.tensor_tensor(ytmp[num0:num0 + 64, :], ps[ob][num0:num0 + 64, :], rz[num0:num0 + 64, :], ALU.mult),
                              reads=[bps[ob], brz], writes=[bytmp])
                        mk.op("dve", lambda e: e.tensor_tensor(ybT[num0:num0 + 64, hp, c * 512:(c + 1) * 512], ytmp[num0:num0 + 64, :],
                                                                SG[kk][num0:num0 + 64, c * 512:(c + 1) * 512], ALU.mult),
                              reads=[bytmp, bSG[kk]], writes=[bybT[hp][c]])
                    if bg is not None and m % 4 == 3:
                        next(bg, None)

            NP = 8 if 'hp1' not in LIM else 1
            for _ in proj_gen(0, 0):
                pass
            bkvsel = [Buf("kvselb%d" % i) for i in range(2)]

            def precast_sel(i_, src, gate):
                for s_ in range(4):
                    v = src[s_]
                    for r in range(4):
                        sv = v[0:1536, :].rearrange("(t i r) f -> r i t f", r=16, i=32)[r]
                        mk.dma("pool", kvselb[i_][s_, r * 32:(r + 1) * 32, 0:3, :], sv, wbuf=bkvsel[i_], after=gate)
                    mk.dma("pool", kvselb[i_][s_, :, 3:7, :], v[1536:2048, :].rearrange("(j p) f -> p j f", p=128), wbuf=bkvsel[i_], after=gate)

            for hp in range(NP):
                bg = proj_gen(hp + 1, (hp + 1) % 2) if hp + 1 < NP else None
                if hp == 1:
                    precast_sel(0, I["ck"], [bybT[0][3]])
                if hp == 2:
                    precast_sel(1, I["cv"], [bybT[1][3]])
                if hp == 3:
                    precast_wout(0, [bybT[2][3]])
                if hp == 4:
                    precast_wout(1, [bybT[3][3]])
                attn(hp, hp % 2, bg)
                if bg is not None:
                    for _ in bg:
                        pass
            mk.barrier()
        with contextlib.ExitStack() as sS:
            smask = T("smask", [128, 4, 8, 4], BF16, sS)
            bsmask = Buf("smask")
            mk.dma("sp", smask[:], C["smask"], wbuf=bsmask)
            Ksel = [T("Ksel%d" % i, [128, 7, 1024], BF16, sS) for i in range(2)]
            Vsel = [T("Vsel%d" % i, [128, 7, 1024], BF16, sS) for i in range(2)]
            bKsel = [Buf("Ksel%d" % i) for i in range(2)]
            bVsel = [Buf("Vsel%d" % i) for i in range(2)]
            KselT = [T("KselT%d" % i, [128, 896], BF16, sS) for i in range(2)]
            bKselT = [Buf("KselT%d" % i) for i in range(2)]
            VselA = [T("VselA%d" % i, [128, 7, 192], BF16, sS) for i in range(2)]
            bVselA = [Buf("VselA%d" % i) for i in range(2)]
            for i in range(2):
                mk.op("dve", lambda e, i=i: e.memset(VselA[i][:, :, 64:128], 1.0), writes=[bVselA[i]])
            pts = [T("pts%d" % i, [128, 32], BF16, sS) for i in range(2)]
            bpts = [Buf("pts%d" % i) for i in range(2)]
            rzs = T("rzs", [128, 4], F32, sS)
            brzs = Buf("rzs")
            yts = T("yts", [128, 4], F32, sS)
            byts = Buf("yts")
            cs = {"kt": 0, "pt": 0, "sb": 0}

            def load_sel(dst, bdst, src, s_):
                i_ = 0 if src is I["ck"] else 1
                mk.dma("sp", dst[:, :, :], kvselb[i_][s_], wbuf=bdst, after=[bkvsel[i_]])

            load_sel(Ksel[0], bKsel[0], I["ck"], 0)
            load_sel(Vsel[0], bVsel[0], I["cv"], 0)
            for s_ in range(4):
                kb = s_ % 2
                if s_ + 1 < 4:
                    load_sel(Ksel[1 - kb], bKsel[1 - kb], I["ck"], s_ + 1)
                    load_sel(Vsel[1 - kb], bVsel[1 - kb], I["cv"], s_ + 1)
                for hp in range(8):
                    kt = cs["kt"] % 2
                    cs["kt"] += 1
                    for it in range(7):
                        mk.op("pe", lambda e, it=it: e.transpose(psb[:, it * 128:(it + 1) * 128], Ksel[kb][:, it, hp * 128:(hp + 1) * 128],
                                                                 cst["identb"][:, :]),
                              reads=[bKsel[kb], bcst], writes=[bpsb], mark=(it == 6), n=128)
                    mk.op("act", lambda e: e.copy(KselT[kt][:, :], psb[:, 0:896]), reads=[bpsb], writes=[bKselT[kt]])
                    mk.op("act", lambda e: e.copy(VselA[kt][:, :, 0:64], Vsel[kb][:, :, hp * 128:hp * 128 + 64]),
                          reads=[bVsel[kb]], writes=[bVselA[kt]], n=448)
                    mk.op("act", lambda e: e.copy(VselA[kt][:, :, 128:192], Vsel[kb][:, :, hp * 128 + 64:hp * 128 + 128]),
                          reads=[bVsel[kb]], writes=[bVselA[kt]], n=448)
                    for hh in range(2):
                        hb0 = hh * 64
                        vcol = 0 if hh == 0 else 64
                        num0, z0 = (0, 64) if hh == 0 else (64, 0)
                        sb = 3 + (cs["sb"] % 2)
                        ob = 5 + (cs["sb"] % 2)
                        cs["sb"] += 1
                        pk = cs["pt"] % 2
                        cs["pt"] += 1
                        qv = QTs[hb0:hb0 + 64, hp, s_ * 4:s_ * 4 + 4]
                        for it in range(7):
                            mk.op("pe", lambda e, it=it: e.matmul(ps[sb][:, it * 4:it * 4 + 4], KselT[kt][hb0:hb0 + 64, it * 128:(it + 1) * 128], qv,
                                                                  start=True, stop=True),
                                  reads=[bKselT[kt], bsmp[hp]], writes=[bps[sb]], mark=False, n=16)
                        mk.op("pe", lambda e: e.matmul(ps[sb][0:32, 28:32], KTs[hb0:hb0 + 64, hp, :], qv, start=True, stop=True),
                              reads=[bsmp[hp]], writes=[bps[sb]])
                        mk.op("act", lambda e: e.activation(pts[pk][:, 0:28], ps[sb][:, 0:28], AF.Exp, scale=0.125), reads=[bps[sb]], writes=[bpts[pk]], n=28)
                        mk.op("act", lambda e: e.activation(pts[pk][0:32, 28:32], ps[sb][0:32, 28:32], AF.Exp, scale=0.125),
                              reads=[bps[sb]], writes=[bpts[pk]], n=4)
                        mk.op("dve", lambda e: e.tensor_tensor(pts[pk][:, 0:28], pts[pk][:, 0:28],
                                                               smask[:, s_, 0:7, :], ALU.mult),
                              reads=[bpts[pk], bsmask], writes=[bpts[pk]], n=28)
                        mk.op("dve", lambda e: e.tensor_tensor(pts[pk][0:32, 28:32], pts[pk][0:32, 28:32], smask[0:32, s_, 7, :], ALU.mult),
                              reads=[bpts[pk], bsmask], writes=[bpts[pk]], n=4)
                        for it in range(7):
                            mk.op("pe", lambda e, it=it: e.matmul(ps[ob][:, 0:4], VselA[kt][:, it, vcol:vcol + 128], pts[pk][:, it * 4:it * 4 + 4],
                                                                  start=(it == 0), stop=False),
                                  reads=[bVselA[kt], bpts[pk]], writes=[bps[ob]], mark=False, n=16)
                        mk.op("pe", lambda e: e.matmul(ps[ob][:, 0:4], VAs[:, hp, vcol:vcol + 128], pts[pk][0:32, 28:32], start=False, stop=True),
                              reads=[bsmp[hp], bpts[pk]], writes=[bps[ob]])
                        mk.op("dve", lambda e: e.reciprocal(rzs[num0:num0 + 64, :], ps[ob][z0:z0 + 64, 0:4]), reads=[bps[ob]], writes=[brzs], n=16)
                        mk.op("dve", lambda e: e.tensor_tensor(yts[num0:num0 + 64, :], ps[ob][num0:num0 + 64, 0:4], rzs[num0:num0 + 64, :], ALU.mult),
                              reads=[bps[ob], brzs], writes=[byts], n=4)
                        mk.op("dve", lambda e: e.tensor_tensor(ybT[num0:num0 + 64, hp, NTOK + s_ * 4:NTOK + s_ * 4 + 4], yts[num0:num0 + 64, :],
                                                               SGs[num0:num0 + 64, hp, s_ * 4:s_ * 4 + 4], ALU.mult),
                              reads=[byts, bsmp[hp]], writes=[bybT[hp][4]], n=4)
            for hp in range(8):
                mk.op("dve", lambda e, hp=hp: e.memset(ybT[:, hp, NTOK + NS:NCOLP], 0.0), writes=[bybT[hp][4]])
            mk.barrier()
        sAtt.close()
        if dbg:
            bdbg = Buf("dbg")
            mk.barrier()
            mk.dma("sp", DBG["ybT"], ybT[:, :, 0:NCOL], rbuf=bdbg, is_output=True)
            mk.dma("sp", DBG["hT"], hT[:, :, 0:NCOL], rbuf=bdbg, is_output=True)

        sCD = st.enter_context(contextlib.ExitStack())
        yaT = T("yaT", [128, 8, NCOLP], BF16, sCD)
        byaT = [[Buf("yaT%d_%d" % (c, tc)) for tc in range(5)] for c in range(8)]
        with contextlib.ExitStack() as sC:
            yconv2 = [T("yconv%d" % i, [128, 8, 512], F32, sC) for i in range(2)]
            byc2 = [[Buf("yconv%d_%d" % (i, c)) for c in range(8)] for i in range(2)]
            meant2 = [T("meant0", [128, 512], F32, sC)] * 2
            bmean2 = [Buf("meant0")] * 2
            rstdt2 = [T("rstdt0", [128, 512], F32, sC)] * 2
            brstd2 = [Buf("rstdt0")] * 2
            uext = [T("uext%d" % i, [128, 544], F32, sC) for i in range(2)]
            buext = [Buf("uext%d" % i) for i in range(2)]
            tails = T("tails", [128, 8, 30], F32, sC)
            btails = [Buf("tails%d" % c) for c in range(8)]
            ybf = [T("ybf0", [128, 512], BF16, sC)] * 2
            bybf = [Buf("ybf0")] * 2
            ysq = [T("ysq0", [128, 512], BF16, sC)] * 2
            bysq = [Buf("ysq0")] * 2
            lnt = [T("lnt%d" % i, [128, 512], F32, sC) for i in range(2)]
            blnt = [Buf("lnt%d" % i) for i in range(2)]
            saog = [T("saog%d" % i, [128, 512], BF16, sC) for i in range(2)]
            bsaog = [Buf("saog%d" % i) for i in range(2)]
            uexts = T("uexts", [128, 8, 4, 34], F32, sC)
            buexts = [Buf("uexts%d" % c) for c in range(8)]
            cct = T("cct", [128, 1024], F32, sC)
            bcct = Buf("cct")
            ostg = [T("ostg%d" % i, [32, 128], F32, sC) for i in range(2)]
            bostg = [Buf("ostg%d" % i) for i in range(2)]
            cn = {"u": 0, "sg": 0, "yb": 0, "ln": 0, "sa": 0, "os": 0, "dg": 0}
            dgs = [T("dgs%d" % i, [128, 31, 128], BF16, sC) for i in range(2)]
            bdgs = [Buf("dgs%d" % i) for i in range(2)]
            ubf = [T("ubf%d" % i, [128, 544], BF16, sC) for i in range(2)]
            bubf = [Buf("ubf%d" % i) for i in range(2)]
            for c in range(8):
                mk.op("dve", lambda e, c=c: e.memset(tails[:, c, :], 0.0), writes=[btails[c]])
            mk.dma("sp", cct[0:120, :], I["cconv"], wbuf=bcct)
            for c in range(8):
                mk.op("pe", lambda e, c=c: e.transpose(ps[0][:, 0:120], cct[0:120, c * 128:(c + 1) * 128], cst["identf"][0:120, 0:120]),
                      reads=[bcct, bcst], writes=[bps[0]], n=128)
                mk.op("act", lambda e, c=c: e.copy(uexts[:, c, :, 0:30], ps[0][:, 0:120].rearrange("p (s t) -> p s t", t=30)),
                      reads=[bps[0]], writes=[buexts[c]])
            sgt = [cct[:, i * 512:(i + 1) * 512] for i in range(2)]
            bsgt = [Buf("sgt%d" % i) for i in range(2)]
            for b_ in bsgt:
                b_.lwn = bcct.lwn
                b_.rdn = list(bcct.rdn)
            bd2d = Buf("d2d")
            for s_ in range(4):
                mk.dma_d2d("sp", O["ncs"][s_, 0:26, :], I["cconv"][s_ * 30 + 4:s_ * 30 + 30, :], bd2d)

            def out_rows(src_ap, bsrc, ncols, dst_fn):
                k = cn["os"] % 2
                cn["os"] += 1
                mk.op("pe", lambda e: e.transpose(ps[1][0:ncols, 0:128], src_ap, cst["identf"][:, :]),
                      reads=[bsrc, bcst], writes=[bps[1]], n=128)
                mk.op("act", lambda e: e.copy(ostg[k][0:ncols, :], ps[1][0:ncols, 0:128]), reads=[bps[1]], writes=[bostg[k]])
                dst_fn(ostg[k], bostg[k])

            bwsA = [Buf("wA%d" % i) for i in range(NW)]
            bwsB = [Buf("wB%d" % i) for i in range(NW)]
            wsub = {"A": 0, "B": 0}

            def wload_sub(which, parts):
                k_ = wsub[which] % NW
                wsub[which] += 1
                base = 0 if which == "A" else 256
                bufs = bwsA if which == "A" else bwsB
                off = base
                for ap in parts:
                    wd_ = ap.shape[1]
                    mk.dma("pool", wsl[k_][:, :, off:off + wd_], ap.rearrange("(c p) f -> p c f", p=128), wbuf=bufs[k_])
                    off += wd_
                return wsl[k_][:, :, base:base + 256], bufs[k_]

            for tc in (4, 0, 1, 2, 3):
                c0, wd = TCH[tc]
                yconv, byc = yconv2[tc % 2], byc2[tc % 2]
                meant, bmean, rstdt, brstd = meant2[tc % 2], bmean2[tc % 2], rstdt2[tc % 2], brstd2[tc % 2]
                for c in range(8):
                    w, bw = wload_sub("A", [I["w_in_ab"][:, c * 128:(c + 1) * 128], I["w_in_ab"][:, 1024 + c * 128:1024 + (c + 1) * 128]])
                    proj_fm(w, bw, 0, tc, 0)
                    proj_fm(w, bw, 128, tc, 1)
                    k = cn["sg"] % 2
                    cn["sg"] += 1
                    sigmoid_to(sgt[k][:, 0:wd], bsgt[k], ps[1][:, 0:wd], bps[1])
                    if tc < 4:
                        u = cn["u"] % 2
                        cn["u"] += 1
                        mk.op("dve", lambda e: e.tensor_tensor(uext[u][:, 30:542], ps[0][:, :], sgt[k][:, :], ALU.mult),
                              reads=[bps[0], bsgt[k]], writes=[buext[u]])
                        mk.op("dve", lambda e, c=c: e.tensor_copy(uext[u][:, 0:30], tails[:, c, :]), reads=[btails[c]], writes=[buext[u]])
                        mk.op("dve", lambda e, c=c: e.tensor_copy(tails[:, c, :], uext[u][:, 512:542]), reads=[buext[u]], writes=[btails[c]])
                        dk = cn["dg"] % 2
                        cn["dg"] += 1
                        mk.op("dve", lambda e, c=c: e.tensor_tensor(
                            dgs[dk][:, :, :], cst["identb"][:, :].rearrange("p (o q) -> p o q", o=1).to_broadcast([128, 31, 128]),
                            prm["convw"][:, c, :].rearrange("p (j o) -> p j o", o=1).to_broadcast([128, 31, 128]), ALU.mult),
                            reads=[bcst], writes=[bdgs[dk]], n=3000)
                        mk.op("act", lambda e: e.copy(ubf[u][:, 0:542], uext[u][:, 0:542]), reads=[buext[u]], writes=[bubf[u]])
                        cb = 4 + dk
                        NPE = 31
                        for j in range(NPE):
                            mk.op("pe", lambda e, j=j: e.matmul(ps[cb][:, :], dgs[dk][:, j, :], ubf[u][:, j:j + 512], start=(j == 0), stop=(j == NPE - 1)),
                                  reads=[bdgs[dk], bubf[u]], writes=[bps[cb]], mark=(j == NPE - 1))
                        mk.op("dve", lambda e, c=c: e.tensor_scalar(yconv[:, c, :], ps[cb][:, :], prm["convb"][:, c:c + 1], None, ALU.add),
                              reads=[bps[cb], bcst], writes=[byc[c]])
                        for j in range(NPE, 31):
                            mk.op("dve", lambda e, c=c, j=j: e.scalar_tensor_tensor(yconv[:, c, :], uext[u][:, j:j + 512],
                                                                                    prm["convw"][:, c, j:j + 1], yconv[:, c, :], ALU.mult, ALU.add),
                                  reads=[buext[u], bcst, byc[c]], writes=[byc[c]])
                        if tc == 3:
                            out_rows(uext[u][:, 510:542], buext[u], 32,
                                     lambda stg, bstg, c=c: mk.dma("sp", O["ncp"][:, c * 128:(c + 1) * 128], stg[2:32, :], rbuf=bstg, is_output=True))
                    else:
                        uv = uexts[:, c, :, 30:34]
                        mk.op("dve", lambda e: e.tensor_tensor(uv, ps[0][:, 0:NS].rearrange("p (s t) -> p s t", t=4),
                                                               sgt[k][:, 0:NS].rearrange("p (s t) -> p s t", t=4), ALU.mult),
                              reads=[bps[0], bsgt[k]], writes=[buexts[c]])
                        mk.op("dve", lambda e, c=c: e.memset(yconv[:, c, 0:32], 0.0), writes=[byc[c]])
                        yv = yconv[:, c, 0:NS].rearrange("p (s t) -> p s t", t=4)
                        mk.op("dve", lambda e, c=c: e.tensor_scalar(yv, uexts[:, c, :, 0:4], prm["convw"][:, c, 0:1],
                                                                  prm["convb"][:, c:c + 1], ALU.mult, ALU.add),
                              reads=[buexts[c], bcst], writes=[byc[c]])
                        for j in range(1, 31):
                            mk.op("dve", lambda e, c=c, j=j: e.scalar_tensor_tensor(yv, uexts[:, c, :, j:j + 4], prm["convw"][:, c, j:j + 1],
                                                                                    yv, ALU.mult, ALU.add),
                                  reads=[buexts[c], bcst, byc[c]], writes=[byc[c]], n=16)
                        u = cn["u"] % 2
                        cn["u"] += 1
                        mk.op("dve", lambda e: e.memset(uext[u][:, 0:32], 0.0), writes=[buext[u]])
                        mk.op("dve", lambda e: e.tensor_copy(uext[u][:, 0:NS].rearrange("p (s t) -> p s t", t=4), uv),
                              reads=[buexts[c]], writes=[buext[u]])

                        def dst_fn(stg, bstg, c=c):
                            for s_ in range(4):
                                mk.dma("sp", O["ncs"][s_, 26:30, c * 128:(c + 1) * 128], stg[s_ * 4:s_ * 4 + 4, :], rbuf=bstg, is_output=True)
                        out_rows(uext[u][:, 0:32], buext[u], 32, dst_fn)
                for c in range(8):
                    k = cn["yb"] % 2
                    cn["yb"] += 1
                    mk.op("act", lambda e, c=c: e.copy(ybf[k][:, 0:wd], yconv[:, c, 0:wd]), reads=[byc[c]], writes=[bybf[k]])
                    mk.op("act", lambda e, c=c: e.activation(ysq[k][:, 0:wd], yconv[:, c, 0:wd], AF.Square), reads=[byc[c]], writes=[bysq[k]])
                    mk.op("pe", lambda e, c=c: e.matmul(ps[2][:, 0:wd], onesb[:, :], ybf[k][:, 0:wd], start=(c == 0), stop=(c == 7)),
                          reads=[bones, bybf[k]], writes=[bps[2]])
                    mk.op("pe", lambda e, c=c: e.matmul(ps[3][:, 0:wd], onesb[:, :], ysq[k][:, 0:wd], start=(c == 0), stop=(c == 7)),
                          reads=[bones, bysq[k]], writes=[bps[3]])
                mk.op("dve", lambda e: e.tensor_scalar(meant[:, 0:wd], ps[2][:, 0:wd], 1.0 / 1024, None, ALU.mult), reads=[bps[2]], writes=[bmean])
                mk.op("dve", lambda e: e.tensor_tensor(rstdt[:, 0:wd], meant[:, 0:wd], meant[:, 0:wd], ALU.mult), reads=[bmean], writes=[brstd])
                mk.op("dve", lambda e: e.scalar_tensor_tensor(rstdt[:, 0:wd], ps[3][:, 0:wd], 1.0 / 1024, rstdt[:, 0:wd], ALU.mult, ALU.subtract),
                      reads=[bps[3], brstd], writes=[brstd])
                mk.op("dve", lambda e: e.tensor_scalar(rstdt[:, 0:wd], rstdt[:, 0:wd], EPS, None, ALU.add), reads=[brstd], writes=[brstd])
                mk.op("act", lambda e: e.activation(rstdt[:, 0:wd], rstdt[:, 0:wd], AF.Ln), reads=[brstd], writes=[brstd])
                mk.op("act", lambda e: e.activation(rstdt[:, 0:wd], rstdt[:, 0:wd], AF.Exp, scale=-0.5), reads=[brstd], writes=[brstd])
                for c in range(8):
                    w, bw = wload_sub("B", [I["w_in_ab"][:, 2048 + c * 128:2048 + (c + 1) * 128]])
                    proj_fm(w, bw, 0, tc, 6)
                    k = cn["sa"] % 2
                    cn["sa"] += 1
                    k2 = cn["sg"] % 2
                    cn["sg"] += 1
                    silu_to(saog[k][:, 0:wd], bsaog[k], ps[6][:, 0:wd], bps[6], sgt[k2][:, 0:wd], bsgt[k2])
                    l = cn["ln"] % 2
                    cn["ln"] += 1
                    mk.op("dve", lambda e, c=c: e.tensor_tensor(lnt[l][:, 0:wd], yconv[:, c, 0:wd], meant[:, 0:wd], ALU.subtract),
                          reads=[byc[c], bmean], writes=[blnt[l]])
                    mk.op("dve", lambda e: e.tensor_tensor(lnt[l][:, 0:wd], lnt[l][:, 0:wd], rstdt[:, 0:wd], ALU.mult),
                          reads=[blnt[l], brstd], writes=[blnt[l]])
                    mk.op("act", lambda e, c=c: e.activation(lnt[l][:, 0:wd], lnt[l][:, 0:wd], AF.Identity, bias=prm["lnb"][:, c:c + 1],
                                                             scale=prm["lng"][:, c:c + 1]),
                          reads=[blnt[l], bcst], writes=[blnt[l]])
                    k3 = cn["sg"] % 2
                    cn["sg"] += 1
                    sigmoid_to(sgt[k3][:, 0:wd], bsgt[k3], lnt[l][:, 0:wd], blnt[l])
                    mk.op("dve", lambda e: e.tensor_tensor(lnt[l][:, 0:wd], lnt[l][:, 0:wd], sgt[k3][:, 0:wd], ALU.mult),
                          reads=[blnt[l], bsgt[k3]], writes=[blnt[l]])
                    mk.op("dve", lambda e, c=c: e.tensor_tensor(yaT[:, c, c0:c0 + wd], lnt[l][:, 0:wd], saog[k][:, 0:wd], ALU.mult),
                          reads=[blnt[l], bsaog[k]], writes=[byaT[c][tc]])
            mk.barrier()

        def out_proj_stage(sD, zsrc, w_dram, layer, x_in_fn, final):
            wout = T("wout%d" % layer, [128, 16, 1024], BF16, sD)
            bwoutg = [Buf("wout%d" % g8) for g8 in range(8)]
            for g8 in range(8):
                mk.dma("sp", wout[:, g8 * 2:(g8 + 1) * 2, :], woutb[layer][g8 * 256:(g8 + 1) * 256, :].rearrange("(e p) f -> p e f", p=128),
                       wbuf=bwoutg[g8], after=[bwoutb[layer]])
            gpo = T("gpo%d" % layer, [128, 1024], F32, sD)
            bgpo = Buf("gpo")
            mk.dma("sp", gpo[:], I["gpost"][:, layer, :], wbuf=bgpo)
            yn = [T("yn%d_%d" % (layer, i), [128, 1024], F32, sD) for i in range(2)]
            byn = [Buf("yn%d" % i) for i in range(2)]
            x1t = [T("x1t%d_%d" % (layer, i), [128, 1024], F32, sD) for i in range(2)]
            bx1t = [Buf("x1t%d" % i) for i in range(2)]
            sso = [T("sso%d_%d" % (layer, i), [128, 8], F32, sD) for i in range(2)]
            bsso = [Buf("sso%d" % i) for i in range(2)]
            for ti in range(17):
                row0, P = TILES[ti]
                tc = min(ti // 4, 4)
                k = ti % 2
                nreal = P if ti < 16 else NS
                mk.dma("sp", xr[k][:nreal, :], x_in_fn(ti), wbuf=bxr[k])
                for half in range(2):
                    for e_ in range(16):
                        zt, zb = zsrc(e_, tc)
                        mk.op("pe", lambda e, e_=e_, half=half, zt=zt: e.matmul(ps[2 * k + half][:P, :], zt[:, row0:row0 + P],
                                                                                wout[:, e_, half * 512:(half + 1) * 512],
                                                                                start=(e_ == 0), stop=(e_ == 15)),
                              reads=[zb, bwoutg[e_ // 2]], writes=[bps[2 * k + half]], mark=(e_ == 15))
                ss, bs_ = sso[k], bsso[k]
                for half in range(2):
                    mk.op("act", lambda e, half=half: e.activation(junk[:P, 0:512], ps[2 * k + half][:P, :], AF.Square, accum_out=ss[:P, half:half + 1]),
                          reads=[bps[2 * k + half]], writes=[bjunk, bs_])
                mk.op("dve", lambda e: e.tensor_tensor(ss[:P, 2:3], ss[:P, 0:1], ss[:P, 1:2], ALU.add), reads=[bs_], writes=[bs_])
                mk.op("dve", lambda e: e.tensor_scalar(ss[:P, 2:3], ss[:P, 2:3], 1.0 / 1024, EPS, ALU.mult, ALU.add), reads=[bs_], writes=[bs_])
                mk.op("act", lambda e: e.activation(ss[:P, 2:3], ss[:P, 2:3], AF.Ln), reads=[bs_], writes=[bs_])
                mk.op("act", lambda e: e.activation(ss[:P, 3:4], ss[:P, 2:3], AF.Exp, scale=-0.5), reads=[bs_], writes=[bs_])
                for half in range(2):
                    mk.op("dve", lambda e, half=half: e.scalar_tensor_tensor(yn[k][:P, half * 512:(half + 1) * 512], ps[2 * k + half][:P, :], ss[:P, 3:4],
                                                                             gpo[:P, half * 512:(half + 1) * 512], ALU.mult, ALU.mult),
                          reads=[bps[2 * k + half], bs_, bgpo], writes=[byn[k]])
                if ti == 16:
                    mk.op("dve", lambda e: e.memset(x1t[k][0:32, :], 0.0), writes=[bx1t[k]])
                mk.op("dve", lambda e: e.tensor_tensor(x1t[k][:nreal, :], yn[k][:nreal, :], xr[k][:nreal, :], ALU.add),
                      reads=[byn[k], bxr[k]], writes=[bx1t[k]])
                final(ti, x1t[k], bx1t[k], nreal)

        def zsrc0(e_, tc):
            return (yaT[:, e_, :], byaT[e_][tc]) if e_ < 8 else (ybT[:, e_ - 8, :], bybT[e_ - 8][tc])

        def x1_dst(ti):
            row0, P = TILES[ti]
            return x1s[row0:row0 + (P if ti < 16 else NS), :]

        x1bufs0 = []

        def final0(ti, xt, bxt, nreal):
            mk.dma("sp", x1_dst(ti), xt[:nreal, :], rbuf=bxt)
            if bxt not in x1bufs0:
                x1bufs0.append(bxt)
            if dbg:
                row0, P = TILES[ti]
                mk.dma("sp", DBG["x1"][row0:row0 + nreal, :], xt[:nreal, :], rbuf=bxt, is_output=True)
            norm_to_hT(xt, bxt, ti, 1)

        if dbg:
            mk.dma("sp", DBG["yaT"], yaT[:, :, 0:NCOL], rbuf=bdbg, is_output=True)
        with contextlib.ExitStack() as sD:
            out_proj_stage(sD, zsrc0, I["w_out_ab"], 0, x_src, final0)
            mk.barrier()
        sCD.close()

        x1bufs = x1bufs0
        with contextlib.ExitStack() as sL1:
            zhi = T("zhi", [128, 8, NCOLP], BF16, sL1)
            bz = [[Buf("z%d_%d" % (h, tc)) for tc in range(5)] for h in range(16)]

            def zT(h):
                return ybT[:, h, :] if h < 8 else zhi[:, h - 8, :]

            with contextlib.ExitStack() as sH:
                lbt = T("lbt", [128, 16], F32, sH)
                oml = T("oml", [128, 16], F32, sH)
                blb = Buf("lb")
                mk.op("dve", lambda e: e.tensor_tensor(lbt[:], prm["hlb"][:, 1, :], prm["hlb"][:, 0, :], ALU.subtract), reads=[bcst], writes=[blb])
                sigmoid_to(lbt[:], blb, lbt[:], blb)
                mk.op("dve", lambda e: e.tensor_scalar(oml[:], lbt[:], -1.0, 1.0, ALU.mult, ALU.add), reads=[blb], writes=[blb])
                R2 = lambda name, shape, dt: ([T("%s%d" % (name, i), shape, dt, sH) for i in range(2)], [Buf("%s%d" % (name, i)) for i in range(2)])
                RD = 2
                R3 = lambda name, shape, dt: ([T("%s%d" % (name, i), shape, dt, sH) for i in range(RD)], [Buf("%s%d" % (name, i)) for i in range(RD)])
                qs, bqs = R3("qs", [128, 512], F32)
                fg, bfg = R3("fg", [128, 512], F32)
                sgn, bsgn = R3("sgn", [128, 512], F32)
                bc, bbc = R3("bc", [128, 512], F32)
                ebt, bebt = R3("ebt", [128, 512], F32)
                enb, benb = R3("enb", [128, 512], F32)
                kpp = T("kpp", [128, NCOLP], BF16, sH)
                bkpp = [Buf("kpp%d" % tc) for tc in range(5)]
                HSETS = 1
                hsets = []
                for i_ in range(HSETS):
                    d_ = {}
                    for nm_ in ("qp", "kp", "sgate"):
                        d_[nm_] = T("%s_%d" % (nm_, i_), [128, NCOLP], BF16, sH)
                        d_["b" + nm_] = [Buf("%s%d_%d" % (nm_, i_, tc)) for tc in range(5)]
                    d_["ebl"] = T("ebl_%d" % i_, [128, 40], F32, sH)
                    d_["bebl"] = [Buf("ebl%d_%d" % (i_, tc)) for tc in range(5)]
                    d_["Vh"] = T("Vh_%d" % i_, [128, 16, 128], BF16, sH)
                    d_["bVh"] = [Buf("Vh%d_%d" % (i_, g)) for g in range(4)]
                    d_["ktok"] = T("ktok_%d" % i_, [128, 2, 16, 128], BF16, sH)
                    d_["bktok"] = [Buf("ktok%d_%d" % (i_, g)) for g in range(2)]
                    mk.op("dve", lambda e, t_=d_["ktok"]: e.memset(t_[:, :, :, :], 0.0), writes=d_["bktok"], n=4096)
                    hsets.append(d_)
                S0_ = T("S0", [128, 4, 128], F32, sH)
                bS0_ = Buf("S0")
                for d_ in hsets:
                    d_["S0"], d_["bS0"] = S0_, bS0_
                Vhs = T("Vhs", [32, 128], BF16, sH)
                bVhs = Buf("Vhs")
                ktoks = T("ktoks", [32, 128], BF16, sH)
                bktoks = Buf("ktoks")
                ktm = T("ktm", [32, 4, 128], BF16, sH)
                bktm = Buf("ktm")
                atm, batm = R2("atm", [128, 128], BF16)
                Sf, bSf = R2("Sf", [128, 128], F32)
                Sb = [T("Sb%d" % i, [128, 128], BF16, sH) for i in range(4)]
                bSb = [Buf("Sb%d" % i) for i in range(4)]
                osq, bosq = R2("osq", [128, 512], BF16)
                rst, brst = sgn, bsgn
                zt, bzt = bc, bbc
                S0b = T("S0b", [128, 4, 128], BF16, sH)
                bS0b = Buf("S0b")
                Sn = T("Sn", [128, 2, 128], F32, sH)
                bSn = Buf("Sn")
                hc = {"r": 0, "atm": 0, "sb": 0, "o": 0, "z": 0}
                psbF1 = psb[:, :].bitcast(F32)
                st4 = I["st"].rearrange("(s h) d v -> h d s v", h=16)
                nss4 = O["nss"].rearrange("(s h) d v -> h d s v", h=16)

                for h in range(16):
                    hs_ = hsets[h % HSETS]
                    qp, bqp, kp, bkp, sgate, bsgate = (hs_["qp"], hs_["bqp"], hs_["kp"], hs_["bkp"], hs_["sgate"], hs_["bsgate"])
                    ebl, bebl, Vh, bVh, ktok, bktok, S0, bS0 = (hs_["ebl"], hs_["bebl"], hs_["Vh"], hs_["bVh"], hs_["ktok"], hs_["bktok"],
                                                                hs_["S0"], hs_["bS0"])
                    w, bw = wload([I["w_in_c"][:, h * 128:(h + 1) * 128], I["w_in_c"][:, 2048 + h * 128:2048 + (h + 1) * 128],
                                   I["w_in_c"][:, 4096 + h * 128:4096 + (h + 1) * 128], I["w_in_c"][:, 6144 + h * 128:6144 + (h + 1) * 128]])
                    mk.dma("sp", S0[:], st4[h], wbuf=bS0)
                    for tc in range(5):
                        c0, wd = TCH[tc]
                        r = hc["r"] % RD
                        hc["r"] += 1
                        proj_fm(w, bw, 0, tc, 0)
                        silu_to(qs[r][:, 0:wd], bqs[r], ps[0][:, 0:wd], bps[0], enb[r][:, 0:wd], benb[r], xs=(qs[r][:, 0:wd], bqs[r]))
                        proj_fm(w, bw, 128, tc, 1)
                        sigmoid_to(fg[r][:, 0:wd], bfg[r], ps[1][:, 0:wd], bps[1], keep_e=(sgn[r][:, 0:wd], bsgn[r]))
                        mk.op("dve", lambda e: e.tensor_tensor(sgn[r][:, 0:wd], sgn[r][:, 0:wd], fg[r][:, 0:wd], ALU.mult),
                              reads=[bsgn[r], bfg[r]], writes=[bsgn[r]])
                        mk.op("dve", lambda e: e.tensor_scalar(fg[r][:, 0:wd], fg[r][:, 0:wd], oml[:, h:h + 1], lbt[:, h:h + 1], ALU.mult, ALU.add),
                              reads=[bfg[r], blb], writes=[bfg[r]])
                        mk.op("act", lambda e: e.activation(fg[r][:, 0:wd], fg[r][:, 0:wd], AF.Ln), reads=[bfg[r]], writes=[bfg[r]])
                        rmask = cst["resetm"][:, 0:wd] if tc < 4 else cst["resets"][:, 0:wd]
                        mk.op("dve", lambda e: e.tensor_tensor_scan(bc[r][:, 0:wd], rmask, fg[r][:, 0:wd], 0.0, ALU.mult, ALU.add),
                              reads=[bfg[r], bcst], writes=[bbc[r]], n=2 * wd)
                        mk.op("act", lambda e: e.activation(ebt[r][:, 0:wd], bc[r][:, 0:wd], AF.Exp), reads=[bbc[r]], writes=[bebt[r]])
                        mk.op("act", lambda e: e.activation(enb[r][:, 0:wd], bc[r][:, 0:wd], AF.Exp, scale=-1.0), reads=[bbc[r]], writes=[benb[r]])
                        cl = 64 if tc < 4 else 4
                        ncl = wd // cl
                        mk.op("dve", lambda e: e.tensor_copy(ebl[:, tc * 8:tc * 8 + ncl], ebt[r][:, cl - 1:wd:cl]), reads=[bebt[r]], writes=[bebl[tc]])
                        mk.op("dve", lambda e: e.tensor_tensor(qp[:, c0:c0 + wd], qs[r][:, 0:wd], ebt[r][:, 0:wd], ALU.mult),
                              reads=[bqs[r], bebt[r]], writes=[bqp[tc]])
                        mk.op("dve", lambda e: e.scalar_tensor_tensor(kp[:, c0:c0 + wd], sgn[r][:, 0:wd], oml[:, h:h + 1], enb[r][:, 0:wd], ALU.mult, ALU.mult),
                              reads=[bsgn[r], blb, benb[r]], writes=[bkp[tc]])
                        eblb = ebl[:, tc * 8:tc * 8 + ncl].rearrange("p (c o) -> p c o", o=1).to_broadcast([128, ncl, cl])
                        mk.op("dve", lambda e: e.tensor_tensor(kpp[:, c0:c0 + wd].rearrange("p (c t) -> p c t", t=cl),
                                                               kp[:, c0:c0 + wd].rearrange("p (c t) -> p c t", t=cl), eblb, ALU.mult),
                              reads=[bkp[tc], bebl[tc]], writes=[bkpp[tc]])
                        proj_fm(w, bw, 384, tc, (psbF1, bpsb))
                        silu_to(sgate[:, c0:c0 + wd], bsgate[tc], psbF1[:, 0:wd], bpsb, bc[r][:, 0:wd], bbc[r], xs=(qs[r][:, 0:wd], bqs[r]))
                    for g in range(5):
                        bank = g % 2
                        if g < 4:
                            for j in range(4):
                                ti = g * 4 + j
                                for d in range(8):
                                    mk.op("pe", lambda e, d=d, j=j, ti=ti: e.matmul(ps[bank][:, j * 128:(j + 1) * 128], hT[:, d, ti * 128:(ti + 1) * 128],
                                                                                    w[:, d, 256:384], start=(d == 0), stop=(d == 7)),
                                          reads=[bw, bhT[ti]], writes=[bps[bank]], mark=(d == 7 and j == 3), n=128)
                            mk.op("act", lambda e: e.copy(Vh[:, g * 4:g * 4 + 4, :], ps[bank][:, :].rearrange("p (j f) -> p j f", f=128)),
                                  reads=[bps[bank]], writes=[bVh[g]])
                        else:
                            for d in range(8):
                                mk.op("pe", lambda e, d=d: e.matmul(ps[bank][0:32, 0:128], hT[:, d, NTOK:NCOLP], w[:, d, 256:384],
                                                                    start=(d == 0), stop=(d == 7)),
                                      reads=[bw, bhT[16]], writes=[bps[bank]], mark=(d == 7), n=128)
                            mk.op("act", lambda e: e.copy(Vhs[:, :], ps[bank][0:32, 0:128]), reads=[bps[bank]], writes=[bVhs])
                    for g in range(2):
                        for j in range(8):
                            ti = g * 8 + j
                            mk.op("pe", lambda e, j=j, ti=ti: e.transpose(psb[:, j * 128:(j + 1) * 128], kpp[:, ti * 128:(ti + 1) * 128], cst["identb"][:, :]),
                                  reads=[bkpp[ti // 4], bcst], writes=[bpsb], mark=(j == 7), n=128)
                        mk.op("act", lambda e: e.copy(ktok[0:64, 0, g * 8:(g + 1) * 8, :], psb[0:64, :].rearrange("p (j f) -> p j f", f=128)),
                              reads=[bpsb], writes=[bktok[g]])
                        mk.op("act", lambda e: e.copy(ktok[64:128, 1, g * 8:(g + 1) * 8, :], psb[64:128, :].rearrange("p (j f) -> p j f", f=128)),
                              reads=[bpsb], writes=[bktok[g]])
                    mk.op("pe", lambda e: e.transpose(psb[0:32, 0:128], kpp[:, NTOK:NCOLP], cst["identb"][:, :]), reads=[bkpp[4], bcst], writes=[bpsb], n=128)
                    mk.op("act", lambda e: e.copy(ktoks[:, :], psb[0:32, 0:128]), reads=[bpsb], writes=[bktoks])

                    def finish_o(ob, c0, wd, tc):
                        k = hc["z"] % 2
                        hc["z"] += 1
                        mk.op("act", lambda e: e.activation(osq[k][:, 0:wd], ps[ob][:, 0:wd], AF.Square), reads=[bps[ob]], writes=[bosq[k]])
                        mk.op("pe", lambda e: e.matmul(ps[6][:, 0:wd], onesb[:, :], osq[k][:, 0:wd], start=True, stop=True),
                              reads=[bones, bosq[k]], writes=[bps[6]])
                        mk.op("dve", lambda e: e.tensor_scalar(rst[k][:, 0:wd], ps[6][:, 0:wd], 1.0 / 128, EPS, ALU.mult, ALU.add),
                              reads=[bps[6]], writes=[brst[k]])
                        mk.op("act", lambda e: e.activation(rst[k][:, 0:wd], rst[k][:, 0:wd], AF.Ln), reads=[brst[k]], writes=[brst[k]])
                        mk.op("act", lambda e: e.activation(rst[k][:, 0:wd], rst[k][:, 0:wd], AF.Exp, scale=-0.5), reads=[brst[k]], writes=[brst[k]])
                        mk.op("dve", lambda e: e.scalar_tensor_tensor(zt[k][:, 0:wd], ps[ob][:, 0:wd], prm["gnorm"][:, 0:1], rst[k][:, 0:wd], ALU.mult, ALU.mult),
                              reads=[bps[ob], bcst, brst[k]], writes=[bzt[k]])
                        mk.op("dve", lambda e: e.tensor_tensor(zT(h)[:, c0:c0 + wd], zt[k][:, 0:wd], sgate[:, c0:c0 + wd], ALU.mult),
                              reads=[bzt[k], bsgate[tc]], writes=[bz[h][tc]])

                    scur = None
                    sbcur = None
                    for Tt in range(16):
                        tc = Tt // 4
                        a_ = hc["atm"] % 2
                        hc["atm"] += 1
                        cols = slice(Tt * 128, (Tt + 1) * 128)
                        mk.op("pe", lambda e: e.matmul(ps[6][:, 0:128], kp[:, cols], qp[:, cols], start=True, stop=True),
                              reads=[bkp[tc], bqp[tc]], writes=[bps[6]], n=128)
                        mk.op("dve", lambda e: e.tensor_tensor(atm[a_][:, :], ps[6][:, 0:128], cst["hmask"][:, :], ALU.mult),
                              reads=[bps[6], bcst], writes=[batm[a_]], n=128)
                        if Tt % 4 == 0:
                            ob = 4 + (hc["o"] % 2)
                            hc["o"] += 1
                        oc = (Tt % 4) * 128
                        sb_for = []
                        for cc in range(2):
                            c = Tt * 2 + cc
                            sb_for.append(sbcur)
                            ub = 2 + cc
                            mk.op("pe", lambda e, cc=cc: e.matmul(ps[ub][:, 0:128], ktok[:, cc, Tt, :], Vh[:, Tt, :],
                                                                  start=True, stop=True),
                                  reads=[bktok[Tt // 8], bVh[Tt // 4]], writes=[bps[ub]], n=128)
                            nxt = 0 if scur is None else 1 - scur
                            if scur is None:
                                mk.op("dve", lambda e: e.tensor_copy(Sf[nxt][:, :], ps[ub][:, 0:128]), reads=[bps[ub]], writes=[bSf[nxt]])
                            else:
                                mk.op("dve", lambda e, c=c, scur=scur: e.scalar_tensor_tensor(Sf[nxt][:, :], Sf[scur][:, :], ebl[:, c:c + 1], ps[ub][:, 0:128],
                                                                                              ALU.mult, ALU.add),
                                      reads=[bSf[scur], bebl[tc], bps[ub]], writes=[bSf[nxt]], n=128)
                            scur = nxt
                            sbcur = hc["sb"] % 4
                            hc["sb"] += 1
                            mk.op("act", lambda e, sbcur=sbcur, scur=scur: e.copy(Sb[sbcur][:, :], Sf[scur][:, :]), reads=[bSf[scur]], writes=[bSb[sbcur]], n=128)
                        n_extra = sum(1 for x in sb_for if x is not None)
                        mk.op("pe", lambda e: e.matmul(ps[ob][:, oc:oc + 128], Vh[:, Tt, :], atm[a_][:, :], start=True, stop=(n_extra == 0)),
                              reads=[bVh[Tt // 4], batm[a_]], writes=[bps[ob]], mark=(n_extra == 0), n=128)
                        done = 0
                        for cc in range(2):
                            if sb_for[cc] is None:
                                continue
                            done += 1
                            mk.op("pe", lambda e, cc=cc: e.matmul(ps[ob][:, oc + cc * 64:oc + (cc + 1) * 64], Sb[sb_for[cc]][:, :],
                                                                  qp[:, Tt * 128 + cc * 64:Tt * 128 + (cc + 1) * 64], start=False, stop=(done == n_extra)),
                                  reads=[bSb[sb_for[cc]], bqp[tc]], writes=[bps[ob]], mark=(done == n_extra), n=64)
                        if Tt % 4 == 3:
                            finish_o(ob, tc * 512, 512, tc)
                    mk.dma("sp", O["nsp"][h], Sf[scur][:, :], rbuf=bSf[scur], is_output=True)
                    mk.op("act", lambda e: e.copy(S0b[:], S0[:]), reads=[bS0], writes=[bS0b])
                    mk.op("pe", lambda e: e.matmul(ps[6][0:32, 0:32], kp[:, NTOK:NCOLP], qp[:, NTOK:NCOLP], start=True, stop=True),
                          reads=[bkp[4], bqp[4]], writes=[bps[6]])
                    a_ = hc["atm"] % 2
                    hc["atm"] += 1
                    mk.op("dve", lambda e: e.tensor_tensor(atm[a_][0:32, 0:32], ps[6][0:32, 0:32], cst["smaskh"][:, :], ALU.mult),
                          reads=[bps[6], bcst], writes=[batm[a_]])
                    ob = 4 + (hc["o"] % 2)
                    hc["o"] += 1
                    mk.op("pe", lambda e: e.matmul(ps[ob][:, 0:32], Vhs[:, :], atm[a_][0:32, 0:32], start=True, stop=False),
                          reads=[bVhs, batm[a_]], writes=[bps[ob]], mark=False)
                    for s_ in range(4):
                        mk.op("pe", lambda e, s_=s_: e.matmul(ps[ob][:, s_ * 4:s_ * 4 + 4], S0b[:, s_, :], qp[:, NTOK + s_ * 4:NTOK + s_ * 4 + 4],
                                                              start=False, stop=(s_ == 3)),
                              reads=[bS0b, bqp[4]], writes=[bps[ob]], mark=(s_ == 3))
                    finish_o(ob, NTOK, 32, 4)
                    for s_ in range(4):
                        mk.op("dve", lambda e, s_=s_: e.tensor_scalar(ktm[:, s_, :], ktoks[:, :], cst["sels"][:, s_:s_ + 1], None, ALU.mult),
                              reads=[bktoks, bcst], writes=[bktm])
                    for s_ in range(4):
                        ub = 2 + s_ % 2
                        mk.op("pe", lambda e, s_=s_: e.matmul(ps[ub][:, 0:128], ktm[:, s_, :], Vhs[:, :], start=True, stop=True),
                              reads=[bktm, bVhs], writes=[bps[ub]])
                        mk.op("dve", lambda e, s_=s_: e.scalar_tensor_tensor(Sn[:, s_ % 2, :], S0[:, s_, :], ebl[:, 32 + s_:33 + s_], ps[ub][:, 0:128],
                                                                             ALU.mult, ALU.add),
                              reads=[bS0, bebl[4], bps[ub]], writes=[bSn])
                        if s_ % 2 == 1:
                            mk.dma("sp", nss4[h][:, s_ - 1:s_ + 1, :], Sn[:], rbuf=bSn, is_output=True)
                mk.barrier()

            def zsrc1(e_, tc):
                return (zT(e_), bz[e_][tc])

            def final1(ti, xt, bxt, nreal):
                row0, P = TILES[ti]
                if ti < 16:
                    mk.dma("sp", O["yp"][row0:row0 + P, :], xt[:P, :], rbuf=bxt, is_output=True)
                else:
                    mk.dma("sp", O["ys"][:, :], xt[:NS, :], rbuf=bxt, is_output=True)

            mk.barrier()
            with contextlib.ExitStack() as sD1:
                out_proj_stage(sD1, zsrc1, I["w_out_c"], 1, x1_dst, final1)
                mk.barrier()
            if dbg:
                mk.dma("sp", DBG["zT"][:, 0:8, :], ybT[:, :, 0:NCOL], rbuf=bdbg, is_output=True)
                mk.dma("sp", DBG["zT"][:, 8:16, :], zhi[:, :, 0:NCOL], rbuf=bdbg, is_output=True)
                mk.barrier()

        mk.finish()
        print("instr counts", mk.ninstr, "dma sems", mk.nsem)
    return nc


def make_in_maps(inputs, cores=range(8)):
    f = lambda a: np.ascontiguousarray(np.asarray(a, dtype=np.float32))
    consts = host_consts()
    pre, post = f(inputs["pre_norm"]), f(inputs["post_norm"])
    shared = {
        "w_in_ab": f(inputs["w_in_ab"][0]), "w_out_ab": f(inputs["w_out_ab"][0]),
        "w_in_c": f(inputs["w_in_c"][0]), "w_out_c": f(inputs["w_out_c"][0]),
        "gpre": f(pre.reshape(2, 8, 128).transpose(2, 0, 1)),
        "gpost": f(np.broadcast_to(post[None, :, :], (128, 2, 1024))),
        "convw": f(np.asarray(inputs["conv_w"][0]).reshape(31, 8, 128).transpose(2, 1, 0)),
        "convb": f(np.asarray(inputs["conv_b"][0]).reshape(8, 128).T),
        "lng": f(np.asarray(inputs["conv_ln_g"][0]).reshape(8, 128).T),
        "lnb": f(np.asarray(inputs["conv_ln_b"][0]).reshape(8, 128).T),
        "gnorm": f(np.asarray(inputs["hgrn_gnorm"][0]).reshape(128, 1)),
        "hlb": f(np.asarray(inputs["hgrn_lb"]).reshape(2, 16, 128).transpose(2, 0, 1)),
    }
    shared.update(consts)
    maps = []
    for c in cores:
        m = dict(shared)
        m["xp"] = f(inputs["x_prompt"][c])
        m["xs"] = f(np.asarray(inputs["x_sample"][4 * c:4 * c + 4]).reshape(NS, 1024))
        m["cconv"] = f(np.asarray(inputs["cache_conv"][0, 4 * c:4 * c + 4]).reshape(120, 1024))
        m["ck"] = f(np.asarray(inputs["cache_swa_k"][0, 4 * c:4 * c + 4]).reshape(4, 2048, 1024))
        m["cv"] = f(np.asarray(inputs["cache_swa_v"][0, 4 * c:4 * c + 4]).reshape(4, 2048, 1024))
        m["st"] = f(np.asarray(inputs["state_hgrn"][0, 4 * c:4 * c + 4]).reshape(64, 128, 128))
        maps.append(m)
    return maps


def assemble(results):
    n = len(results)
    g = lambda k: [np.asarray(r[k], dtype=np.float32) for r in results]
    yp = np.stack(g("yp"), 0)
    ys = np.concatenate([a.reshape(4, 4, 1024) for a in g("ys")], 0)
    ncp = np.stack(g("ncp"), 0)[None]
    ncs = np.concatenate(g("ncs"), 0)[None]
    nkp = np.stack([a.reshape(2048, 16, 64) for a in g("nkp")], 0)[None]
    nvp = np.stack([a.reshape(2048, 16, 64) for a in g("nvp")], 0)[None]
    nks = np.concatenate([a.reshape(4, 4, 16, 64) for a in g("nks")], 0)[None]
    nvs = np.concatenate([a.reshape(4, 4, 16, 64) for a in g("nvs")], 0)[None]
    nsp = np.stack(g("nsp"), 0)[None]
    nss = np.concatenate([a.reshape(4, 16, 128, 128) for a in g("nss")], 0)[None]
    return (yp, ys, ncp, ncs, nkp, nvp, nks, nvs, nsp, nss)


def kernel(**inputs):
    nc = build()
    maps = make_in_maps(inputs)
    res = run_bass_kernel_spmd(nc, maps, core_ids=list(range(8)))
    return assemble(res.results)
```

```python
import contextlib
import numpy as np
import ml_dtypes
import concourse.bass as bass
import concourse.mybir as mybir
from concourse.bass_utils import run_bass_kernel_spmd

F32 = mybir.dt.float32
BF16 = mybir.dt.bfloat16
AF = mybir.ActivationFunctionType
ALU = mybir.AluOpType
AX = mybir.AxisListType

ENGS = ("pe", "act", "dve", "pool", "sp")
NTOK = 2048
NS = 16
NCOL = NTOK + NS
NCOLP = NTOK + 32
EPS = 1e-6
TCH = [(0, 512), (512, 512), (1024, 512), (1536, 512), (2048, 32)]
PAST = 16384


import heapq
import os

SCHED = os.environ.get("MK_SCHED", "1") == "1"


class Buf:
    __slots__ = ("name", "excl", "lwn", "rdn", "wsem", "rsem", "wtot", "rtot")

    def __init__(self, name, excl=False):
        self.name = name
        self.excl = excl
        self.lwn = None
        self.rdn = []
        self.wsem = None
        self.rsem = None
        self.wtot = 0
        self.rtot = 0


class Node:
    __slots__ = ("idx", "e", "fns", "deps", "cost", "kind", "buf", "ticket", "semval", "start", "fin", "nd", "users", "is_output", "grp")

    def __init__(self, idx, e, fns, cost, kind="op", buf=None):
        self.idx = idx
        self.e = e
        self.fns = fns
        self.deps = set()
        self.cost = cost
        self.kind = kind
        self.buf = buf
        self.ticket = None
        self.semval = None
        self.start = None
        self.fin = 0.0
        self.nd = 0
        self.users = []
        self.is_output = False
        self.grp = None


import types


def freeze(fn):
    if fn.__closure__ is None:
        return fn
    cells = []
    for c in fn.__closure__:
        try:
            cells.append(types.CellType(c.cell_contents))
        except ValueError:
            cells.append(c)
    return types.FunctionType(fn.__code__, fn.__globals__, fn.__name__, fn.__defaults__, tuple(cells))


PER_ELEM = {"pe": 0.00046, "act": 0.00072, "dve": 0.0009, "pool": 0.0021, "sp": 0.0}
FIXED = {"pe": 0.04, "act": 0.14, "dve": 0.09, "pool": 0.3, "sp": 0.1}


class MK:
    def __init__(self, nc, stack):
        self.nc = nc
        self.stack = stack
        self.eng = {"pe": nc.tensor, "act": nc.scalar, "dve": nc.vector, "pool": nc.gpsimd, "sp": nc.sync}
        self.sem = {e: stack.enter_context(nc.semaphore("s_" + e)) for e in ENGS}
        self.cnt = {e: 0 for e in ENGS}
        self.seen = {e: {} for e in ENGS}
        self.nsem = 0
        self.ninstr = {e: 0 for e in ENGS}
        self.nodes = []
        self.nidx = 0
        self.pend = {e: [] for e in ENGS}
        self.dmabufs = []
        self.outbufs = []

    def _add_writer_dep(self, node, w):
        if w.grp is not None:
            node.deps.update(w.grp)
        else:
            node.deps.add(w)

    def _deps(self, node, reads, writes):
        for b in reads:
            if b.lwn is not None:
                self._add_writer_dep(node, b.lwn)
            if b.excl:
                for r in b.rdn:
                    if r.e != node.e:
                        node.deps.add(r)
        for b in writes:
            if b.lwn is not None:
                self._add_writer_dep(node, b.lwn)
            for r in b.rdn:
                node.deps.add(r)
        for b in reads:
            b.rdn.append(node)
        for b in writes:
            b.lwn = node
            b.rdn = []
        node.deps.discard(node)

    def op(self, e, fn, reads=(), writes=(), mark=True, n=512):
        cost = FIXED[e] + PER_ELEM[e] * n
        fn = freeze(fn)
        if not mark:
            self.pend[e].append((fn, list(reads), list(writes), cost))
            return
        fns = [fn]
        rr, ww = list(reads), list(writes)
        if self.pend[e]:
            pre = self.pend[e]
            self.pend[e] = []
            fns = [p[0] for p in pre] + fns
            for p in pre:
                rr += p[1]
                ww += p[2]
                cost += p[3]
            rr = list(dict.fromkeys(rr))
            ww = list(dict.fromkeys(ww))
        node = Node(self.nidx, e, fns, cost)
        self.nidx += 1
        self._deps(node, rr, ww)
        self.nodes.append(node)

    def newsem(self):
        self.nsem += 1
        return self.stack.enter_context(self.nc.semaphore("d%d" % self.nsem))

    def dma(self, q, out, in_, wbuf=None, rbuf=None, is_output=False, after=(), **kw):
        assert not self.pend[q]
        fn = lambda eng: eng.dma_start(out=out, in_=in_, **kw)
        try:
            dcost = 2.0 + max(in_.nbytes(), out.nbytes()) / 150e3
        except Exception:
            dcost = 2.5
        if wbuf is not None:
            assert rbuf is None
            node = Node(self.nidx, q, [fn], dcost, "dmaw", wbuf)
            prev = wbuf.lwn
            if prev is not None and prev.kind == "dmaw" and prev.buf is wbuf and prev.e == q and not wbuf.rdn:
                node.deps = set(prev.deps)
                if prev.grp is None:
                    prev.grp = [prev]
                prev.grp.append(node)
                node.grp = prev.grp
                wbuf.lwn = node
            else:
                self._deps(node, [], [wbuf])
            if wbuf.wsem is None:
                wbuf.wsem = self.newsem()
                self.dmabufs.append(wbuf)
        else:
            node = Node(self.nidx, q, [fn], dcost, "dmar", rbuf)
            self._deps(node, [rbuf], [])
            if rbuf.rsem is None:
                rbuf.rsem = self.newsem()
                self.dmabufs.append(rbuf)
            node.is_output = is_output
            if is_output and rbuf not in self.outbufs:
                self.outbufs.append(rbuf)
        for b in after:
            if b.lwn is not None:
                self._add_writer_dep(node, b.lwn)
        self.nidx += 1
        self.nodes.append(node)

    def dma_d2d(self, q, out, in_, tok):
        fn = lambda eng: eng.dma_start(out=out, in_=in_)
        node = Node(self.nidx, q, [fn], 2.5, "dmar", tok)
        self.nidx += 1
        if tok.rsem is None:
            tok.rsem = self.newsem()
            self.dmabufs.append(tok)
        if tok not in self.outbufs:
            self.outbufs.append(tok)
        tok.rdn.append(node)
        self.nodes.append(node)

    def _schedule(self, nodes):
        inseg = set(id(n) for n in nodes)
        for n in nodes:
            n.nd = 0
            n.users = []
        for n in nodes:
            for d in n.deps:
                if id(d) in inseg:
                    n.nd += 1
                    d.users.append(n)
        if not SCHED:
            return {e: [n for n in nodes if n.e == e] for e in ENGS}
        PRIO = os.environ.get("MK_PRIO", "bl")
        bl = {}
        for n in reversed(nodes):
            m = 0.0
            for u in n.users:
                if bl[id(u)] > m:
                    m = bl[id(u)]
            bl[id(n)] = m + n.cost + 0.15
        if PRIO == "bl":
            key = lambda n: (-bl[id(n)], n.idx)
        else:
            key = lambda n: (n.idx, 0)
        CP = {} if os.environ.get("MK_SIMLOG") == "2" else None
        free = {e: 0.0 for e in ENGS}
        pending = {e: [] for e in ENGS}
        avail = {e: [] for e in ENGS}
        ready_t = {}
        for n in nodes:
            if n.nd == 0:
                heapq.heappush(pending[n.e], (0.0, n.idx, n))
        order = {e: [] for e in ENGS}
        left = len(nodes)
        while left:
            best = None
            for e in ENGS:
                pe_, av = pending[e], avail[e]
                while pe_ and pe_[0][0] <= free[e]:
                    _, i_, n_ = heapq.heappop(pe_)
                    heapq.heappush(av, (key(n_), i_, n_))
                if av:
                    cand = (free[e], av[0][0], e, 0)
                elif pe_:
                    cand = (pe_[0][0], key(pe_[0][2]), e, 1)
                else:
                    continue
                if best is None or cand < best:
                    best = cand
            stt, _, e, src = best
            if src == 0:
                _, _, n = heapq.heappop(avail[e])
            else:
                _, _, n = heapq.heappop(pending[e])
            n.start = stt
            if CP is not None:
                dd = max((d for d in n.deps if id(d) in inseg), key=lambda d: d.fin, default=None)
                if order[e] and abs(free[e] - stt) < 1e-9:
                    CP[id(n)] = ("eng", order[e][-1])
                elif dd is not None:
                    CP[id(n)] = ("dep", dd)
            issue = 0.3 if n.kind != "op" else n.cost
            free[e] = stt + issue
            n.fin = stt + n.cost
            order[e].append(n)
            left -= 1
            for u in n.users:
                u.nd -= 1
                rt = ready_t.get(id(u), 0.0)
                lat = n.fin + (0.10 if u.e != n.e else 0.05)
                if lat > rt:
                    rt = lat
                ready_t[id(u)] = rt
                if u.nd == 0:
                    heapq.heappush(pending[u.e], (rt, u.idx, u))
        if os.environ.get("MK_SIMLOG"):
            mx = max(n.fin for n in nodes)
            busy = {e: round(sum(n.cost for n in order[e])) for e in ENGS}
            print("SEG nodes=%d sim_makespan=%.0fus busy=%s" % (len(nodes), mx, busy))
            if CP is not None:
                cur = max(nodes, key=lambda n: n.fin)
                acc = {}
                path = []
                while cur is not None:
                    why, prv = CP.get(id(cur), (None, None))
                    k_ = (cur.e, cur.kind, why)
                    acc[k_] = acc.get(k_, 0.0) + (cur.fin - cur.start)
                    path.append((round(cur.start, 1), cur.e, cur.kind, why, cur.idx))
                    cur = prv
                print("   critical path time by (engine, kind, reason):", {k: round(v) for k, v in sorted(acc.items(), key=lambda x: -x[1])})
                print("   path len", len(path), "sample:", path[:: max(1, len(path) // 25)])
        return order

    def _wait(self, e, key, sem, val):
        if val <= 0 or self.seen[e].get(key, 0) >= val:
            return
        self.seen[e][key] = val
        self.eng[e].wait_ge(sem, val)
        self.ninstr[e] += 1

    def _wait_node(self, e, d):
        if d.kind == "op":
            if d.e == "pe" and e == "pe":
                return
            self._wait(e, d.e, self.sem[d.e], d.ticket)
        elif d.kind == "dmaw":
            self._wait(e, ("w", id(d.buf)), d.buf.wsem, d.semval)
        else:
            self._wait(e, ("r", id(d.buf)), d.buf.rsem, d.semval)

    def flush(self):
        for e in ENGS:
            assert not self.pend[e], "dangling unmarked ops on " + e
        nodes = self.nodes
        self.nodes = []
        if not nodes:
            return
        order = self._schedule(nodes)
        for e in ENGS:
            for n in order[e]:
                if n.kind == "op":
                    self.cnt[e] += 1
                    n.ticket = self.cnt[e]
                elif n.kind == "dmaw":
                    n.buf.wtot += 16
                    n.semval = n.buf.wtot
                else:
                    n.buf.rtot += 16
                    n.semval = n.buf.rtot
        for e in ENGS:
            for n in order[e]:
                dmax = {}
                for d in sorted(n.deps, key=lambda x: x.idx):
                    if d.kind == "op":
                        self._wait_node(e, d)
                    else:
                        k_ = (d.kind, id(d.buf))
                        if k_ not in dmax or d.semval > dmax[k_].semval:
                            dmax[k_] = d
                for d in dmax.values():
                    self._wait_node(e, d)
                ins = None
                for fn in n.fns:
                    ins = fn(self.eng[e])
                    self.ninstr[e] += 1
                if n.kind == "op":
                    ins.then_inc(self.sem[e], 1)
                elif n.kind == "dmaw":
                    ins.then_inc(n.buf.wsem, 16)
                else:
                    ins.then_inc(n.buf.rsem, 16)
                n.fns = None

    def barrier(self):
        self.flush()
        for e in ENGS:
            for f in ENGS:
                if not (e == "pe" and f == "pe"):
                    self._wait(e, f, self.sem[f], self.cnt[f])
            for b in self.dmabufs:
                if b.wsem is not None:
                    self._wait(e, ("w", id(b)), b.wsem, b.wtot)
                if b.rsem is not None:
                    self._wait(e, ("r", id(b)), b.rsem, b.rtot)

    def wait_dma_reads(self, e, bufs):
        self.flush()
        for b in bufs:
            if b.rsem is not None:
                self._wait(e, ("r", id(b)), b.rsem, b.rtot)

    def finish(self):
        self.flush()
        for b in self.outbufs:
            self._wait("sp", ("r", id(b)), b.rsem, b.rtot)
        for f in ENGS:
            if f != "sp":
                self._wait("sp", f, self.sem[f], self.cnt[f])


def host_consts():
    c = {}
    c["identb"] = np.eye(128, dtype=np.float32).astype(ml_dtypes.bfloat16)
    c["identf"] = np.eye(128, dtype=np.float32)
    perm = np.zeros((128, 128), np.float32)
    for m in range(128):
        base = (m // 64) * 64
        src = base + ((m % 64) + 32) % 64
        perm[src, m] = 1.0
    c["permb"] = perm.astype(ml_dtypes.bfloat16)
    half = 32
    inv_freq = 10000.0 ** (-(np.arange(half, dtype=np.float64) / half))
    pos = np.concatenate([np.arange(NTOK, dtype=np.float64),
                          np.tile(PAST + np.arange(4, dtype=np.float64), 4), np.zeros(16, np.float64)])
    ang = pos[None, :] * inv_freq[:, None]
    p = np.arange(128)
    fi = (p % 64) % 32
    sign = np.where((p % 64) < 32, -1.0, 1.0)
    c["cosT"] = np.cos(ang)[fi, :].astype(np.float32)
    c["sinT"] = (np.sin(ang)[fi, :] * sign[:, None]).astype(np.float32)
    def mult(dist):
        m = np.zeros(dist.shape, np.float32)
        ok = dist >= 0
        m += ok & (dist <= 128)
        m += ok & (dist % 4 == 0) & (dist <= 512)
        m += ok & (dist % 16 == 0) & (dist <= 2048)
        return m
    k = np.arange(128)[:, None]
    q = np.arange(19 * 128)[None, :] - 3 * 128
    c["maskall"] = mult(q - k).astype(ml_dtypes.bfloat16)
    sm = np.zeros((128, 4, 8, 4), np.float32)
    pp = np.arange(128)
    for tq in range(4):
        for it in range(3):
            row = 16 * (32 * it + (pp % 32)) + (pp // 32)
            sm[:, :, it, tq] = mult(2048 + tq - row)[:, None]
        for it in range(3, 7):
            row = 1536 + 128 * (it - 3) + pp
            sm[:, :, it, tq] = mult(2048 + tq - row)[:, None]
        for sq in range(4):
            for tk in range(4):
                if tk <= tq:
                    sm[sq * 4 + tk, sq, 7, tq] = mult(np.array([tq - tk]))[0]
    c["smask"] = sm.astype(ml_dtypes.bfloat16)
    rm = np.ones((128, 512), np.float32); rm[:, 0::64] = 0.0
    c["resetm"] = rm.astype(ml_dtypes.bfloat16)
    rs = np.ones((128, 32), np.float32); rs[:, 0::4] = 0.0
    c["resets"] = rs.astype(ml_dtypes.bfloat16)
    a = np.arange(32)
    c["smaskh"] = ((a[:, None] <= a[None, :]) & (a[:, None] // 4 == a[None, :] // 4) & (a[None, :] < 16)).astype(np.float32).astype(ml_dtypes.bfloat16)
    sel = np.zeros((32, 4), np.float32)
    for sq in range(4):
        sel[sq * 4:(sq + 1) * 4, sq] = 1.0
    c["sels"] = sel
    s = np.arange(128)[:, None]
    t = np.arange(128)[None, :]
    c["hmask"] = ((s <= t) & (s // 64 == t // 64)).astype(np.float32).astype(ml_dtypes.bfloat16)
    return c


CONST_SPECS = [("identb", [128, 128], BF16), ("identf", [128, 128], F32), ("permb", [128, 128], BF16),
               ("cosT", [128, NCOLP], F32), ("sinT", [128, NCOLP], F32), ("maskall", [128, 19 * 128], BF16),
               ("hmask", [128, 128], BF16), ("smask", [128, 4, 8, 4], BF16),
               ("resetm", [128, 512], BF16), ("resets", [128, 32], BF16), ("smaskh", [32, 32], BF16), ("sels", [32, 4], F32)]

IN_SPECS = [
    ("xp", [NTOK, 1024]), ("xs", [NS, 1024]), ("cconv", [4 * 30, 1024]), ("ck", [4, 2048, 1024]),
    ("cv", [4, 2048, 1024]), ("st", [4 * 16, 128, 128]),
    ("w_in_ab", [1024, 7168]), ("w_out_ab", [2048, 1024]), ("w_in_c", [1024, 8192]), ("w_out_c", [2048, 1024]),
    ("gpre", [128, 2, 8]), ("gpost", [128, 2, 1024]), ("convw", [128, 8, 31]), ("convb", [128, 8]),
    ("lng", [128, 8]), ("lnb", [128, 8]), ("gnorm", [128, 1]), ("hlb", [128, 2, 16]),
]

OUT_SPECS = [
    ("yp", [NTOK, 1024]), ("ys", [NS, 1024]), ("ncp", [30, 1024]), ("ncs", [4, 30, 1024]),
    ("nkp", [NTOK, 1024]), ("nvp", [NTOK, 1024]), ("nks", [NS, 1024]), ("nvs", [NS, 1024]),
    ("nsp", [16, 128, 128]), ("nss", [4 * 16, 128, 128]),
]


def build(stop_after=None, dbg=False):
    import os
    LIM = os.environ.get('MK_LIM', '')
    nc = bass.Bass("TRN2", target_bir_lowering=False)
    I = {n: nc.dram_tensor(n, s, F32, kind="ExternalInput").ap() for n, s in IN_SPECS}
    C = {n: nc.dram_tensor(n, s, d, kind="ExternalInput").ap() for n, s, d in CONST_SPECS}
    O = {n: nc.dram_tensor(n, s, F32, kind="ExternalOutput").ap() for n, s in OUT_SPECS}
    x1s = nc.dram_tensor("x1s", [NCOL, 1024], F32, kind="Internal").ap()
    woutb = [nc.dram_tensor("woutb%d" % i, [2048, 1024], BF16, kind="Internal").ap() for i in range(2)]
    kvselb = [nc.dram_tensor("kvselb%d" % i, [4, 128, 7, 1024], BF16, kind="Internal").ap() for i in range(2)]
    DBG = {}
    if dbg:
        DBG["x1"] = nc.dram_tensor("dbg_x1", [NCOL, 1024], F32, kind="ExternalOutput").ap()
        DBG["ybT"] = nc.dram_tensor("dbg_ybT", [128, 8, NCOL], BF16, kind="ExternalOutput").ap()
        DBG["hT"] = nc.dram_tensor("dbg_hT", [128, 8, NCOL], BF16, kind="ExternalOutput").ap()
        DBG["yaT"] = nc.dram_tensor("dbg_yaT", [128, 8, NCOL], BF16, kind="ExternalOutput").ap()
        DBG["zT"] = nc.dram_tensor("dbg_zT", [128, 16, NCOL], BF16, kind="ExternalOutput").ap()

    with contextlib.ExitStack() as st:
        mk = MK(nc, st)

        def T(name, shape, dt=F32, stack=st):
            return stack.enter_context(nc.sbuf_tensor("sb_" + name, shape, dt))

        def PS(name, shape, dt=F32, stack=st):
            return stack.enter_context(nc.psum_tensor("pp_" + name, shape, dt))

        ps = [PS("ps%d" % i, [128, 512]) for i in range(7)]
        bps = [Buf("ps%d" % i, excl=True) for i in range(7)]
        psb = PS("psb", [128, 1024], BF16)
        bpsb = Buf("psb", excl=True)

        cst = {}
        bcst = Buf("consts")
        for n, s, d in CONST_SPECS:
            if n in ("cosT", "sinT", "maskall", "smask"):
                continue
            cst[n] = T("c_" + n, s, d)
            mk.dma("sp", cst[n][:], C[n], wbuf=bcst)
        prm = {}
        for n in ("gpre", "convw", "convb", "lng", "lnb", "gnorm", "hlb"):
            s = dict(IN_SPECS)[n]
            prm[n] = T("p_" + n, s, F32)
            mk.dma("sp", prm[n][:], I[n], wbuf=bcst)
        onesb = T("onesb", [128, 128], BF16)
        bones = Buf("onesb")
        mk.op("dve", lambda e: e.memset(onesb[:], 1.0), writes=[bones])

        bwoutb = [Buf("woutb%d" % i) for i in range(2)]

        def precast_wout(i_, gate):
            nm_ = ("w_out_ab", "w_out_c")[i_]
            for g4 in range(4):
                mk.dma("pool", woutb[i_][g4 * 512:(g4 + 1) * 512, :], I[nm_][g4 * 512:(g4 + 1) * 512, :], wbuf=bwoutb[i_], after=gate)

        hT = T("hT", [128, 8, NCOLP], BF16)
        bhT = [Buf("hT%d" % i) for i in range(17)]

        ybT = T("ybT", [128, 8, NCOLP], BF16)
        bybT = [[Buf("ybT%d_%d" % (hp, tc)) for tc in range(5)] for hp in range(8)]

        def hT_bufs(tc):
            return [bhT[16]] if tc == 4 else bhT[4 * tc:4 * tc + 4]

        TILES = [(i * 128, 128) for i in range(16)] + [(NTOK, 32)]

        nrm = {"i": 0}
        junk = T("junk", [128, 1024], BF16)
        bjunk = Buf("junk")
        hb = [T("hb%d" % i, [128, 1024], BF16) for i in range(2)]
        bhb = [Buf("hb%d" % i) for i in range(2)]
        ssb = [T("ss%d" % i, [128, 4], F32) for i in range(2)]
        bss = [Buf("ss%d" % i) for i in range(2)]

        def rms_stats(xt, bx, P, width, ss, bssb):
            mk.op("act", lambda e: e.activation(junk[:P, 0:width], xt, AF.Square, accum_out=ss[:P, 0:1]),
                  reads=[bx], writes=[bjunk, bssb])
            mk.op("dve", lambda e: e.tensor_scalar(ss[:P, 1:2], ss[:P, 0:1], 1.0 / width, EPS, ALU.mult, ALU.add),
                  reads=[bssb], writes=[bssb])
            mk.op("act", lambda e: e.activation(ss[:P, 1:2], ss[:P, 1:2], AF.Ln), reads=[bssb], writes=[bssb])
            mk.op("act", lambda e: e.activation(ss[:P, 2:3], ss[:P, 1:2], AF.Exp, scale=-0.5), reads=[bssb], writes=[bssb])

        def norm_to_hT(xt, bx, ti, layer):
            row0, P = TILES[ti]
            k = nrm["i"] % 2
            nrm["i"] += 1
            ss, bs_ = ssb[k], bss[k]
            rms_stats(xt[:P, :], bx, P, 1024, ss, bs_)
            mk.op("act", lambda e: e.activation(hb[k][:P, :], xt[:P, :], AF.Identity, scale=ss[:P, 2:3]),
                  reads=[bx, bs_], writes=[bhb[k]])
            if 'noT' in LIM:
                return
            for d in range(8):
                mk.op("pe", lambda e, d=d: e.transpose(psb[:, d * 128:d * 128 + P], hb[k][:P, d * 128:(d + 1) * 128],
                                                      cst["identb"][:P, :P]),
                      reads=[bhb[k], bcst], writes=[bpsb], mark=(d == 7), n=128)
            src = psb[:, :].rearrange("p (d t) -> p d t", t=128)[:, :, 0:P]
            g = prm["gpre"][:, layer, :].rearrange("p (d o) -> p d o", o=1).to_broadcast([128, 8, P])
            mk.op("dve", lambda e: e.tensor_tensor(hT[:, :, row0:row0 + P], src, g, ALU.mult),
                  reads=[bpsb, bcst], writes=[bhT[ti]])

        xr = [T("xr%d" % i, [128, 1024], F32) for i in range(2)]
        bxr = [Buf("xr%d" % i) for i in range(2)]

        def x_src(ti):
            row0, P = TILES[ti]
            return I["xp"][row0:row0 + P, :] if ti < 16 else I["xs"][:, :]

        for ti in range((16 if 'nosamp' in LIM else 17) if 'ti1' not in LIM else 1):
            row0, P = TILES[ti]
            k = ti % 2
            mk.dma("sp", xr[k][:(P if ti < 16 else NS), :], x_src(ti), wbuf=bxr[k])
            if 'noN' not in LIM:
                norm_to_hT(xr[k], bxr[k], ti, 0)

        NW = 2
        wsl = [T("w%d" % i, [128, 8, 512], BF16) for i in range(NW)]
        bws = [Buf("w%d" % i) for i in range(NW)]
        wstate = {"n": 0}

        def wload(parts):
            k = wstate["n"] % NW
            wstate["n"] += 1
            off = 0
            for ap in parts:
                w = ap.shape[1]
                mk.dma("pool", wsl[k][:, :, off:off + w], ap.rearrange("(c p) f -> p c f", p=128), wbuf=bws[k])
                off += w
            return wsl[k], bws[k]

        def proj_fm(w, bw, f0, tc, bank):
            c0, wd = TCH[tc]
            pt_, pb_ = (ps[bank], bps[bank]) if isinstance(bank, int) else bank
            for d in range(8):
                mk.op("pe", lambda e, d=d: e.matmul(pt_[:, 0:wd], w[:, d, f0:f0 + 128], hT[:, d, c0:c0 + wd],
                                                    start=(d == 0), stop=(d == 7)),
                      reads=[bw] + hT_bufs(tc), writes=[pb_], mark=(d == 7), n=wd)

        def sigmoid_to(dst, bdst, src, bsrc, keep_e=None):
            e_ap, e_b = keep_e if keep_e is not None else (dst, bdst)
            mk.op("act", lambda e: e.activation(e_ap, src, AF.Exp, scale=-1.0), reads=[bsrc], writes=[e_b])
            mk.op("act", lambda e: e.activation(dst, e_ap, AF.Ln, bias=1.0), reads=[e_b], writes=[bdst])
            mk.op("act", lambda e: e.activation(dst, dst, AF.Exp, scale=-1.0), reads=[bdst], writes=[bdst])

        def silu_to(dst, bdst, src, bsrc, tmp, btmp, eng="dve", xs=None):
            if os.environ.get("MK_NOXS"):
                xs = None
            if xs is not None:
                mk.op("dve", lambda e: e.tensor_copy(xs[0], src), reads=[bsrc], writes=[xs[1]])
            sigmoid_to(tmp, btmp, src, bsrc)
            if xs is not None:
                mk.op(eng, lambda e: e.tensor_tensor(dst, xs[0], tmp, ALU.mult), reads=[xs[1], btmp], writes=[bdst])
            else:
                mk.op(eng, lambda e: e.tensor_tensor(dst, src, tmp, ALU.mult), reads=[bsrc, btmp], writes=[bdst])

        sAtt = st.enter_context(contextlib.ExitStack())
        KTs = T("KTs", [128, 8, 32], BF16, sAtt)
        QTs = T("QTs", [128, 8, 32], BF16, sAtt)
        SGs = T("SGs", [128, 8, 32], BF16, sAtt)
        VAs = T("VAs", [32, 8, 192], BF16, sAtt)
        bsmp = [Buf("smp%d" % hp) for hp in range(8)]
        mk.op("dve", lambda e: e.memset(VAs[:, :, 64:128], 1.0), writes=bsmp)

        with contextlib.ExitStack() as sB:
            if 'noB' in LIM:
                raise_skip = True
            cosT = T("cosT", [128, NCOLP], F32, sB)
            sinT = T("sinT", [128, NCOLP], F32, sB)
            maskall = T("maskall", [128, 19 * 128], BF16, sB)
            brope = Buf("rope")
            mk.dma("sp", cosT[:], C["cosT"], wbuf=brope)
            mk.dma("sp", sinT[:], C["sinT"], wbuf=brope)
            mk.dma("sp", maskall[:], C["maskall"], wbuf=brope)
            KT = [T("KT%d" % i, [128, 2, NCOLP], BF16, sB) for i in range(2)]
            bKT = [Buf("KT%d" % i) for i in range(2)]
            for i in range(2):
                mk.op("dve", lambda e, i=i: e.memset(KT[i][:, :, :], 0.0), writes=[bKT[i]], n=4200)
            QT = [T("QT%d" % i, [128, NCOLP], BF16, sB) for i in range(2)]
            bQT = [Buf("QT%d" % i) for i in range(2)]
            SG = [T("SG%d" % i, [128, NCOLP], BF16, sB) for i in range(2)]
            bSG = [Buf("SG%d" % i) for i in range(2)]
            VA = [T("VA%d" % i, [128, 16, 192], BF16, sB) for i in range(2)]
            bVA = [Buf("VA%d" % i) for i in range(2)]
            for i in range(2):
                mk.op("dve", lambda e, i=i: e.memset(VA[i][:, :, 64:128], 1.0), writes=[bVA[i]])
            raw = [T("raw%d" % i, [128, 512], BF16, sB) for i in range(2)]
            braw = [Buf("raw%d" % i) for i in range(2)]
            t1 = [T("t1_%d" % i, [128, 512], F32, sB) for i in range(2)]
            bt1 = [Buf("t1_%d" % i) for i in range(2)]
            t2 = [T("t2_%d" % i, [128, 512], F32, sB) for i in range(2)]
            bt2 = [Buf("t2_%d" % i) for i in range(2)]
            kst = [T("kst%d" % i, [128, 4, 128], F32, sB) for i in range(2)]
            bkst = [Buf("kst%d" % i) for i in range(2)]
            pt = [T("pt%d" % i, [128, 512], BF16, sB) for i in range(4)]
            bpt = [Buf("pt%d" % i) for i in range(4)]
            rz = T("rz", [128, 512], F32, sB)
            brz = Buf("rz")
            ytmp = T("ytmp", [128, 512], F32, sB)
            bytmp = Buf("ytmp")
            ctr = {"rope": 0, "kst": 0, "pt": 0, "st": 0, "ot": 0, "sg": 0}
            sgtB = [T("sgtB%d" % i, [128, 512], F32, sB) for i in range(2)]
            bsgtB = [Buf("sgtB%d" % i) for i in range(2)]

            def rope_chunk(bank, tc, dst, bdst, want_f32):
                c0, wd = TCH[tc]
                k = ctr["rope"] % 2
                ctr["rope"] += 1
                mk.op("act", lambda e: e.copy(raw[k][:, 0:wd], ps[bank][:, 0:wd]), reads=[bps[bank]], writes=[braw[k]])
                mk.op("dve", lambda e: e.tensor_tensor(t1[k][:, 0:wd], ps[bank][:, 0:wd], cosT[:, c0:c0 + wd], ALU.mult),
                      reads=[bps[bank], brope], writes=[bt1[k]])
                mk.op("pe", lambda e: e.matmul(ps[2][:, 0:wd], cst["permb"][:], raw[k][:, 0:wd], start=True, stop=True),
                      reads=[bcst, braw[k]], writes=[bps[2]])
                mk.op("dve", lambda e: e.tensor_tensor(t2[k][:, 0:wd], ps[2][:, 0:wd], sinT[:, c0:c0 + wd], ALU.mult),
                      reads=[bps[2], brope], writes=[bt2[k]])
                if want_f32:
                    mk.op("dve", lambda e: e.tensor_tensor(t1[k][:, 0:wd], t1[k][:, 0:wd], t2[k][:, 0:wd], ALU.add),
                          reads=[bt1[k], bt2[k]], writes=[bt1[k]])
                    mk.op("act", lambda e: e.copy(dst[0:64, 0, c0:c0 + wd], t1[k][0:64, 0:wd]), reads=[bt1[k]], writes=[bdst])
                    mk.op("dve", lambda e: e.tensor_copy(dst[64:128, 1, c0:c0 + wd], t1[k][64:128, 0:wd]), reads=[bt1[k]], writes=[bdst])
                    return t1[k], bt1[k]
                mk.op("dve", lambda e: e.tensor_tensor(dst[:, c0:c0 + wd], t1[k][:, 0:wd], t2[k][:, 0:wd], ALU.add),
                      reads=[bt1[k], bt2[k]], writes=[bdst])
                return None, None

            psbF = psb[:, :].bitcast(F32)
            ST_RING = [(ps[3], bps[3]), (ps[4], bps[4]), (psbF, bpsb)]

            def proj_gen(hp, kk):
                w, bw = wload([I["w_in_ab"][:, 3072 + hp * 128:3072 + (hp + 1) * 128],
                               I["w_in_ab"][:, 4096 + hp * 128:4096 + (hp + 1) * 128],
                               I["w_in_ab"][:, 5120 + hp * 128:5120 + (hp + 1) * 128],
                               I["w_in_ab"][:, 6144 + hp * 128:6144 + (hp + 1) * 128]])
                for tc in range(5):
                    c0, wd = TCH[tc]
                    bank = tc % 2
                    proj_fm(w, bw, 128, tc, bank)
                    kf, bkf = rope_chunk(bank, tc, KT[kk], bKT[kk], True)
                    ks = ctr["kst"] % 2
                    ctr["kst"] += 1
                    nsub = (wd + 127) // 128
                    for j in range(nsub):
                        P = min(128, wd - j * 128)
                        mk.op("pe", lambda e, j=j, P=P: e.transpose(ps[2][0:P, j * 128:(j + 1) * 128],
                                                                      kf[:, j * 128:j * 128 + P], cst["identf"][:, :]),
                              reads=[bkf, bcst], writes=[bps[2]], mark=(j == nsub - 1), n=128)
                    P = min(128, wd)
                    mk.op("act", lambda e, P=P, nsub=nsub: e.copy(
                        kst[ks][0:P, 0:nsub, :], ps[2][0:P, 0:nsub * 128].rearrange("p (j f) -> p j f", f=128)),
                        reads=[bps[2]], writes=[bkst[ks]])
                    if tc < 4:
                        dst = O["nkp"][c0:c0 + 512, hp * 128:(hp + 1) * 128].rearrange("(j p) f -> p j f", p=128)
                        mk.dma("sp", dst, kst[ks][:, :, :], rbuf=bkst[ks], is_output=True)
                    else:
                        mk.dma("sp", O["nks"][:, hp * 128:(hp + 1) * 128], kst[ks][0:NS, 0, :], rbuf=bkst[ks], is_output=True)
                    yield
                for g in range(5):
                    bank = g % 2
                    if g < 4:
                        for j in range(4):
                            ti = g * 4 + j
                            for d in range(8):
                                mk.op("pe", lambda e, d=d, j=j, ti=ti: e.matmul(
                                    ps[bank][:, j * 128:(j + 1) * 128], hT[:, d, ti * 128:(ti + 1) * 128],
                                    w[:, d, 256:384], start=(d == 0), stop=(d == 7)),
                                    reads=[bw, bhT[ti]], writes=[bps[bank]], mark=(d == 7 and j == 3), n=128)
                        src = ps[bank][:, :].rearrange("p (j f) -> p j f", f=128)
                        mk.op("act", lambda e: e.copy(VA[kk][:, g * 4:g * 4 + 4, 0:64], src[:, :, 0:64]),
                              reads=[bps[bank]], writes=[bVA[kk]])
                        mk.op("act", lambda e: e.copy(VA[kk][:, g * 4:g * 4 + 4, 128:192], src[:, :, 64:128]),
                              reads=[bps[bank]], writes=[bVA[kk]])
                        ks = ctr["kst"] % 2
                        ctr["kst"] += 1
                        mk.op("dve", lambda e: e.tensor_copy(kst[ks][:, :, :], src), reads=[bps[bank]], writes=[bkst[ks]])
                        dst = O["nvp"][g * 512:(g + 1) * 512, hp * 128:(hp + 1) * 128].rearrange("(j p) f -> p j f", p=128)
                        mk.dma("sp", dst, kst[ks][:, :, :], rbuf=bkst[ks], is_output=True)
                    else:
                        for d in range(8):
                            mk.op("pe", lambda e, d=d: e.matmul(ps[bank][0:32, 0:128], hT[:, d, NTOK:NCOLP], w[:, d, 256:384],
                                                                start=(d == 0), stop=(d == 7)),
                                  reads=[bw, bhT[16]], writes=[bps[bank]], mark=(d == 7), n=128)
                        ks = ctr["kst"] % 2
                        ctr["kst"] += 1
                        mk.op("dve", lambda e: e.tensor_copy(kst[ks][0:32, 0, :], ps[bank][0:32, 0:128]),
                              reads=[bps[bank]], writes=[bkst[ks]])
                        mk.op("act", lambda e: e.copy(VAs[:, hp, 0:64], ps[bank][0:32, 0:64]), reads=[bps[bank]], writes=[bsmp[hp]])
                        mk.op("act", lambda e: e.copy(VAs[:, hp, 128:192], ps[bank][0:32, 64:128]), reads=[bps[bank]], writes=[bsmp[hp]])
                        mk.dma("sp", O["nvs"][:, hp * 128:(hp + 1) * 128], kst[ks][0:NS, 0, :], rbuf=bkst[ks], is_output=True)
                    yield
                for tc in range(5):
                    bank = tc % 2
                    proj_fm(w, bw, 0, tc, bank)
                    rope_chunk(bank, tc, QT[kk], bQT[kk], False)
                    yield
                for tc in range(5):
                    c0, wd = TCH[tc]
                    bank = tc % 2
                    proj_fm(w, bw, 384, tc, bank)
                    k_ = ctr["sg"] % 2
                    ctr["sg"] += 1
                    silu_to(SG[kk][:, c0:c0 + wd], bSG[kk], ps[bank][:, 0:wd], bps[bank], sgtB[k_][:, 0:wd], bsgtB[k_],
                            eng="dve", xs=(t2[k_][:, 0:wd], bt2[k_]))
                    yield
                mk.op("dve", lambda e: e.tensor_copy(KTs[0:64, hp, :], KT[kk][0:64, 0, NTOK:NCOLP]), reads=[bKT[kk]], writes=[bsmp[hp]])
                mk.op("dve", lambda e: e.tensor_copy(KTs[64:128, hp, :], KT[kk][64:128, 1, NTOK:NCOLP]), reads=[bKT[kk]], writes=[bsmp[hp]])
                mk.op("dve", lambda e: e.tensor_copy(QTs[:, hp, :], QT[kk][:, NTOK:NCOLP]), reads=[bQT[kk]], writes=[bsmp[hp]])
                mk.op("dve", lambda e: e.tensor_copy(SGs[:, hp, :], SG[kk][:, NTOK:NCOLP]), reads=[bSG[kk]], writes=[bsmp[hp]])

            def attn(hp, kk, bg):
                its = [(hh, c, j) for hh in range(2) for c in range(4) for j in range(4 * c + 4)]
                LA = 2
                slots = {}
                obs = {}
                for n in range(len(its) + LA):
                    if n < len(its):
                        hh, c, j = its[n]
                        hb0 = hh * 64
                        stt, bst = ST_RING[ctr["st"] % 3]
                        ctr["st"] += 1
                        slots[n] = (stt, bst)
                        mk.op("pe", lambda e: e.matmul(stt[:, :], KT[kk][:, hh, j * 128:(j + 1) * 128],
                                                       QT[kk][:, c * 512:(c + 1) * 512], start=True, stop=True),
                              reads=[bKT[kk], bQT[kk]], writes=[bst])
                    m = n - LA
                    if m < 0:
                        continue
                    hh, c, j = its[m]
                    nj = 4 * c + 4
                    vcol = 0 if hh == 0 else 64
                    num0, z0 = (0, 64) if hh == 0 else (64, 0)
                    if j == 0:
                        obs[(hh, c)] = 5 + (ctr["ot"] % 2)
                        ctr["ot"] += 1
                    ob = obs[(hh, c)]
                    stt, bst = slots.pop(m)
                    pk = ctr["pt"] % 4
                    ctr["pt"] += 1
                    mk.op("act", lambda e: e.activation(pt[pk][:, :], stt[:, :], AF.Exp, scale=0.125), reads=[bst], writes=[bpt[pk]])
                    mo = (4 * c - j + 3) * 128
                    mk.op("dve", lambda e: e.tensor_tensor(pt[pk][:, :], pt[pk][:, :], maskall[:, mo:mo + 512], ALU.mult),
                          reads=[bpt[pk], brope], writes=[bpt[pk]])
                    mk.op("pe", lambda e: e.matmul(ps[ob][:, :], VA[kk][:, j, vcol:vcol + 128], pt[pk][:, :],
                                                   start=(j == 0), stop=(j == nj - 1)),
                          reads=[bVA[kk], bpt[pk]], writes=[bps[ob]])
                    if j == nj - 1:
                        mk.op("act", lambda e: e.activation(rz[num0:num0 + 64, :], ps[ob][z0:z0 + 64, :], AF.Ln), reads=[bps[ob]], writes=[brz])
                        mk.op("act", lambda e: e.activation(rz[num0:num0 + 64, :], rz[num0:num0 + 64, :], AF.Exp, scale=-1.0), reads=[brz], writes=[brz])
                        mk.op("dve", lambda e: e.tensor_tensor(ytmp[num0:num0 + 64, :], ps[ob][num0:num0 + 64, :], rz[num0:num0 + 64, :], ALU.mult),
                              reads=[bps[ob], brz], writes=[bytmp])
                        mk.op("dve", lambda e: e.tensor_tensor(ybT[num0:num0 + 64, hp, c * 512:(c + 1) * 512], ytmp[num0:num0 + 64, :],
                                                                SG[kk][num0:num0 + 64, c * 512:(c + 1) * 512], ALU.mult),
                              reads=[bytmp, bSG[kk]], writes=[bybT[hp][c]])
                    if bg is not None and m % 4 == 3:
                        next(bg, None)

            NP = 8 if 'hp1' not in LIM else 1
            for _ in proj_gen(0, 0):
                pass
            bkvsel = [Buf("kvselb%d" % i) for i in range(2)]

            def precast_sel(i_, src, gate):
                for s_ in range(4):
                    v = src[s_]
                    for r in range(4):
                        sv = v[0:1536, :].rearrange("(t i r) f -> r i t f", r=16, i=32)[r]
                        mk.dma("pool", kvselb[i_][s_, r * 32:(r + 1) * 32, 0:3, :], sv, wbuf=bkvsel[i_], after=gate)
                    mk.dma("pool", kvselb[i_][s_, :, 3:7, :], v[1536:2048, :].rearrange("(j p) f -> p j f", p=128), wbuf=bkvsel[i_], after=gate)

            for hp in range(NP):
                bg = proj_gen(hp + 1, (hp + 1) % 2) if hp + 1 < NP else None
                if hp == 1:
                    precast_sel(0, I["ck"], [bybT[0][3]])
                if hp == 2:
                    precast_sel(1, I["cv"], [bybT[1][3]])
                if hp == 3:
                    precast_wout(0, [bybT[2][3]])
                if hp == 4:
                    precast_wout(1, [bybT[3][3]])
                attn(hp, hp % 2, bg)
                if bg is not None:
                    for _ in bg:
                        pass
            mk.barrier()
        with contextlib.ExitStack() as sS:
            smask = T("smask", [128, 4, 8, 4], BF16, sS)
            bsmask = Buf("smask")
            mk.dma("sp", smask[:], C["smask"], wbuf=bsmask)
            Ksel = [T("Ksel%d" % i, [128, 7, 1024], BF16, sS) for i in range(2)]
            Vsel = [T("Vsel%d" % i, [128, 7, 1024], BF16, sS) for i in range(2)]
            bKsel = [Buf("Ksel%d" % i) for i in range(2)]
            bVsel = [Buf("Vsel%d" % i) for i in range(2)]
            KselT = [T("KselT%d" % i, [128, 896], BF16, sS) for i in range(2)]
            bKselT = [Buf("KselT%d" % i) for i in range(2)]
            VselA = [T("VselA%d" % i, [128, 7, 192], BF16, sS) for i in range(2)]
            bVselA = [Buf("VselA%d" % i) for i in range(2)]
            for i in range(2):
                mk.op("dve", lambda e, i=i: e.memset(VselA[i][:, :, 64:128], 1.0), writes=[bVselA[i]])
            pts = [T("pts%d" % i, [128, 32], BF16, sS) for i in range(2)]
            bpts = [Buf("pts%d" % i) for i in range(2)]
            rzs = T("rzs", [128, 4], F32, sS)
            brzs = Buf("rzs")
            yts = T("yts", [128, 4], F32, sS)
            byts = Buf("yts")
            cs = {"kt": 0, "pt": 0, "sb": 0}

            def load_sel(dst, bdst, src, s_):
                i_ = 0 if src is I["ck"] else 1
                mk.dma("sp", dst[:, :, :], kvselb[i_][s_], wbuf=bdst, after=[bkvsel[i_]])

            load_sel(Ksel[0], bKsel[0], I["ck"], 0)
            load_sel(Vsel[0], bVsel[0], I["cv"], 0)
            for s_ in range(4):
                kb = s_ % 2
                if s_ + 1 < 4:
                    load_sel(Ksel[1 - kb], bKsel[1 - kb], I["ck"], s_ + 1)
                    load_sel(Vsel[1 - kb], bVsel[1 - kb], I["cv"], s_ + 1)
                for hp in range(8):
                    kt = cs["kt"] % 2
                    cs["kt"] += 1
                    for it in range(7):
                        mk.op("pe", lambda e, it=it: e.transpose(psb[:, it * 128:(it + 1) * 128], Ksel[kb][:, it, hp * 128:(hp + 1) * 128],
                                                                 cst["identb"][:, :]),
                              reads=[bKsel[kb], bcst], writes=[bpsb], mark=(it == 6), n=128)
                    mk.op("act", lambda e: e.copy(KselT[kt][:, :], psb[:, 0:896]), reads=[bpsb], writes=[bKselT[kt]])
                    mk.op("act", lambda e: e.copy(VselA[kt][:, :, 0:64], Vsel[kb][:, :, hp * 128:hp * 128 + 64]),
                          reads=[bVsel[kb]], writes=[bVselA[kt]], n=448)
                    mk.op("act", lambda e: e.copy(VselA[kt][:, :, 128:192], Vsel[kb][:, :, hp * 128 + 64:hp * 128 + 128]),
                          reads=[bVsel[kb]], writes=[bVselA[kt]], n=448)
                    for hh in range(2):
                        hb0 = hh * 64
                        vcol = 0 if hh == 0 else 64
                        num0, z0 = (0, 64) if hh == 0 else (64, 0)
                        sb = 3 + (cs["sb"] % 2)
                        ob = 5 + (cs["sb"] % 2)
                        cs["sb"] += 1
                        pk = cs["pt"] % 2
                        cs["pt"] += 1
                        qv = QTs[hb0:hb0 + 64, hp, s_ * 4:s_ * 4 + 4]
                        for it in range(7):
                            mk.op("pe", lambda e, it=it: e.matmul(ps[sb][:, it * 4:it * 4 + 4], KselT[kt][hb0:hb0 + 64, it * 128:(it + 1) * 128], qv,
                                                                  start=True, stop=True),
                                  reads=[bKselT[kt], bsmp[hp]], writes=[bps[sb]], mark=False, n=16)
                        mk.op("pe", lambda e: e.matmul(ps[sb][0:32, 28:32], KTs[hb0:hb0 + 64, hp, :], qv, start=True, stop=True),
                              reads=[bsmp[hp]], writes=[bps[sb]])
                        mk.op("act", lambda e: e.activation(pts[pk][:, 0:28], ps[sb][:, 0:28], AF.Exp, scale=0.125), reads=[bps[sb]], writes=[bpts[pk]], n=28)
                        mk.op("act", lambda e: e.activation(pts[pk][0:32, 28:32], ps[sb][0:32, 28:32], AF.Exp, scale=0.125),
                              reads=[bps[sb]], writes=[bpts[pk]], n=4)
                        mk.op("dve", lambda e: e.tensor_tensor(pts[pk][:, 0:28], pts[pk][:, 0:28],
                                                               smask[:, s_, 0:7, :], ALU.mult),
                              reads=[bpts[pk], bsmask], writes=[bpts[pk]], n=28)
                        mk.op("dve", lambda e: e.tensor_tensor(pts[pk][0:32, 28:32], pts[pk][0:32, 28:32], smask[0:32, s_, 7, :], ALU.mult),
                              reads=[bpts[pk], bsmask], writes=[bpts[pk]], n=4)
                        for it in range(7):
                            mk.op("pe", lambda e, it=it: e.matmul(ps[ob][:, 0:4], VselA[kt][:, it, vcol:vcol + 128], pts[pk][:, it * 4:it * 4 + 4],
                                                                  start=(it == 0), stop=False),
                                  reads=[bVselA[kt], bpts[pk]], writes=[bps[ob]], mark=False, n=16)
                        mk.op("pe", lambda e: e.matmul(ps[ob][:, 0:4], VAs[:, hp, vcol:vcol + 128], pts[pk][0:32, 28:32], start=False, stop=True),
                              reads=[bsmp[hp], bpts[pk]], writes=[bps[ob]])
                        mk.op("dve", lambda e: e.reciprocal(rzs[num0:num0 + 64, :], ps[ob][z0:z0 + 64, 0:4]), reads=[bps[ob]], writes=[brzs], n=16)
                        mk.op("dve", lambda e: e.tensor_tensor(yts[num0:num0 + 64, :], ps[ob][num0:num0 + 64, 0:4], rzs[num0:num0 + 64, :], ALU.mult),
                              reads=[bps[ob], brzs], writes=[byts], n=4)
                        mk.op("dve", lambda e: e.tensor_tensor(ybT[num0:num0 + 64, hp, NTOK + s_ * 4:NTOK + s_ * 4 + 4], yts[num0:num0 + 64, :],
                                                               SGs[num0:num0 + 64, hp, s_ * 4:s_ * 4 + 4], ALU.mult),
                              reads=[byts, bsmp[hp]], writes=[bybT[hp][4]], n=4)
            for hp in range(8):
                mk.op("dve", lambda e, hp=hp: e.memset(ybT[:, hp, NTOK + NS:NCOLP], 0.0), writes=[bybT[hp][4]])
            mk.barrier()
        sAtt.close()
        if dbg:
            bdbg = Buf("dbg")
            mk.barrier()
            mk.dma("sp", DBG["ybT"], ybT[:, :, 0:NCOL], rbuf=bdbg, is_output=True)
            mk.dma("sp", DBG["hT"], hT[:, :, 0:NCOL], rbuf=bdbg, is_output=True)

        sCD = st.enter_context(contextlib.ExitStack())
        yaT = T("yaT", [128, 8, NCOLP], BF16, sCD)
        byaT = [[Buf("yaT%d_%d" % (c, tc)) for tc in range(5)] for c in range(8)]
        with contextlib.ExitStack() as sC:
            yconv2 = [T("yconv%d" % i, [128, 8, 512], F32, sC) for i in range(2)]
            byc2 = [[Buf("yconv%d_%d" % (i, c)) for c in range(8)] for i in range(2)]
            meant2 = [T("meant0", [128, 512], F32, sC)] * 2
            bmean2 = [Buf("meant0")] * 2
            rstdt2 = [T("rstdt0", [128, 512], F32, sC)] * 2
            brstd2 = [Buf("rstdt0")] * 2
            uext = [T("uext%d" % i, [128, 544], F32, sC) for i in range(2)]
            buext = [Buf("uext%d" % i) for i in range(2)]
            tails = T("tails", [128, 8, 30], F32, sC)
            btails = [Buf("tails%d" % c) for c in range(8)]
            ybf = [T("ybf0", [128, 512], BF16, sC)] * 2
            bybf = [Buf("ybf0")] * 2
            ysq = [T("ysq0", [128, 512], BF16, sC)] * 2
            bysq = [Buf("ysq0")] * 2
            lnt = [T("lnt%d" % i, [128, 512], F32, sC) for i in range(2)]
            blnt = [Buf("lnt%d" % i) for i in range(2)]
            saog = [T("saog%d" % i, [128, 512], BF16, sC) for i in range(2)]
            bsaog = [Buf("saog%d" % i) for i in range(2)]
            uexts = T("uexts", [128, 8, 4, 34], F32, sC)
            buexts = [Buf("uexts%d" % c) for c in range(8)]
            cct = T("cct", [128, 1024], F32, sC)
            bcct = Buf("cct")
            ostg = [T("ostg%d" % i, [32, 128], F32, sC) for i in range(2)]
            bostg = [Buf("ostg%d" % i) for i in range(2)]
            cn = {"u": 0, "sg": 0, "yb": 0, "ln": 0, "sa": 0, "os": 0, "dg": 0}
            dgs = [T("dgs%d" % i, [128, 31, 128], BF16, sC) for i in range(2)]
            bdgs = [Buf("dgs%d" % i) for i in range(2)]
            ubf = [T("ubf%d" % i, [128, 544], BF16, sC) for i in range(2)]
            bubf = [Buf("ubf%d" % i) for i in range(2)]
            for c in range(8):
                mk.op("dve", lambda e, c=c: e.memset(tails[:, c, :], 0.0), writes=[btails[c]])
            mk.dma("sp", cct[0:120, :], I["cconv"], wbuf=bcct)
            for c in range(8):
                mk.op("pe", lambda e, c=c: e.transpose(ps[0][:, 0:120], cct[0:120, c * 128:(c + 1) * 128], cst["identf"][0:120, 0:120]),
                      reads=[bcct, bcst], writes=[bps[0]], n=128)
                mk.op("act", lambda e, c=c: e.copy(uexts[:, c, :, 0:30], ps[0][:, 0:120].rearrange("p (s t) -> p s t", t=30)),
                      reads=[bps[0]], writes=[buexts[c]])
            sgt = [cct[:, i * 512:(i + 1) * 512] for i in range(2)]
            bsgt = [Buf("sgt%d" % i) for i in range(2)]
            for b_ in bsgt:
                b_.lwn = bcct.lwn
                b_.rdn = list(bcct.rdn)
            bd2d = Buf("d2d")
            for s_ in range(4):
                mk.dma_d2d("sp", O["ncs"][s_, 0:26, :], I["cconv"][s_ * 30 + 4:s_ * 30 + 30, :], bd2d)

            def out_rows(src_ap, bsrc, ncols, dst_fn):
                k = cn["os"] % 2
                cn["os"] += 1
                mk.op("pe", lambda e: e.transpose(ps[1][0:ncols, 0:128], src_ap, cst["identf"][:, :]),
                      reads=[bsrc, bcst], writes=[bps[1]], n=128)
                mk.op("act", lambda e: e.copy(ostg[k][0:ncols, :], ps[1][0:ncols, 0:128]), reads=[bps[1]], writes=[bostg[k]])
                dst_fn(ostg[k], bostg[k])

            bwsA = [Buf("wA%d" % i) for i in range(NW)]
            bwsB = [Buf("wB%d" % i) for i in range(NW)]
            wsub = {"A": 0, "B": 0}

            def wload_sub(which, parts):
                k_ = wsub[which] % NW
                wsub[which] += 1
                base = 0 if which == "A" else 256
                bufs = bwsA if which == "A" else bwsB
                off = base
                for ap in parts:
                    wd_ = ap.shape[1]
                    mk.dma("pool", wsl[k_][:, :, off:off + wd_], ap.rearrange("(c p) f -> p c f", p=128), wbuf=bufs[k_])
                    off += wd_
                return wsl[k_][:, :, base:base + 256], bufs[k_]

            for tc in (4, 0, 1, 2, 3):
                c0, wd = TCH[tc]
                yconv, byc = yconv2[tc % 2], byc2[tc % 2]
                meant, bmean, rstdt, brstd = meant2[tc % 2], bmean2[tc % 2], rstdt2[tc % 2], brstd2[tc % 2]
                for c in range(8):
                    w, bw = wload_sub("A", [I["w_in_ab"][:, c * 128:(c + 1) * 128], I["w_in_ab"][:, 1024 + c * 128:1024 + (c + 1) * 128]])
                    proj_fm(w, bw, 0, tc, 0)
                    proj_fm(w, bw, 128, tc, 1)
                    k = cn["sg"] % 2
                    cn["sg"] += 1
                    sigmoid_to(sgt[k][:, 0:wd], bsgt[k], ps[1][:, 0:wd], bps[1])
                    if tc < 4:
                        u = cn["u"] % 2
                        cn["u"] += 1
                        mk.op("dve", lambda e: e.tensor_tensor(uext[u][:, 30:542], ps[0][:, :], sgt[k][:, :], ALU.mult),
                              reads=[bps[0], bsgt[k]], writes=[buext[u]])
                        mk.op("dve", lambda e, c=c: e.tensor_copy(uext[u][:, 0:30], tails[:, c, :]), reads=[btails[c]], writes=[buext[u]])
                        mk.op("dve", lambda e, c=c: e.tensor_copy(tails[:, c, :], uext[u][:, 512:542]), reads=[buext[u]], writes=[btails[c]])
                        dk = cn["dg"] % 2
                        cn["dg"] += 1
                        mk.op("dve", lambda e, c=c: e.tensor_tensor(
                            dgs[dk][:, :, :], cst["identb"][:, :].rearrange("p (o q) -> p o q", o=1).to_broadcast([128, 31, 128]),
                            prm["convw"][:, c, :].rearrange("p (j o) -> p j o", o=1).to_broadcast([128, 31, 128]), ALU.mult),
                            reads=[bcst], writes=[bdgs[dk]], n=3000)
                        mk.op("act", lambda e: e.copy(ubf[u][:, 0:542], uext[u][:, 0:542]), reads=[buext[u]], writes=[bubf[u]])
                        cb = 4 + dk
                        NPE = 31
                        for j in range(NPE):
                            mk.op("pe", lambda e, j=j: e.matmul(ps[cb][:, :], dgs[dk][:, j, :], ubf[u][:, j:j + 512], start=(j == 0), stop=(j == NPE - 1)),
                                  reads=[bdgs[dk], bubf[u]], writes=[bps[cb]], mark=(j == NPE - 1))
                        mk.op("dve", lambda e, c=c: e.tensor_scalar(yconv[:, c, :], ps[cb][:, :], prm["convb"][:, c:c + 1], None, ALU.add),
                              reads=[bps[cb], bcst], writes=[byc[c]])
                        for j in range(NPE, 31):
                            mk.op("dve", lambda e, c=c, j=j: e.scalar_tensor_tensor(yconv[:, c, :], uext[u][:, j:j + 512],
                                                                                    prm["convw"][:, c, j:j + 1], yconv[:, c, :], ALU.mult, ALU.add),
                                  reads=[buext[u], bcst, byc[c]], writes=[byc[c]])
                        if tc == 3:
                            out_rows(uext[u][:, 510:542], buext[u], 32,
                                     lambda stg, bstg, c=c: mk.dma("sp", O["ncp"][:, c * 128:(c + 1) * 128], stg[2:32, :], rbuf=bstg, is_output=True))
                    else:
                        uv = uexts[:, c, :, 30:34]
                        mk.op("dve", lambda e: e.tensor_tensor(uv, ps[0][:, 0:NS].rearrange("p (s t) -> p s t", t=4),
                                                               sgt[k][:, 0:NS].rearrange("p (s t) -> p s t", t=4), ALU.mult),
                              reads=[bps[0], bsgt[k]], writes=[buexts[c]])
                        mk.op("dve", lambda e, c=c: e.memset(yconv[:, c, 0:32], 0.0), writes=[byc[c]])
                        yv = yconv[:, c, 0:NS].rearrange("p (s t) -> p s t", t=4)
                        mk.op("dve", lambda e, c=c: e.tensor_scalar(yv, uexts[:, c, :, 0:4], prm["convw"][:, c, 0:1],
                                                                  prm["convb"][:, c:c + 1], ALU.mult, ALU.add),
                              reads=[buexts[c], bcst], writes=[byc[c]])
                        for j in range(1, 31):
                            mk.op("dve", lambda e, c=c, j=j: e.scalar_tensor_tensor(yv, uexts[:, c, :, j:j + 4], prm["convw"][:, c, j:j + 1],
                                                                                    yv, ALU.mult, ALU.add),
                                  reads=[buexts[c], bcst, byc[c]], writes=[byc[c]], n=16)
                        u = cn["u"] % 2
                        cn["u"] += 1
                        mk.op("dve", lambda e: e.memset(uext[u][:, 0:32], 0.0), writes=[buext[u]])
                        mk.op("dve", lambda e: e.tensor_copy(uext[u][:, 0:NS].rearrange("p (s t) -> p s t", t=4), uv),
                              reads=[buexts[c]], writes=[buext[u]])

                        def dst_fn(stg, bstg, c=c):
                            for s_ in range(4):
                                mk.dma("sp", O["ncs"][s_, 26:30, c * 128:(c + 1) * 128], stg[s_ * 4:s_ * 4 + 4, :], rbuf=bstg, is_output=True)
                        out_rows(uext[u][:, 0:32], buext[u], 32, dst_fn)
                for c in range(8):
                    k = cn["yb"] % 2
                    cn["yb"] += 1
                    mk.op("act", lambda e, c=c: e.copy(ybf[k][:, 0:wd], yconv[:, c, 0:wd]), reads=[byc[c]], writes=[bybf[k]])
                    mk.op("act", lambda e, c=c: e.activation(ysq[k][:, 0:wd], yconv[:, c, 0:wd], AF.Square), reads=[byc[c]], writes=[bysq[k]])
                    mk.op("pe", lambda e, c=c: e.matmul(ps[2][:, 0:wd], onesb[:, :], ybf[k][:, 0:wd], start=(c == 0), stop=(c == 7)),
                          reads=[bones, bybf[k]], writes=[bps[2]])
                    mk.op("pe", lambda e, c=c: e.matmul(ps[3][:, 0:wd], onesb[:, :], ysq[k][:, 0:wd], start=(c == 0), stop=(c == 7)),
                          reads=[bones, bysq[k]], writes=[bps[3]])
                mk.op("dve", lambda e: e.tensor_scalar(meant[:, 0:wd], ps[2][:, 0:wd], 1.0 / 1024, None, ALU.mult), reads=[bps[2]], writes=[bmean])
                mk.op("dve", lambda e: e.tensor_tensor(rstdt[:, 0:wd], meant[:, 0:wd], meant[:, 0:wd], ALU.mult), reads=[bmean], writes=[brstd])
                mk.op("dve", lambda e: e.scalar_tensor_tensor(rstdt[:, 0:wd], ps[3][:, 0:wd], 1.0 / 1024, rstdt[:, 0:wd], ALU.mult, ALU.subtract),
                      reads=[bps[3], brstd], writes=[brstd])
                mk.op("dve", lambda e: e.tensor_scalar(rstdt[:, 0:wd], rstdt[:, 0:wd], EPS, None, ALU.add), reads=[brstd], writes=[brstd])
                mk.op("act", lambda e: e.activation(rstdt[:, 0:wd], rstdt[:, 0:wd], AF.Ln), reads=[brstd], writes=[brstd])
                mk.op("act", lambda e: e.activation(rstdt[:, 0:wd], rstdt[:, 0:wd], AF.Exp, scale=-0.5), reads=[brstd], writes=[brstd])
                for c in range(8):
                    w, bw = wload_sub("B", [I["w_in_ab"][:, 2048 + c * 128:2048 + (c + 1) * 128]])
                    proj_fm(w, bw, 0, tc, 6)
                    k = cn["sa"] % 2
                    cn["sa"] += 1
                    k2 = cn["sg"] % 2
                    cn["sg"] += 1
                    silu_to(saog[k][:, 0:wd], bsaog[k], ps[6][:, 0:wd], bps[6], sgt[k2][:, 0:wd], bsgt[k2])
                    l = cn["ln"] % 2
                    cn["ln"] += 1
                    mk.op("dve", lambda e, c=c: e.tensor_tensor(lnt[l][:, 0:wd], yconv[:, c, 0:wd], meant[:, 0:wd], ALU.subtract),
                          reads=[byc[c], bmean], writes=[blnt[l]])
                    mk.op("dve", lambda e: e.tensor_tensor(lnt[l][:, 0:wd], lnt[l][:, 0:wd], rstdt[:, 0:wd], ALU.mult),
                          reads=[blnt[l], brstd], writes=[blnt[l]])
                    mk.op("act", lambda e, c=c: e.activation(lnt[l][:, 0:wd], lnt[l][:, 0:wd], AF.Identity, bias=prm["lnb"][:, c:c + 1],
                                                             scale=prm["lng"][:, c:c + 1]),
                          reads=[blnt[l], bcst], writes=[blnt[l]])
                    k3 = cn["sg"] % 2
                    cn["sg"] += 1
                    sigmoid_to(sgt[k3][:, 0:wd], bsgt[k3], lnt[l][:, 0:wd], blnt[l])
                    mk.op("dve", lambda e: e.tensor_tensor(lnt[l][:, 0:wd], lnt[l][:, 0:wd], sgt[k3][:, 0:wd], ALU.mult),
                          reads=[blnt[l], bsgt[k3]], writes=[blnt[l]])
                    mk.op("dve", lambda e, c=c: e.tensor_tensor(yaT[:, c, c0:c0 + wd], lnt[l][:, 0:wd], saog[k][:, 0:wd], ALU.mult),
                          reads=[blnt[l], bsaog[k]], writes=[byaT[c][tc]])
            mk.barrier()

        def out_proj_stage(sD, zsrc, w_dram, layer, x_in_fn, final):
            wout = T("wout%d" % layer, [128, 16, 1024], BF16, sD)
            bwoutg = [Buf("wout%d" % g8) for g8 in range(8)]
            for g8 in range(8):
                mk.dma("sp", wout[:, g8 * 2:(g8 + 1) * 2, :], woutb[layer][g8 * 256:(g8 + 1) * 256, :].rearrange("(e p) f -> p e f", p=128),
                       wbuf=bwoutg[g8], after=[bwoutb[layer]])
            gpo = T("gpo%d" % layer, [128, 1024], F32, sD)
            bgpo = Buf("gpo")
            mk.dma("sp", gpo[:], I["gpost"][:, layer, :], wbuf=bgpo)
            yn = [T("yn%d_%d" % (layer, i), [128, 1024], F32, sD) for i in range(2)]
            byn = [Buf("yn%d" % i) for i in range(2)]
            x1t = [T("x1t%d_%d" % (layer, i), [128, 1024], F32, sD) for i in range(2)]
            bx1t = [Buf("x1t%d" % i) for i in range(2)]
            sso = [T("sso%d_%d" % (layer, i), [128, 8], F32, sD) for i in range(2)]
            bsso = [Buf("sso%d" % i) for i in range(2)]
            for ti in range(17):
                row0, P = TILES[ti]
                tc = min(ti // 4, 4)
                k = ti % 2
                nreal = P if ti < 16 else NS
                mk.dma("sp", xr[k][:nreal, :], x_in_fn(ti), wbuf=bxr[k])
                for half in range(2):
                    for e_ in range(16):
                        zt, zb = zsrc(e_, tc)
                        mk.op("pe", lambda e, e_=e_, half=half, zt=zt: e.matmul(ps[2 * k + half][:P, :], zt[:, row0:row0 + P],
                                                                                wout[:, e_, half * 512:(half + 1) * 512],
                                                                                start=(e_ == 0), stop=(e_ == 15)),
                              reads=[zb, bwoutg[e_ // 2]], writes=[bps[2 * k + half]], mark=(e_ == 15))
                ss, bs_ = sso[k], bsso[k]
                for half in range(2):
                    mk.op("act", lambda e, half=half: e.activation(junk[:P, 0:512], ps[2 * k + half][:P, :], AF.Square, accum_out=ss[:P, half:half + 1]),
                          reads=[bps[2 * k + half]], writes=[bjunk, bs_])
                mk.op("dve", lambda e: e.tensor_tensor(ss[:P, 2:3], ss[:P, 0:1], ss[:P, 1:2], ALU.add), reads=[bs_], writes=[bs_])
                mk.op("dve", lambda e: e.tensor_scalar(ss[:P, 2:3], ss[:P, 2:3], 1.0 / 1024, EPS, ALU.mult, ALU.add), reads=[bs_], writes=[bs_])
                mk.op("act", lambda e: e.activation(ss[:P, 2:3], ss[:P, 2:3], AF.Ln), reads=[bs_], writes=[bs_])
                mk.op("act", lambda e: e.activation(ss[:P, 3:4], ss[:P, 2:3], AF.Exp, scale=-0.5), reads=[bs_], writes=[bs_])
                for half in range(2):
                    mk.op("dve", lambda e, half=half: e.scalar_tensor_tensor(yn[k][:P, half * 512:(half + 1) * 512], ps[2 * k + half][:P, :], ss[:P, 3:4],
                                                                             gpo[:P, half * 512:(half + 1) * 512], ALU.mult, ALU.mult),
                          reads=[bps[2 * k + half], bs_, bgpo], writes=[byn[k]])
                if ti == 16:
                    mk.op("dve", lambda e: e.memset(x1t[k][0:32, :], 0.0), writes=[bx1t[k]])
                mk.op("dve", lambda e: e.tensor_tensor(x1t[k][:nreal, :], yn[k][:nreal, :], xr[k][:nreal, :], ALU.add),
                      reads=[byn[k], bxr[k]], writes=[bx1t[k]])
                final(ti, x1t[k], bx1t[k], nreal)

        def zsrc0(e_, tc):
            return (yaT[:, e_, :], byaT[e_][tc]) if e_ < 8 else (ybT[:, e_ - 8, :], bybT[e_ - 8][tc])

        def x1_dst(ti):
            row0, P = TILES[ti]
            return x1s[row0:row0 + (P if ti < 16 else NS), :]

        x1bufs0 = []

        def final0(ti, xt, bxt, nreal):
            mk.dma("sp", x1_dst(ti), xt[:nreal, :], rbuf=bxt)
            if bxt not in x1bufs0:
                x1bufs0.append(bxt)
            if dbg:
                row0, P = TILES[ti]
                mk.dma("sp", DBG["x1"][row0:row0 + nreal, :], xt[:nreal, :], rbuf=bxt, is_output=True)
            norm_to_hT(xt, bxt, ti, 1)

        if dbg:
            mk.dma("sp", DBG["yaT"], yaT[:, :, 0:NCOL], rbuf=bdbg, is_output=True)
        with contextlib.ExitStack() as sD:
            out_proj_stage(sD, zsrc0, I["w_out_ab"], 0, x_src, final0)
            mk.barrier()
        sCD.close()

        x1bufs = x1bufs0
        with contextlib.ExitStack() as sL1:
            zhi = T("zhi", [128, 8, NCOLP], BF16, sL1)
            bz = [[Buf("z%d_%d" % (h, tc)) for tc in range(5)] for h in range(16)]

            def zT(h):
                return ybT[:, h, :] if h < 8 else zhi[:, h - 8, :]

            with contextlib.ExitStack() as sH:
                lbt = T("lbt", [128, 16], F32, sH)
                oml = T("oml", [128, 16], F32, sH)
                blb = Buf("lb")
                mk.op("dve", lambda e: e.tensor_tensor(lbt[:], prm["hlb"][:, 1, :], prm["hlb"][:, 0, :], ALU.subtract), reads=[bcst], writes=[blb])
                sigmoid_to(lbt[:], blb, lbt[:], blb)
                mk.op("dve", lambda e: e.tensor_scalar(oml[:], lbt[:], -1.0, 1.0, ALU.mult, ALU.add), reads=[blb], writes=[blb])
                R2 = lambda name, shape, dt: ([T("%s%d" % (name, i), shape, dt, sH) for i in range(2)], [Buf("%s%d" % (name, i)) for i in range(2)])
                RD = 2
                R3 = lambda name, shape, dt: ([T("%s%d" % (name, i), shape, dt, sH) for i in range(RD)], [Buf("%s%d" % (name, i)) for i in range(RD)])
                qs, bqs = R3("qs", [128, 512], F32)
                fg, bfg = R3("fg", [128, 512], F32)
                sgn, bsgn = R3("sgn", [128, 512], F32)
                bc, bbc = R3("bc", [128, 512], F32)
                ebt, bebt = R3("ebt", [128, 512], F32)
                enb, benb = R3("enb", [128, 512], F32)
                kpp = T("kpp", [128, NCOLP], BF16, sH)
                bkpp = [Buf("kpp%d" % tc) for tc in range(5)]
                HSETS = 1
                hsets = []
                for i_ in range(HSETS):
                    d_ = {}
                    for nm_ in ("qp", "kp", "sgate"):
                        d_[nm_] = T("%s_%d" % (nm_, i_), [128, NCOLP], BF16, sH)
                        d_["b" + nm_] = [Buf("%s%d_%d" % (nm_, i_, tc)) for tc in range(5)]
                    d_["ebl"] = T("ebl_%d" % i_, [128, 40], F32, sH)
                    d_["bebl"] = [Buf("ebl%d_%d" % (i_, tc)) for tc in range(5)]
                    d_["Vh"] = T("Vh_%d" % i_, [128, 16, 128], BF16, sH)
                    d_["bVh"] = [Buf("Vh%d_%d" % (i_, g)) for g in range(4)]
                    d_["ktok"] = T("ktok_%d" % i_, [128, 2, 16, 128], BF16, sH)
                    d_["bktok"] = [Buf("ktok%d_%d" % (i_, g)) for g in range(2)]
                    mk.op("dve", lambda e, t_=d_["ktok"]: e.memset(t_[:, :, :, :], 0.0), writes=d_["bktok"], n=4096)
                    hsets.append(d_)
                S0_ = T("S0", [128, 4, 128], F32, sH)
                bS0_ = Buf("S0")
                for d_ in hsets:
                    d_["S0"], d_["bS0"] = S0_, bS0_
                Vhs = T("Vhs", [32, 128], BF16, sH)
                bVhs = Buf("Vhs")
                ktoks = T("ktoks", [32, 128], BF16, sH)
                bktoks = Buf("ktoks")
                ktm = T("ktm", [32, 4, 128], BF16, sH)
                bktm = Buf("ktm")
                atm, batm = R2("atm", [128, 128], BF16)
                Sf, bSf = R2("Sf", [128, 128], F32)
                Sb = [T("Sb%d" % i, [128, 128], BF16, sH) for i in range(4)]
                bSb = [Buf("Sb%d" % i) for i in range(4)]
                osq, bosq = R2("osq", [128, 512], BF16)
                rst, brst = sgn, bsgn
                zt, bzt = bc, bbc
                S0b = T("S0b", [128, 4, 128], BF16, sH)
                bS0b = Buf("S0b")
                Sn = T("Sn", [128, 2, 128], F32, sH)
                bSn = Buf("Sn")
                hc = {"r": 0, "atm": 0, "sb": 0, "o": 0, "z": 0}
                psbF1 = psb[:, :].bitcast(F32)
                st4 = I["st"].rearrange("(s h) d v -> h d s v", h=16)
                nss4 = O["nss"].rearrange("(s h) d v -> h d s v", h=16)

                for h in range(16):
                    hs_ = hsets[h % HSETS]
                    qp, bqp, kp, bkp, sgate, bsgate = (hs_["qp"], hs_["bqp"], hs_["kp"], hs_["bkp"], hs_["sgate"], hs_["bsgate"])
                    ebl, bebl, Vh, bVh, ktok, bktok, S0, bS0 = (hs_["ebl"], hs_["bebl"], hs_["Vh"], hs_["bVh"], hs_["ktok"], hs_["bktok"],
                                                                hs_["S0"], hs_["bS0"])
                    w, bw = wload([I["w_in_c"][:, h * 128:(h + 1) * 128], I["w_in_c"][:, 2048 + h * 128:2048 + (h + 1) * 128],
                                   I["w_in_c"][:, 4096 + h * 128:4096 + (h + 1) * 128], I["w_in_c"][:, 6144 + h * 128:6144 + (h + 1) * 128]])
                    mk.dma("sp", S0[:], st4[h], wbuf=bS0)
                    for tc in range(5):
                        c0, wd = TCH[tc]
                        r = hc["r"] % RD
                        hc["r"] += 1
                        proj_fm(w, bw, 0, tc, 0)
                        silu_to(qs[r][:, 0:wd], bqs[r], ps[0][:, 0:wd], bps[0], enb[r][:, 0:wd], benb[r], xs=(qs[r][:, 0:wd], bqs[r]))
                        proj_fm(w, bw, 128, tc, 1)
                        sigmoid_to(fg[r][:, 0:wd], bfg[r], ps[1][:, 0:wd], bps[1], keep_e=(sgn[r][:, 0:wd], bsgn[r]))
                        mk.op("dve", lambda e: e.tensor_tensor(sgn[r][:, 0:wd], sgn[r][:, 0:wd], fg[r][:, 0:wd], ALU.mult),
                              reads=[bsgn[r], bfg[r]], writes=[bsgn[r]])
                        mk.op("dve", lambda e: e.tensor_scalar(fg[r][:, 0:wd], fg[r][:, 0:wd], oml[:, h:h + 1], lbt[:, h:h + 1], ALU.mult, ALU.add),
                              reads=[bfg[r], blb], writes=[bfg[r]])
                        mk.op("act", lambda e: e.activation(fg[r][:, 0:wd], fg[r][:, 0:wd], AF.Ln), reads=[bfg[r]], writes=[bfg[r]])
                        rmask = cst["resetm"][:, 0:wd] if tc < 4 else cst["resets"][:, 0:wd]
                        mk.op("dve", lambda e: e.tensor_tensor_scan(bc[r][:, 0:wd], rmask, fg[r][:, 0:wd], 0.0, ALU.mult, ALU.add),
                              reads=[bfg[r], bcst], writes=[bbc[r]], n=2 * wd)
                        mk.op("act", lambda e: e.activation(ebt[r][:, 0:wd], bc[r][:, 0:wd], AF.Exp), reads=[bbc[r]], writes=[bebt[r]])
                        mk.op("act", lambda e: e.activation(enb[r][:, 0:wd], bc[r][:, 0:wd], AF.Exp, scale=-1.0), reads=[bbc[r]], writes=[benb[r]])
                        cl = 64 if tc < 4 else 4
                        ncl = wd // cl
                        mk.op("dve", lambda e: e.tensor_copy(ebl[:, tc * 8:tc * 8 + ncl], ebt[r][:, cl - 1:wd:cl]), reads=[bebt[r]], writes=[bebl[tc]])
                        mk.op("dve", lambda e: e.tensor_tensor(qp[:, c0:c0 + wd], qs[r][:, 0:wd], ebt[r][:, 0:wd], ALU.mult),
                              reads=[bqs[r], bebt[r]], writes=[bqp[tc]])
                        mk.op("dve", lambda e: e.scalar_tensor_tensor(kp[:, c0:c0 + wd], sgn[r][:, 0:wd], oml[:, h:h + 1], enb[r][:, 0:wd], ALU.mult, ALU.mult),
                              reads=[bsgn[r], blb, benb[r]], writes=[bkp[tc]])
                        eblb = ebl[:, tc * 8:tc * 8 + ncl].rearrange("p (c o) -> p c o", o=1).to_broadcast([128, ncl, cl])
                        mk.op("dve", lambda e: e.tensor_tensor(kpp[:, c0:c0 + wd].rearrange("p (c t) -> p c t", t=cl),
                                                               kp[:, c0:c0 + wd].rearrange("p (c t) -> p c t", t=cl), eblb, ALU.mult),
                              reads=[bkp[tc], bebl[tc]], writes=[bkpp[tc]])
                        proj_fm(w, bw, 384, tc, (psbF1, bpsb))
                        silu_to(sgate[:, c0:c0 + wd], bsgate[tc], psbF1[:, 0:wd], bpsb, bc[r][:, 0:wd], bbc[r], xs=(qs[r][:, 0:wd], bqs[r]))
                    for g in range(5):
                        bank = g % 2
                        if g < 4:
                            for j in range(4):
                                ti = g * 4 + j
                                for d in range(8):
                                    mk.op("pe", lambda e, d=d, j=j, ti=ti: e.matmul(ps[bank][:, j * 128:(j + 1) * 128], hT[:, d, ti * 128:(ti + 1) * 128],
                                                                                    w[:, d, 256:384], start=(d == 0), stop=(d == 7)),
                                          reads=[bw, bhT[ti]], writes=[bps[bank]], mark=(d == 7 and j == 3), n=128)
                            mk.op("act", lambda e: e.copy(Vh[:, g * 4:g * 4 + 4, :], ps[bank][:, :].rearrange("p (j f) -> p j f", f=128)),
                                  reads=[bps[bank]], writes=[bVh[g]])
                        else:
                            for d in range(8):
                                mk.op("pe", lambda e, d=d: e.matmul(ps[bank][0:32, 0:128], hT[:, d, NTOK:NCOLP], w[:, d, 256:384],
                                                                    start=(d == 0), stop=(d == 7)),
                                      reads=[bw, bhT[16]], writes=[bps[bank]], mark=(d == 7), n=128)
                            mk.op("act", lambda e: e.copy(Vhs[:, :], ps[bank][0:32, 0:128]), reads=[bps[bank]], writes=[bVhs])
                    for g in range(2):
                        for j in range(8):
                            ti = g * 8 + j
                            mk.op("pe", lambda e, j=j, ti=ti: e.transpose(psb[:, j * 128:(j + 1) * 128], kpp[:, ti * 128:(ti + 1) * 128], cst["identb"][:, :]),
                                  reads=[bkpp[ti // 4], bcst], writes=[bpsb], mark=(j == 7), n=128)
                        mk.op("act", lambda e: e.copy(ktok[0:64, 0, g * 8:(g + 1) * 8, :], psb[0:64, :].rearrange("p (j f) -> p j f", f=128)),
                              reads=[bpsb], writes=[bktok[g]])
                        mk.op("act", lambda e: e.copy(ktok[64:128, 1, g * 8:(g + 1) * 8, :], psb[64:128, :].rearrange("p (j f) -> p j f", f=128)),
                              reads=[bpsb], writes=[bktok[g]])
                    mk.op("pe", lambda e: e.transpose(psb[0:32, 0:128], kpp[:, NTOK:NCOLP], cst["identb"][:, :]), reads=[bkpp[4], bcst], writes=[bpsb], n=128)
                    mk.op("act", lambda e: e.copy(ktoks[:, :], psb[0:32, 0:128]), reads=[bpsb], writes=[bktoks])

                    def finish_o(ob, c0, wd, tc):
                        k = hc["z"] % 2
                        hc["z"] += 1
                        mk.op("act", lambda e: e.activation(osq[k][:, 0:wd], ps[ob][:, 0:wd], AF.Square), reads=[bps[ob]], writes=[bosq[k]])
                        mk.op("pe", lambda e: e.matmul(ps[6][:, 0:wd], onesb[:, :], osq[k][:, 0:wd], start=True, stop=True),
                              reads=[bones, bosq[k]], writes=[bps[6]])
                        mk.op("dve", lambda e: e.tensor_scalar(rst[k][:, 0:wd], ps[6][:, 0:wd], 1.0 / 128, EPS, ALU.mult, ALU.add),
                              reads=[bps[6]], writes=[brst[k]])
                        mk.op("act", lambda e: e.activation(rst[k][:, 0:wd], rst[k][:, 0:wd], AF.Ln), reads=[brst[k]], writes=[brst[k]])
                        mk.op("act", lambda e: e.activation(rst[k][:, 0:wd], rst[k][:, 0:wd], AF.Exp, scale=-0.5), reads=[brst[k]], writes=[brst[k]])
                        mk.op("dve", lambda e: e.scalar_tensor_tensor(zt[k][:, 0:wd], ps[ob][:, 0:wd], prm["gnorm"][:, 0:1], rst[k][:, 0:wd], ALU.mult, ALU.mult),
                              reads=[bps[ob], bcst, brst[k]], writes=[bzt[k]])
                        mk.op("dve", lambda e: e.tensor_tensor(zT(h)[:, c0:c0 + wd], zt[k][:, 0:wd], sgate[:, c0:c0 + wd], ALU.mult),
                              reads=[bzt[k], bsgate[tc]], writes=[bz[h][tc]])

                    scur = None
                    sbcur = None
                    for Tt in range(16):
                        tc = Tt // 4
                        a_ = hc["atm"] % 2
                        hc["atm"] += 1
                        cols = slice(Tt * 128, (Tt + 1) * 128)
                        mk.op("pe", lambda e: e.matmul(ps[6][:, 0:128], kp[:, cols], qp[:, cols], start=True, stop=True),
                              reads=[bkp[tc], bqp[tc]], writes=[bps[6]], n=128)
                        mk.op("dve", lambda e: e.tensor_tensor(atm[a_][:, :], ps[6][:, 0:128], cst["hmask"][:, :], ALU.mult),
                              reads=[bps[6], bcst], writes=[batm[a_]], n=128)
                        if Tt % 4 == 0:
                            ob = 4 + (hc["o"] % 2)
                            hc["o"] += 1
                        oc = (Tt % 4) * 128
                        sb_for = []
                        for cc in range(2):
                            c = Tt * 2 + cc
                            sb_for.append(sbcur)
                            ub = 2 + cc
                            mk.op("pe", lambda e, cc=cc: e.matmul(ps[ub][:, 0:128], ktok[:, cc, Tt, :], Vh[:, Tt, :],
                                                                  start=True, stop=True),
                                  reads=[bktok[Tt // 8], bVh[Tt // 4]], writes=[bps[ub]], n=128)
                            nxt = 0 if scur is None else 1 - scur
                            if scur is None:
                                mk.op("dve", lambda e: e.tensor_copy(Sf[nxt][:, :], ps[ub][:, 0:128]), reads=[bps[ub]], writes=[bSf[nxt]])
                            else:
                                mk.op("dve", lambda e, c=c, scur=scur: e.scalar_tensor_tensor(Sf[nxt][:, :], Sf[scur][:, :], ebl[:, c:c + 1], ps[ub][:, 0:128],
                                                                                              ALU.mult, ALU.add),
                                      reads=[bSf[scur], bebl[tc], bps[ub]], writes=[bSf[nxt]], n=128)
                            scur = nxt
                            sbcur = hc["sb"] % 4
                            hc["sb"] += 1
                            mk.op("act", lambda e, sbcur=sbcur, scur=scur: e.copy(Sb[sbcur][:, :], Sf[scur][:, :]), reads=[bSf[scur]], writes=[bSb[sbcur]], n=128)
                        n_extra = sum(1 for x in sb_for if x is not None)
                        mk.op("pe", lambda e: e.matmul(ps[ob][:, oc:oc + 128], Vh[:, Tt, :], atm[a_][:, :], start=True, stop=(n_extra == 0)),
                              reads=[bVh[Tt // 4], batm[a_]], writes=[bps[ob]], mark=(n_extra == 0), n=128)
                        done = 0
                        for cc in range(2):
                            if sb_for[cc] is None:
                                continue
                            done += 1
                            mk.op("pe", lambda e, cc=cc: e.matmul(ps[ob][:, oc + cc * 64:oc + (cc + 1) * 64], Sb[sb_for[cc]][:, :],
                                                                  qp[:, Tt * 128 + cc * 64:Tt * 128 + (cc + 1) * 64], start=False, stop=(done == n_extra)),
                                  reads=[bSb[sb_for[cc]], bqp[tc]], writes=[bps[ob]], mark=(done == n_extra), n=64)
                        if Tt % 4 == 3:
                            finish_o(ob, tc * 512, 512, tc)
                    mk.dma("sp", O["nsp"][h], Sf[scur][:, :], rbuf=bSf[scur], is_output=True)
                    mk.op("act", lambda e: e.copy(S0b[:], S0[:]), reads=[bS0], writes=[bS0b])
                    mk.op("pe", lambda e: e.matmul(ps[6][0:32, 0:32], kp[:, NTOK:NCOLP], qp[:, NTOK:NCOLP], start=True, stop=True),
                          reads=[bkp[4], bqp[4]], writes=[bps[6]])
                    a_ = hc["atm"] % 2
                    hc["atm"] += 1
                    mk.op("dve", lambda e: e.tensor_tensor(atm[a_][0:32, 0:32], ps[6][0:32, 0:32], cst["smaskh"][:, :], ALU.mult),
                          reads=[bps[6], bcst], writes=[batm[a_]])
                    ob = 4 + (hc["o"] % 2)
                    hc["o"] += 1
                    mk.op("pe", lambda e: e.matmul(ps[ob][:, 0:32], Vhs[:, :], atm[a_][0:32, 0:32], start=True, stop=False),
                          reads=[bVhs, batm[a_]], writes=[bps[ob]], mark=False)
                    for s_ in range(4):
                        mk.op("pe", lambda e, s_=s_: e.matmul(ps[ob][:, s_ * 4:s_ * 4 + 4], S0b[:, s_, :], qp[:, NTOK + s_ * 4:NTOK + s_ * 4 + 4],
                                                              start=False, stop=(s_ == 3)),
                              reads=[bS0b, bqp[4]], writes=[bps[ob]], mark=(s_ == 3))
                    finish_o(ob, NTOK, 32, 4)
                    for s_ in range(4):
                        mk.op("dve", lambda e, s_=s_: e.tensor_scalar(ktm[:, s_, :], ktoks[:, :], cst["sels"][:, s_:s_ + 1], None, ALU.mult),
                              reads=[bktoks, bcst], writes=[bktm])
                    for s_ in range(4):
                        ub = 2 + s_ % 2
                        mk.op("pe", lambda e, s_=s_: e.matmul(ps[ub][:, 0:128], ktm[:, s_, :], Vhs[:, :], start=True, stop=True),
                              reads=[bktm, bVhs], writes=[bps[ub]])
                        mk.op("dve", lambda e, s_=s_: e.scalar_tensor_tensor(Sn[:, s_ % 2, :], S0[:, s_, :], ebl[:, 32 + s_:33 + s_], ps[ub][:, 0:128],
                                                                             ALU.mult, ALU.add),
                              reads=[bS0, bebl[4], bps[ub]], writes=[bSn])
                        if s_ % 2 == 1:
                            mk.dma("sp", nss4[h][:, s_ - 1:s_ + 1, :], Sn[:], rbuf=bSn, is_output=True)
                mk.barrier()

            def zsrc1(e_, tc):
                return (zT(e_), bz[e_][tc])

            def final1(ti, xt, bxt, nreal):
                row0, P = TILES[ti]
                if ti < 16:
                    mk.dma("sp", O["yp"][row0:row0 + P, :], xt[:P, :], rbuf=bxt, is_output=True)
                else:
                    mk.dma("sp", O["ys"][:, :], xt[:NS, :], rbuf=bxt, is_output=True)

            mk.barrier()
            with contextlib.ExitStack() as sD1:
                out_proj_stage(sD1, zsrc1, I["w_out_c"], 1, x1_dst, final1)
                mk.barrier()
            if dbg:
                mk.dma("sp", DBG["zT"][:, 0:8, :], ybT[:, :, 0:NCOL], rbuf=bdbg, is_output=True)
                mk.dma("sp", DBG["zT"][:, 8:16, :], zhi[:, :, 0:NCOL], rbuf=bdbg, is_output=True)
                mk.barrier()

        mk.finish()
        print("instr counts", mk.ninstr, "dma sems", mk.nsem)
    return nc


def make_in_maps(inputs, cores=range(8)):
    f = lambda a: np.ascontiguousarray(np.asarray(a, dtype=np.float32))
    consts = host_consts()
    pre, post = f(inputs["pre_norm"]), f(inputs["post_norm"])
    shared = {
        "w_in_ab": f(inputs["w_in_ab"][0]), "w_out_ab": f(inputs["w_out_ab"][0]),
        "w_in_c": f(inputs["w_in_c"][0]), "w_out_c": f(inputs["w_out_c"][0]),
        "gpre": f(pre.reshape(2, 8, 128).transpose(2, 0, 1)),
        "gpost": f(np.broadcast_to(post[None, :, :], (128, 2, 1024))),
        "convw": f(np.asarray(inputs["conv_w"][0]).reshape(31, 8, 128).transpose(2, 1, 0)),
        "convb": f(np.asarray(inputs["conv_b"][0]).reshape(8, 128).T),
        "lng": f(np.asarray(inputs["conv_ln_g"][0]).reshape(8, 128).T),
        "lnb": f(np.asarray(inputs["conv_ln_b"][0]).reshape(8, 128).T),
        "gnorm": f(np.asarray(inputs["hgrn_gnorm"][0]).reshape(128, 1)),
        "hlb": f(np.asarray(inputs["hgrn_lb"]).reshape(2, 16, 128).transpose(2, 0, 1)),
    }
    shared.update(consts)
    maps = []
    for c in cores:
        m = dict(shared)
        m["xp"] = f(inputs["x_prompt"][c])
        m["xs"] = f(np.asarray(inputs["x_sample"][4 * c:4 * c + 4]).reshape(NS, 1024))
        m["cconv"] = f(np.asarray(inputs["cache_conv"][0, 4 * c:4 * c + 4]).reshape(120, 1024))
        m["ck"] = f(np.asarray(inputs["cache_swa_k"][0, 4 * c:4 * c + 4]).reshape(4, 2048, 1024))
        m["cv"] = f(np.asarray(inputs["cache_swa_v"][0, 4 * c:4 * c + 4]).reshape(4, 2048, 1024))
        m["st"] = f(np.asarray(inputs["state_hgrn"][0, 4 * c:4 * c + 4]).reshape(64, 128, 128))
        maps.append(m)
    return maps


def assemble(results):
    n = len(results)
    g = lambda k: [np.asarray(r[k], dtype=np.float32) for r in results]
    yp = np.stack(g("yp"), 0)
    ys = np.concatenate([a.reshape(4, 4, 1024) for a in g("ys")], 0)
    ncp = np.stack(g("ncp"), 0)[None]
    ncs = np.concatenate(g("ncs"), 0)[None]
    nkp = np.stack([a.reshape(2048, 16, 64) for a in g("nkp")], 0)[None]
    nvp = np.stack([a.reshape(2048, 16, 64) for a in g("nvp")], 0)[None]
    nks = np.concatenate([a.reshape(4, 4, 16, 64) for a in g("nks")], 0)[None]
    nvs = np.concatenate([a.reshape(4, 4, 16, 64) for a in g("nvs")], 0)[None]
    nsp = np.stack(g("nsp"), 0)[None]
    nss = np.concatenate([a.reshape(4, 16, 128, 128) for a in g("nss")], 0)[None]
    return (yp, ys, ncp, ncs, nkp, nvp, nks, nvs, nsp, nss)


def kernel(**inputs):
    nc = build()
    maps = make_in_maps(inputs)
    res = run_bass_kernel_spmd(nc, maps, core_ids=list(range(8)))
    return assemble(res.results)
```
